# Optimizing a Trainium2 kernel written in Bass

```python
import math
import jax
import jax.numpy as jnp
from jax import lax
import numpy as np

D_MODEL = 1024
BATCH = 8
SEQ = 2048
DEPTH = 2

N_MIXERS = 2
N_HEADS = 16
HEAD_DIM = 64
ATTN_WIDTH = N_HEADS * HEAD_DIM
A_KV_HEADS = 2
A_WINDOW = 128
B_KV_HEADS = 2
CMP_BLOCK = 32
CMP_STRIDE = 16
CMP_HIDDEN = 256
SEL_BLOCK = 64
SEL_TOPK = 16
SEL_LOCAL = 2
B_WINDOW = 512
REL_BUCKETS = 32
REL_MAX_DIST = 1024
D_FF = -(-(8 * D_MODEL) // (3 * 256)) * 256
BLOCK_Q = 128
SEL_QCHUNK = 64
RMS_EPS = 1e-6
NEG_INF = -1e30
FORCE_SCORE = 1e6
N_A_LAYERS = (DEPTH + 1) // 2
N_B_LAYERS = DEPTH // 2
A_QKV_WIDTH = ATTN_WIDTH + 2 * A_KV_HEADS * HEAD_DIM
B_IN_WIDTH = ATTN_WIDTH + 6 * B_KV_HEADS * HEAD_DIM + 3 * N_HEADS

kernel_name = 'hybrid_swa_sink_nsa_trunk'


def rms_norm(x, g):
    x32 = x.astype(jnp.float32)
    y = x32 * lax.rsqrt(jnp.mean(x32 * x32, axis=-1, keepdims=True) + RMS_EPS)
    return (y * g.astype(jnp.float32)).astype(x.dtype)


def t5_bucket(dist):
    n = jnp.maximum(dist, 0)
    max_exact = REL_BUCKETS // 2
    nf = jnp.maximum(n, 1).astype(jnp.float32)
    log_b = max_exact + (jnp.log(nf / max_exact) / math.log(REL_MAX_DIST / max_exact)
                         * (REL_BUCKETS - max_exact)).astype(jnp.int32)
    return jnp.where(n < max_exact, n, jnp.minimum(log_b, REL_BUCKETS - 1))


def head_bias(rel_table, dist, n_kv):
    b = rel_table.astype(jnp.float32)[t5_bucket(dist)]
    b = b.reshape(dist.shape + (n_kv, N_HEADS // n_kv))
    return jnp.moveaxis(b, (-2, -1), (0, 1))


def banded_attention(q, k, v, window, rel_table, sinks):
    B, S, hkv, g_sz, dh = q.shape
    nb = S // BLOCK_Q
    n_prev = (window - 1) // BLOCK_Q + 1
    span = (n_prev + 1) * BLOCK_Q
    pad = ((0, 0), (n_prev * BLOCK_Q, 0), (0, 0), (0, 0))
    kp = jnp.pad(k, pad)
    vp = jnp.pad(v, pad)
    qi = jnp.arange(BLOCK_Q)[:, None]
    kj = jnp.arange(span)[None, :]
    dist = qi - kj + n_prev * BLOCK_Q
    in_win = (dist >= 0) & (dist < window)
    bias = head_bias(rel_table, dist, hkv)
    qb = jnp.moveaxis(q.reshape(B, nb, BLOCK_Q, hkv, g_sz, dh), 1, 0)

    def block(args):
        n, qn = args
        start = n * BLOCK_Q
        kn = lax.dynamic_slice_in_dim(kp, start, span, axis=1)
        vn = lax.dynamic_slice_in_dim(vp, start, span, axis=1)
        s = jnp.einsum('bqhgd,bshd->bhgqs', qn, kn).astype(jnp.float32) + bias
        mask = in_win & (start - n_prev * BLOCK_Q + kj >= 0)
        s = jnp.where(mask, s, NEG_INF)
        if sinks is None:
            p = jax.nn.softmax(s, axis=-1)
        else:
            sk = sinks.astype(jnp.float32).reshape(hkv, g_sz, 1, 1)
            m = jnp.maximum(jnp.max(s, axis=-1, keepdims=True), sk)
            e = jnp.exp(s - m)
            p = e / (jnp.sum(e, axis=-1, keepdims=True) + jnp.exp(sk - m))
        return jnp.einsum('bhgqs,bshd->bqhgd', p.astype(vn.dtype), vn)

    out = lax.map(block, (jnp.arange(nb), qb))
    return jnp.moveaxis(out, 0, 1).reshape(B, S, hkv, g_sz, dh)


def swa_sink_mixer(h, w_qkv, b_qkv, sinks, w_o, b_o, rel_table):
    B, S, _ = h.shape
    g_sz = N_HEADS // A_KV_HEADS
    kvw = A_KV_HEADS * HEAD_DIM
    qkv = h @ w_qkv + b_qkv
    q = qkv[..., :ATTN_WIDTH].reshape(B, S, A_KV_HEADS, g_sz, HEAD_DIM) * HEAD_DIM ** -0.5
    k = qkv[..., ATTN_WIDTH:ATTN_WIDTH + kvw].reshape(B, S, A_KV_HEADS, HEAD_DIM)
    v = qkv[..., ATTN_WIDTH + kvw:].reshape(B, S, A_KV_HEADS, HEAD_DIM)
    o = banded_attention(q, k, v, A_WINDOW, rel_table, sinks)
    return o.reshape(B, S, ATTN_WIDTH) @ w_o + b_o


def compress(x_tok, pos, w1, w2):
    B, S, hkv, dh = x_tok.shape
    nc = (S - CMP_BLOCK) // CMP_STRIDE + 1
    idx = jnp.arange(nc)[:, None] * CMP_STRIDE + jnp.arange(CMP_BLOCK)[None, :]
    blocks = x_tok[:, idx] + pos[:, None, :]
    flat = jnp.moveaxis(blocks, 3, 2).reshape(B, nc, hkv, CMP_BLOCK * dh)
    return jax.nn.gelu(flat @ w1) @ w2


def nsa_mixer(h, w_in, cmp_pos, cmp_w1, cmp_w2, w_o, rel_table):
    B, S, _ = h.shape
    hkv, g_sz, dh = B_KV_HEADS, N_HEADS // B_KV_HEADS, HEAD_DIM
    kvw = hkv * dh
    proj = h @ w_in
    q = proj[..., :ATTN_WIDTH].reshape(B, S, hkv, g_sz, dh) * dh ** -0.5
    kv = proj[..., ATTN_WIDTH:ATTN_WIDTH + 6 * kvw].reshape(B, S, 6, hkv, dh)
    k_cmp, v_cmp, k_slc, v_slc, k_win, v_win = [kv[:, :, i] for i in range(6)]
    gates = jax.nn.sigmoid(proj[..., ATTN_WIDTH + 6 * kvw:].astype(jnp.float32))
    gates = gates.reshape(B, S, hkv, g_sz, 3).astype(h.dtype)

    kc = compress(k_cmp, cmp_pos[0], cmp_w1[0], cmp_w2[0])
    vc = compress(v_cmp, cmp_pos[1], cmp_w1[1], cmp_w2[1])
    nc = kc.shape[1]
    t = jnp.arange(S)[:, None]
    c_start = jnp.arange(nc) * CMP_STRIDE
    c_end = c_start + CMP_BLOCK - 1
    c_valid = c_end[None, :] <= t
    s_c = jnp.einsum('bshgd,bchd->bhgsc', q, kc).astype(jnp.float32)
    s_c = s_c + head_bias(rel_table, t - c_end[None, :], hkv)
    s_c = jnp.where(c_valid, s_c, NEG_INF)
    p_c = jax.nn.softmax(s_c, axis=-1) * c_valid
    o_cmp = jnp.einsum('bhgsc,bchd->bshgd', p_c.astype(h.dtype), vc)

    n_sel = S // SEL_BLOCK
    top_k = min(SEL_TOPK, n_sel)
    s_start = jnp.arange(n_sel) * SEL_BLOCK
    overlap = ((c_start[:, None] < s_start[None, :] + SEL_BLOCK)
               & (c_start[:, None] + CMP_BLOCK > s_start[None, :])).astype(jnp.float32)
    imp = jnp.einsum('bhgsc,cj->bhsj', p_c, overlap)
    cur = t // SEL_BLOCK
    jb = jnp.arange(n_sel)[None, :]
    forced = (jb == 0) | ((cur - jb >= 0) & (cur - jb < SEL_LOCAL))
    score = jnp.where(forced, FORCE_SCORE, jnp.where(jb <= cur, imp, -1.0))
    _, sel_idx = lax.top_k(score, top_k)

    ksb = jnp.moveaxis(k_slc.reshape(B, n_sel, SEL_BLOCK, hkv, dh), 3, 1)
    vsb = jnp.moveaxis(v_slc.reshape(B, n_sel, SEL_BLOCK, hkv, dh), 3, 1)
    n_ch = S // SEL_QCHUNK
    q_ch = jnp.moveaxis(q.reshape(B, n_ch, SEL_QCHUNK, hkv, g_sz, dh), 1, 0)
    idx_ch = jnp.moveaxis(sel_idx.reshape(B, hkv, n_ch, SEL_QCHUNK, top_k), 2, 0)
    bi = jnp.arange(B)[:, None, None, None]
    hi = jnp.arange(hkv)[None, :, None, None]
    table_h = rel_table.astype(jnp.float32).reshape(REL_BUCKETS, hkv, g_sz).transpose(1, 0, 2)
    n_keys = top_k * SEL_BLOCK

    def sel_chunk(args):
        c, qc, ic = args
        kg = ksb[bi, hi, ic].reshape(B, hkv, SEL_QCHUNK, n_keys, dh)
        vg = vsb[bi, hi, ic].reshape(B, hkv, SEL_QCHUNK, n_keys, dh)
        key_pos = (ic[..., None] * SEL_BLOCK + jnp.arange(SEL_BLOCK)).reshape(B, hkv, SEL_QCHUNK, n_keys)
        q_pos = c * SEL_QCHUNK + jnp.arange(SEL_QCHUNK)
        dist = q_pos[:, None] - key_pos
        bias = jnp.moveaxis(table_h[hi, t5_bucket(dist)], -1, 2)
        s = jnp.einsum('bqhgd,bhqkd->bhgqk', qc, kg).astype(jnp.float32) + bias
        s = jnp.where((dist >= 0)[:, :, None], s, NEG_INF)
        p = jax.nn.softmax(s, axis=-1)
        return jnp.einsum('bhgqk,bhqkd->bqhgd', p.astype(vg.dtype), vg)

    o_slc = lax.map(sel_chunk, (jnp.arange(n_ch), q_ch, idx_ch))
    o_slc = jnp.moveaxis(o_slc, 0, 1).reshape(B, S, hkv, g_sz, dh)

    o_win = banded_attention(q, k_win, v_win, B_WINDOW, rel_table, None)

    o = gates[..., 0:1] * o_cmp + gates[..., 1:2] * o_slc + gates[..., 2:3] * o_win
    return o.reshape(B, S, ATTN_WIDTH) @ w_o


def swiglu(h, w_gu, w_down):
    gate, up = jnp.split(h @ w_gu, 2, axis=-1)
    return (jax.nn.silu(gate) * up) @ w_down


def setup_inputs(seed: int = 0) -> dict:
    key = jax.random.key(seed)
    ks = jax.random.split(key, 18)
    f32 = jnp.float32

    def nrm(k, shape, scale):
        return jax.random.normal(k, shape, f32) * scale

    res_scale = (2 * DEPTH) ** -0.5
    return {
        'x': nrm(ks[0], (BATCH, SEQ, D_MODEL), 1.0),
        'rel_table': nrm(ks[1], (REL_BUCKETS, N_HEADS), 0.5),
        'attn_norm': 1.0 + nrm(ks[2], (DEPTH, D_MODEL), 0.05),
        'ffn_norm': 1.0 + nrm(ks[3], (DEPTH, D_MODEL), 0.05),
        'final_norm': 1.0 + nrm(ks[4], (D_MODEL,), 0.05),
        'a_w_qkv': nrm(ks[5], (N_A_LAYERS, D_MODEL, A_QKV_WIDTH), D_MODEL ** -0.5),
        'a_b_qkv': nrm(ks[6], (N_A_LAYERS, A_QKV_WIDTH), 0.02),
        'a_sinks': nrm(ks[7], (N_A_LAYERS, N_HEADS), 1.0),
        'a_w_o': nrm(ks[8], (N_A_LAYERS, ATTN_WIDTH, D_MODEL), ATTN_WIDTH ** -0.5 * res_scale),
        'a_b_o': nrm(ks[9], (N_A_LAYERS, D_MODEL), 0.02),
        'b_w_in': nrm(ks[10], (N_B_LAYERS, D_MODEL, B_IN_WIDTH), D_MODEL ** -0.5),
        'b_cmp_pos': nrm(ks[11], (N_B_LAYERS, 2, CMP_BLOCK, HEAD_DIM), 0.1),
        'b_cmp_w1': nrm(ks[12], (N_B_LAYERS, 2, CMP_BLOCK * HEAD_DIM, CMP_HIDDEN), (CMP_BLOCK * HEAD_DIM) ** -0.5),
        'b_cmp_w2': nrm(ks[13], (N_B_LAYERS, 2, CMP_HIDDEN, HEAD_DIM), CMP_HIDDEN ** -0.5),
        'b_w_o': nrm(ks[14], (N_B_LAYERS, ATTN_WIDTH, D_MODEL), ATTN_WIDTH ** -0.5 * res_scale),
        'ffn_w_gu': nrm(ks[15], (DEPTH, D_MODEL, 2 * D_FF), D_MODEL ** -0.5),
        'ffn_w_down': nrm(ks[16], (DEPTH, D_FF, D_MODEL), D_FF ** -0.5 * res_scale),
    }


def reference(x, rel_table, attn_norm, ffn_norm, final_norm, a_w_qkv, a_b_qkv, a_sinks, a_w_o, a_b_o,
              b_w_in, b_cmp_pos, b_cmp_w1, b_cmp_w2, b_w_o, ffn_w_gu, ffn_w_down):
    h = x
    for layer in range(DEPTH):
        hn = rms_norm(h, attn_norm[layer])
        j = layer // N_MIXERS
        if layer % N_MIXERS == 0:
            h = h + swa_sink_mixer(hn, a_w_qkv[j], a_b_qkv[j], a_sinks[j], a_w_o[j], a_b_o[j], rel_table)
        else:
            h = h + nsa_mixer(hn, b_w_in[j], b_cmp_pos[j], b_cmp_w1[j], b_cmp_w2[j], b_w_o[j], rel_table)
        h = h + swiglu(rms_norm(h, ffn_norm[layer]), ffn_w_gu[layer], ffn_w_down[layer])
    return rms_norm(h, final_norm)
```

```python
import math
from contextlib import ExitStack
import numpy as np
import concourse.bass as bass
import concourse.mybir as mybir
from concourse.bass_utils import run_bass_kernel_spmd

F32 = mybir.dt.float32
BF16 = mybir.dt.bfloat16
AF = mybir.ActivationFunctionType
ALU = mybir.AluOpType
AX = mybir.AxisListType

S = 2048
D = 1024
NT = 16
KC = 8
DFF = 2816
FOFF = 2063
FLEN = 4608
NEG = -1e30


class Res:
    __slots__ = ("name", "w", "r")

    def __init__(self, name=""):
        self.name = name
        self.w = None
        self.r = {}


class DmaSem:
    def __init__(self, handle, name):
        self.handle = handle
        self.name = name
        self.count = 0


class Queue:
    def __init__(self, name, eng, sem):
        self.name = name
        self.eng = eng
        self.sem = sem
        self.ops = []
        self.seen = {}


class Prog:
    def __init__(self, nc, stack):
        self.nc = nc
        self.stack = stack
        self.q = {}
        for name, eng in (("pe", nc.tensor), ("act", nc.scalar), ("dve", nc.vector),
                          ("pool", nc.gpsimd), ("sp", nc.sync)):
            sem = stack.enter_context(nc.semaphore("s_" + name))
            self.q[name] = Queue(name, eng, sem)
        self.n_dsem = 0
        self.pending = []
        self.rcache = {}

    def R(self, *key):
        r = self.rcache.get(key)
        if r is None:
            r = Res(str(key))
            self.rcache[key] = r
        return r

    def dsem(self, name=None):
        self.n_dsem += 1
        h = self.stack.enter_context(self.nc.semaphore("d_%d" % self.n_dsem))
        return DmaSem(h, name or "d%d" % self.n_dsem)

    def sb(self, name, shape, dtype):
        return self.stack.enter_context(self.nc.sbuf_tensor(name, list(shape), dtype))

    def ps(self, name, shape, dtype):
        return self.stack.enter_context(self.nc.psum_tensor(name, list(shape), dtype))

    def _need(self, q, toks):
        waits = []
        for (key, idx) in toks:
            if key is q and q.name == "pe":
                continue
            if q.seen.get(key, -1) >= idx:
                continue
            q.seen[key] = idx
            waits.append((key, idx))
            if isinstance(key, Queue):
                key.ops[idx]["mark"] = True
        return waits

    def _deps(self, q, reads, writes):
        toks = []
        for r in reads:
            if r.w is not None:
                toks.append(r.w)
        for w in writes:
            if w.w is not None:
                toks.append(w.w)
            toks.extend(w.r.values())
        return self._need(q, toks)

    def op(self, qname, fn, reads=(), writes=()):
        q = self.q[qname]
        waits = self._deps(q, reads, writes)
        idx = len(q.ops)
        q.ops.append({"fn": fn, "waits": waits, "mark": False, "dsem": None})
        tok = (q, idx)
        for r in reads:
            r.r[q] = tok
        for w in writes:
            w.w = tok
            w.r = {}
        return tok

    def dma(self, qname, fn, sem, reads=(), writes=(), cont=False):
        q = self.q[qname]
        toks = []
        for r in reads:
            if r.w is not None:
                toks.append(r.w)
        for w in writes:
            if w.w is not None:
                toks.append(w.w)
            toks.extend(w.r.values())
        toks = [t for t in toks if t[0] is not sem]
        if not cont and sem.count > 0:
            toks.append((sem, sem.count))
        waits = self._need(q, toks)
        q.ops.append({"fn": fn, "waits": waits, "mark": False, "dsem": sem})
        sem.count += 16
        tok = (sem, sem.count)
        for r in reads:
            r.r[sem] = tok
        for w in writes:
            w.w = tok
            w.r = {}
        self.pending.append(tok)
        return tok

    def wait_tok(self, qname, toks):
        q = self.q[qname]
        waits = self._need(q, toks)
        if waits:
            q.ops.append({"fn": None, "waits": waits, "mark": False, "dsem": None})

    def barrier(self, keep=()):
        keep_toks = set()
        for r in keep:
            if r.w is not None:
                keep_toks.add(r.w)
        last = []
        for q in self.q.values():
            for i in range(len(q.ops) - 1, -1, -1):
                if q.ops[i]["fn"] is not None and q.ops[i]["dsem"] is None:
                    last.append((q, i))
                    break
        kept_sem = {}
        for (key, val) in keep_toks:
            if isinstance(key, DmaSem):
                kept_sem[key] = max(kept_sem.get(key, 0), val)
        dl = {}
        still = []
        for (sem, val) in self.pending:
            if sem in kept_sem and val <= kept_sem[sem]:
                still.append((sem, val))
                continue
            dl[sem] = max(dl.get(sem, 0), val)
        dtoks = list(dl.items())
        self.pending = still
        for qn in self.q:
            self.wait_tok(qn, last + dtoks)
        kept = {k: v for k, v in self.rcache.items() if v in keep}
        self.rcache = kept

    def emit(self):
        for q in self.q.values():
            c = 0
            for o in q.ops:
                if o["mark"]:
                    c += 1
                o["cnt"] = c
        stats = {}
        with self.nc.Block() as block:
            def run(q):
                def body(eng):
                    nw = 0
                    for o in q.ops:
                        for (key, idx) in o["waits"]:
                            if isinstance(key, Queue):
                                eng.wait_ge(key.sem, key.ops[idx]["cnt"])
                            else:
                                eng.wait_ge(key.handle, idx)
                            nw += 1
                        if o["fn"] is None:
                            continue
                        ins = o["fn"](eng)
                        if o["dsem"] is not None:
                            ins.then_inc(o["dsem"].handle, 16)
                        elif o["mark"]:
                            ins.then_inc(q.sem, 1)
                    stats[q.name] = (len(q.ops), nw)
                return body
            block.tensor(run(self.q["pe"]))
            block.scalar(run(self.q["act"]))
            block.vector(run(self.q["dve"]))
            block.gpsimd(run(self.q["pool"]))
            block.sync(run(self.q["sp"]))
        return stats


def MM(out, lhsT, rhs, start, stop):
    return lambda e: e.matmul(out, lhsT=lhsT, rhs=rhs, start=start, stop=stop, skip_group_check=True)


def TR(out, in_, ident):
    return lambda e: e.transpose(out=out, in_=in_, identity=ident)


def ACTF(out, in_, func, **kw):
    return lambda e: e.activation(out=out, in_=in_, func=func, **kw)


def TT(out, in0, in1, op):
    return lambda e: e.tensor_tensor(out=out, in0=in0, in1=in1, op=op)


def TS(out, in0, s1, s2, op0, op1=None):
    if op1 is None:
        return lambda e: e.tensor_scalar(out=out, in0=in0, scalar1=s1, scalar2=None, op0=op0)
    return lambda e: e.tensor_scalar(out=out, in0=in0, scalar1=s1, scalar2=s2, op0=op0, op1=op1)


def STT(out, in0, scalar, in1, op0, op1):
    return lambda e: e.scalar_tensor_tensor(out=out, in0=in0, scalar=scalar, in1=in1, op0=op0, op1=op1)


def CP(out, in_):
    return lambda e: e.tensor_copy(out=out, in_=in_)


def ACP(out, in_):
    return lambda e: e.copy(out=out, in_=in_)


def DMA(out, in_, slow=False):
    if slow:
        return lambda e: e.dma_start(out=out, in_=in_, allow_slow_non_contiguous=True)
    return lambda e: e.dma_start(out=out, in_=in_)


def MSET(ap, v):
    return lambda e: e.memset(ap, v)


def _t5_bucket_np(dist):
    n = np.maximum(dist, 0)
    nf = np.maximum(n, 1).astype(np.float32)
    v = (np.log(nf / np.float32(16)) / np.float32(math.log(1024 / 16)) * np.float32(16)).astype(np.float32)
    log_b = 16 + v.astype(np.int32)
    return np.where(n < 16, n, np.minimum(log_b, 31))


def host_consts():
    c = {}
    c["c_ident"] = np.eye(128, dtype=np.float32)
    c["c_J"] = np.ascontiguousarray(np.eye(128, dtype=np.float32)[::-1])
    oh = np.zeros((33, FLEN), np.float32)
    idx = np.arange(FLEN)
    d = idx - FOFF
    valid = (d >= 0) & (d < S)
    b = _t5_bucket_np(np.where(valid, d, 0))
    oh[b[valid], idx[valid]] = 1.0
    oh[32, idx[~valid]] = 1.0
    c["c_onehot"] = oh
    E = np.zeros((32, S), np.float32)
    E[np.arange(S) // 64, np.arange(S)] = 1.0
    c["c_E"] = E
    p = np.arange(128)[:, None]
    j = np.arange(128)[None, :]
    c["c_tri"] = np.where(j >= p, np.float32(NEG), np.float32(0)).astype(np.float32)
    selc = np.zeros((128, 8, 64), np.float32)
    for qi in range(8):
        qb = 8 + qi
        t = qb * 128 + np.arange(128)
        cur = t // 64
        for jb in range(32):
            f0 = (jb == 0)
            f1 = (cur - jb == 1)
            f2 = (cur - jb == 0)
            forced = f0 | f1 | f2
            keep = (~forced) & (jb <= cur)
            fn = np.where(f2, 3e6, np.where(f1, 2e6, np.where(f0, 1e6, np.where(jb > cur, -1.0, 0.0))))
            selc[:, qi, jb] = keep.astype(np.float32)
            selc[:, qi, 32 + jb] = fn
    c["c_selc"] = selc
    ov = np.zeros((128, 32), np.float32)
    for pp in range(128):
        cc = 127 - pp
        if cc > 126:
            continue
        for jb in range(32):
            if 4 * jb - 1 <= cc <= 4 * jb + 3:
                ov[pp, jb] = 1.0
    c["c_ovrev"] = ov
    return c


INPUT_SHAPES = {
    "x": [S, D], "rel_table": [32, 16], "attn_norm": [2, D], "ffn_norm": [2, D], "final_norm": [1, D],
    "a_w_qkv": [D, 1280], "a_b_qkv": [1, 1280], "a_sinks": [1, 16], "a_w_o": [D, D], "a_b_o": [1, D],
    "b_w_in": [D, 1840], "b_cmp_pos": [2, 32, 64], "b_cmp_w1": [2, 2048, 256], "b_cmp_w2": [2, 256, 64],
    "b_w_o": [D, D], "ffn_w_gu": [2, D, 2 * DFF], "ffn_w_down": [2, DFF, D],
    "c_ident": [128, 128], "c_J": [128, 128], "c_onehot": [33, FLEN], "c_E": [32, S], "c_tri": [128, 128],
    "c_selc": [128, 8, 64], "c_ovrev": [128, 32],
}


def build(layer_ids, final):
    nc = bass.Bass("TRN2", target_bir_lowering=False)
    I = {k: nc.dram_tensor(k, v, F32, kind="ExternalInput").ap() for k, v in INPUT_SHAPES.items()}
    out = nc.dram_tensor("out", [S, D], F32, kind="ExternalOutput").ap()
    Fd = nc.dram_tensor("Fd", [16, FLEN], F32, kind="Internal").ap()
    ksave = nc.dram_tensor("ksave", [2, 64, S], BF16, kind="Internal").ap()

    def dap(ap, off, pat):
        return bass.AP(ap.tensor, off, pat)

    with ExitStack() as st:
        P = Prog(nc, st)
        R = P.R
        h = P.sb("h", [128, NT, D], F32)
        arenaA = P.sb("arenaA", [128, 16384], BF16)
        arenaB = P.sb("arenaB", [128, 43008], BF16)
        wbuf = [P.sb("wbuf%d" % i, [128, KC, 256], BF16) for i in range(2)]
        tabX = P.sb("tabX", [128, 8, 128], F32)
        gbc = P.sb("gbc", [128, D], F32)
        selc = P.sb("selc", [128, 8, 64], F32)
        ident = P.sb("ident", [128, 128], BF16)
        J_bf = P.sb("J_bf", [128, 128], BF16)
        J32 = P.sb("J32", [128, 128], F32)
        tri = P.sb("tri", [128, 128], F32)
        ovrev = P.sb("ovrev", [128, 32], BF16)
        relaug = P.sb("relaug", [33, 16], F32)
        ssq = P.sb("ssq", [128, NT], F32)
        rstd = P.sb("rstd", [128, NT], F32)
        bq = P.sb("bq", [128, 8], F32)
        bk = P.sb("bk", [128, 1], F32)
        bv_bc = P.sb("bv_bc", [128, 128], F32)
        esink = P.sb("esink", [128, 16], F32)
        b31 = P.sb("b31", [128, 16], F32)
        den = P.sb("den", [128, 8], F32)
        rd = P.sb("rd", [128, 8], F32)
        fac = P.sb("fac", [128, 8], F32)
        impn = P.sb("impn", [128, 8, 32], F32)
        imp = P.sb("imp", [128, 32], F32)
        sc = P.sb("sc", [128, 32], F32)
        sc2 = P.sb("sc2", [128, 32], F32)
        m8 = P.sb("m8", [128, 16], F32)
        mb_bf = P.sb("mb_bf", [128, 32], BF16)
        MBT = P.sb("MBT", [128, 512], BF16)
        Vc = P.sb("Vc", [128, 2, 97], BF16)
        kcT2 = P.sb("kcT2", [128, 2, 128], BF16)
        posT = P.sb("posT", [128, 32], F32)
        w2dup = P.sb("w2dup", [128, 2, 128], BF16)
        w2v = P.sb("w2v", [128, 2, 64], BF16)

        bank = [P.ps("bank%d" % i, [128, 512], F32) for i in range(4)]
        Obuf = [P.ps("Obuf%d" % i, [128, 1024], F32) for i in range(2)]
        for i_ in range(2):
            bank += [Obuf[i_][:, 0:512], Obuf[i_][:, 512:1024]]

        def bank_bf(i):
            return bank[i][:].bitcast(BF16)

        misc_rr = [0]

        att_mode = [False]
        srr = [0]

        def next_sbank():
            srr[0] = (srr[0] + 1) % 4
            return srr[0]

        def misc_bank():
            if att_mode[0]:
                return next_sbank()
            misc_rr[0] ^= 1
            return 6 + misc_rr[0]

        obase = [4]

        def new_branch():
            obase[0] = 10 - obase[0]
            return obase[0]

        def carve(arena, boff, nbytes, dtype, pattern=None, **kw):
            v = arena[:, boff // 2:(boff + nbytes) // 2]
            if dtype is F32:
                v = v.bitcast(F32)
            if pattern:
                v = v.rearrange(pattern, **kw)
            return v

        hnT = carve(arenaA, 0, 32768, BF16, "p (k t) -> p k t", k=KC)
        tabs = carve(arenaA, 0, 32768, F32, "p (s g j) -> p s g j", s=8, g=8)
        blk = carve(arenaA, 0, 8192, BF16, "p (c l) -> p c l", c=128)
        xs_g = carve(arenaA, 8192, 1024, F32)
        t1_g = carve(arenaA, 9216, 1024, F32)
        gl_bf = carve(arenaA, 10240, 512, BF16)
        GT = carve(arenaA, 10752, 512, BF16)

        qTp = carve(arenaB, 0, 32768, BF16, "p (g t) -> p g t", g=8)
        kT1 = carve(arenaB, 32768, 4096, BF16)
        kT2 = carve(arenaB, 36864, 4096, BF16)
        V1 = carve(arenaB, 40960, 4160, BF16, "p (t k c) -> p t k c", t=NT, k=2)
        V2 = carve(arenaB, 45120, 4160, BF16, "p (t k c) -> p t k c", t=NT, k=2)
        gates = carve(arenaB, 49280, 3072, F32, "p (t c) -> p t c", t=NT)
        w_o = carve(arenaB, 52352, 16384, BF16, "p (k n) -> p k n", k=KC)
        w1sb = carve(arenaB, 52352, 16384, BF16, "p (l m) -> p l m", l=32)
        WK = 68736
        E_bf = carve(arenaB, WK + 4096, 4096, BF16)
        PT = [carve(arenaB, WK + 8192 + 1024 * i, 1024, BF16) for i in range(4)]
        o_acc = carve(arenaB, WK + 12288, 2048, F32, "p (g d) -> p g d", g=8)
        attn_bf = carve(arenaB, WK + 12288, 2048, BF16)
        attn_bf1 = carve(arenaB, WK + 14336, 1024, BF16)
        attnT = carve(arenaB, WK + 14336, 2048, BF16)
        attnT1 = carve(arenaB, WK + 15360, 1024, BF16)
        Htmp = carve(arenaB, WK, 4096, F32, "p (g j) -> p g j", g=8)
        cmpT = carve(arenaB, WK, 16512, F32, "p (k t) -> p k t", k=2)
        hn_bf = [carve(arenaB, 2048 * i, 2048, BF16) for i in range(2)]
        junk = carve(arenaB, 4096, 2048, BF16)
        ostage = [carve(arenaB, 8192 + 4096 * i, 4096, F32) for i in range(2)]
        hn_bf2 = [carve(arenaB, 71680 + 2048 * i, 2048, BF16) for i in range(2)]
        junk2 = carve(arenaB, 75776, 2048, BF16)
        ostage2 = [carve(arenaB, 77824 + 4096 * i, 4096, F32) for i in range(2)]
        oh_sb = carve(arenaB, 16384, 18432, F32)
        Fsb = carve(arenaB, 34816, 18432, F32)
        actT = carve(arenaB, 0, 45056, BF16, "p (f t) -> p f t", f=11)
        wd = carve(arenaB, 45056, 22528, BF16, "p (f n) -> p f n", f=11)
        sg = [carve(arenaB, 67584 + 2048 * i, 2048, F32) for i in range(2)]

        sem_w = [P.dsem("w0"), P.dsem("w1")]
        sem_misc = {"sp": [P.dsem("ms%d" % i) for i in range(6)], "pool": [P.dsem("mp%d" % i) for i in range(4)]}
        mrr = {"sp": 0, "pool": 0}

        def msem(qn):
            mrr[qn] = (mrr[qn] + 1) % len(sem_misc[qn])
            return sem_misc[qn][mrr[qn]]

        wrr = [0]

        def next_wbuf():
            i = wrr[0]
            wrr[0] ^= 1
            return i

        sem_x = [P.dsem("x%d" % i) for i in range(4)]
        xv = I["x"].rearrange("(t p) d -> p t d", p=128)
        for gi in range(4):
            for t in range(4 * gi, 4 * gi + 4):
                P.dma("sp", DMA(h[:, t, :], xv[:, t, :]), sem_x[gi], writes=[R("h", t)], cont=(t % 4 != 0))
            for t in range(4 * gi, 4 * gi + 4):
                R("h", t).w = (sem_x[gi], sem_x[gi].count)
        P.dma("pool", DMA(ident[:], I["c_ident"]), msem("pool"), writes=[R("ident")])
        P.dma("pool", DMA(J_bf[:], I["c_J"]), msem("pool"), writes=[R("J_bf")])
        P.dma("sp", DMA(J32[:], I["c_J"]), msem("sp"), writes=[R("J32")])
        P.dma("sp", DMA(tri[:], I["c_tri"]), msem("sp"), writes=[R("tri")])
        P.dma("sp", DMA(selc[:], I["c_selc"]), msem("sp"), writes=[R("selc")])
        P.dma("pool", DMA(ovrev[:], I["c_ovrev"]), msem("pool"), writes=[R("ovrev")])
        P.op("pool", MSET(relaug[32:33, :], NEG), writes=[R("relaug")])
        P.dma("sp", DMA(relaug[0:32, :], I["rel_table"]), msem("sp"), writes=[R("relaug")])
        P.dma("sp", DMA(oh_sb[0:33, :], I["c_onehot"]), msem("sp"), writes=[R("oh")])
        P.dma("sp", DMA(b31[:], dap(I["rel_table"], 31 * 16, [[0, 128], [1, 16]])), msem("sp"), writes=[R("b31")])
        for c in range(FLEN // 512):
            bi = c % 2
            P.op("pe", MM(bank[bi][0:16, :], relaug[0:33, 0:16], oh_sb[0:33, c * 512:(c + 1) * 512], True, True),
                 reads=[R("relaug"), R("oh")], writes=[R("bank", bi)])
            P.op("act", ACP(Fsb[0:16, c * 512:(c + 1) * 512], bank[bi][0:16, :]), reads=[R("bank", bi)], writes=[R("Fsb")])
        sem_F = P.dsem("F")
        P.dma("sp", DMA(Fd, Fsb[0:16, :]), sem_F, reads=[R("Fsb")], writes=[R("Fd")])

        def load_gain(gain_ap_row):
            P.dma("sp", DMA(gbc[:], dap(gain_ap_row, gain_ap_row.offset, [[0, 128], [1, D]])), msem("sp"), writes=[R("gbc")])

        def norm_tiles(tiles, to_out, tmp):
            junk_, hn_, os_ = tmp
            t0_, t1_ = tiles[0], tiles[-1] + 1
            for i in tiles:
                P.op("act", ACTF(junk_, h[:, i, :], AF.Square, accum_out=ssq[:, i:i + 1]),
                     reads=[R("h", i)], writes=[R("ssq", i), R("junk")])
            sr = [R("ssq", i) for i in tiles]
            P.op("dve", TS(rstd[:, t0_:t1_], ssq[:, t0_:t1_], 1.0 / D, 1e-6, ALU.mult, ALU.add), reads=sr, writes=[R("rstd", t0_)])
            P.op("act", ACTF(rstd[:, t0_:t1_], rstd[:, t0_:t1_], AF.Sqrt), reads=[R("rstd", t0_)], writes=[R("rstd", t0_)])
            P.op("dve", lambda e: e.reciprocal(out=rstd[:, t0_:t1_], in_=rstd[:, t0_:t1_]), reads=[R("rstd", t0_)], writes=[R("rstd", t0_)])
            for i in tiles:
                sl = i % 2
                if to_out:
                    P.op("dve", STT(os_[sl], h[:, i, :], rstd[:, i:i + 1], gbc[:], ALU.mult, ALU.mult),
                         reads=[R("h", i), R("rstd", t0_), R("gbc")], writes=[R("ostage", sl)])
                    P.dma("sp", DMA(out[i * 128:(i + 1) * 128, :], os_[sl]), sem_out[sl], reads=[R("ostage", sl)])
                    continue
                P.op("dve", STT(hn_[sl], h[:, i, :], rstd[:, i:i + 1], gbc[:], ALU.mult, ALU.mult),
                     reads=[R("h", i), R("rstd", t0_), R("gbc")], writes=[R("hn_bf", sl)])
                mb = misc_bank()
                bb = bank_bf(mb)
                for kc in range(KC):
                    P.op("pe", TR(bb[:, kc * 128:(kc + 1) * 128], hn_[sl][:, kc * 128:(kc + 1) * 128], ident[:]),
                         reads=[R("hn_bf", sl), R("ident")], writes=[R("bank", mb)])
                P.op("act", ACP(hnT[:, :, i * 128:(i + 1) * 128], bb.rearrange("p (k t) -> p k t", k=KC)),
                     reads=[R("bank", mb)], writes=[R("hnT", i)])

        def norm_phase(gain_ap_row, to_out=False):
            load_gain(gain_ap_row)
            norm_tiles(list(range(NT)), to_out, (junk, hn_bf, ostage))

        def load_w256(src2d, col_specs):
            wi = next_wbuf()
            srcv = src2d.rearrange("(k p) n -> p k n", p=128)
            for ci, (d0, s0, n) in enumerate(col_specs):
                P.dma("pool", DMA(wbuf[wi][:, :, d0:d0 + n], srcv[:, :, s0:s0 + n]), sem_w[wi], writes=[R("wbuf", wi)], cont=(ci > 0))
            return wi

        pre_w = {}

        def load_qpair_chunk(src2d, c, key=None):
            if key is not None and key in pre_w:
                return pre_w.pop(key)
            wi = next_wbuf()
            srcv = src2d.rearrange("(k p) n -> p k n", p=128)
            dst5 = wbuf[wi][:].rearrange("p k (pl two d) -> p k pl two d", pl=2, two=2)
            for two in range(2):
                s = srcv[:, :, two * 512 + 128 * c: two * 512 + 128 * c + 128].rearrange("p k (pl d) -> p k pl d", pl=2)
                for kc in range(KC):
                    P.dma("pool", DMA(dst5[:, kc, :, two, :], s[:, kc, :, :]), sem_w[wi], writes=[R("wbuf", wi)], cont=not (two == 0 and kc == 0))
            return wi

        def hn_reads(tc):
            return [R("hnT", t) for t in range(4 * tc, 4 * tc + 4)]

        def proj_feat(wi, c0, evac, tag):
            for tc in range(4):
                mb = misc_bank()
                for kc in range(KC):
                    P.op("pe", MM(bank[mb][:], wbuf[wi][:, kc, c0:c0 + 128], hnT[:, kc, tc * 512:(tc + 1) * 512],
                                  kc == 0, kc == KC - 1),
                         reads=[R("wbuf", wi)] + hn_reads(tc), writes=[R("bank", mb)])
                evac(tc, mb)

        def proj_tok(wi, c0, n, evac):
            for t in range(NT):
                mb = misc_bank()
                for kc in range(KC):
                    P.op("pe", MM(bank[mb][:, 0:n], hnT[:, kc, t * 128:(t + 1) * 128], wbuf[wi][:, kc, c0:c0 + n],
                                  kc == 0, kc == KC - 1),
                         reads=[R("wbuf", wi), R("hnT", t)], writes=[R("bank", mb)])
                evac(t, mb)

        def prefetch_q(src2d, name):
            for c in range(2):
                pre_w[(name, c)] = load_qpair_chunk(src2d, c)

        def q_proj(src2d, has_bias, name):
            for c in range(4):
                wi = load_qpair_chunk(src2d, c, key=(name, c))
                for pl in range(2):
                    g = 2 * c + pl

                    def ev(tc, mb, g=g):
                        if has_bias:
                            P.op("dve", TS(qTp[:, g, tc * 512:(tc + 1) * 512], bank[mb][:], bq[:, g:g + 1], 0.125, ALU.add, ALU.mult),
                                 reads=[R("bank", mb), R("bq")], writes=[R("qTp", g, tc)])
                        else:
                            P.op("dve", TS(qTp[:, g, tc * 512:(tc + 1) * 512], bank[mb][:], 0.125, None, ALU.mult),
                                 reads=[R("bank", mb)], writes=[R("qTp", g, tc)])
                    proj_feat(wi, pl * 128, ev, "q")

        pipe = []

        def unit(hk, qb, kT, kcol0, tab4, const_bias, Vrhs, ncols, first, last, emask_kb=None, tabres=None,
                 kres=None, vres=None, pre=None, post=None, ob=4):
            pipe.append(dict(hk=hk, qb=qb, kT=kT, kcol0=kcol0, tab4=tab4, const_bias=const_bias, Vrhs=Vrhs, ncols=ncols,
                             first=first, last=last, emask_kb=emask_kb, tabres=tabres, kres=kres, vres=vres, pre=pre, post=post, ob=ob))

        SKEW = 3

        def emit_S(u, half, sl):
            hk, qb = u["hk"], u["qb"]
            if half == 0 and u["pre"] is not None:
                u["pre"]()
            qrd = [R("qTp", g, qb // 4) for g in range(4 * half, 4 * half + 4)]
            bi = next_sbank()
            rhs = qTp[:, 4 * half:4 * half + 4, qb * 128:(qb + 1) * 128]
            P.op("pe", MM(bank[bi][:], u["kT"][:, u["kcol0"]:u["kcol0"] + 128], rhs, True, u["emask_kb"] is None),
                 reads=[u["kres"] or R("kT")] + qrd, writes=[R("bank", bi)])
            if u["emask_kb"] is not None:
                ek = u["emask_kb"]
                P.op("pe", MM(bank[bi][:], E_bf[:, ek * 128:(ek + 1) * 128], MBT[:], False, True),
                     reads=[R("E"), R("MBT")], writes=[R("bank", bi)])
            bv3 = bank[bi][:].rearrange("p (g j) -> p g j", g=4)
            if u["const_bias"]:
                for g4 in range(4):
                    hcol = 8 * hk + 4 * half + g4
                    P.op("act", ACTF(PT[sl][:, g4 * 128:(g4 + 1) * 128], bank[bi][:, g4 * 128:(g4 + 1) * 128], AF.Exp,
                                     bias=b31[:, hcol:hcol + 1]),
                         reads=[R("bank", bi), R("b31")], writes=[R("PT", sl)])
                return
            P.op("dve", TT(bv3, bv3, u["tab4"][:, 4 * half:4 * half + 4, :], ALU.add),
                 reads=[R("bank", bi), u["tabres"] or R("tab")], writes=[R("bank", bi)])
            P.op("act", ACTF(PT[sl], bank[bi][:], AF.Exp), reads=[R("bank", bi)], writes=[R("PT", sl)])

        def emit_PV(u, half, sl):
            ncols = u["ncols"]
            ob = u["ob"] + half
            for g4 in range(4):
                oap = bank[ob][:, g4 * ncols:(g4 + 1) * ncols]
                P.op("pe", MM(oap, PT[sl][:, g4 * 128:(g4 + 1) * 128], u["Vrhs"], u["first"] and (g4 == 0), u["last"]),
                     reads=[R("PT", sl), u["vres"] or R("V")], writes=[R("bank", ob)])
            if half == 1 and u["post"] is not None:
                u["post"]()

        deferred = []
        cur_i = [0]

        def defer(k, fn):
            deferred.append((cur_i[0] + k, fn))

        def run_deferred(upto):
            keep_ = []
            for (due, fn) in list(deferred):
                if due <= upto:
                    deferred.remove((due, fn))
                    fn()
            return

        def flush_pipe():
            att_mode[0] = True
            hu = [(u, half) for u in pipe for half in range(2)]
            n = len(hu)
            for i in range(n + SKEW):
                cur_i[0] = i
                run_deferred(i)
                if i < n:
                    emit_S(hu[i][0], hu[i][1], i % 4)
                if i >= SKEW:
                    j = i - SKEW
                    emit_PV(hu[j][0], hu[j][1], j % 4)
            while deferred:
                cur_i[0] += 1
                run_deferred(cur_i[0])
            del pipe[:]
            att_mode[0] = False

        def O2(ncols, ob):
            return Obuf[(ob - 4) // 2][:].rearrange("p (h c) -> p h c", h=2)[:, :, 0:4 * ncols].rearrange("p h (g c) -> p h g c", g=4)

        def Ores(ob):
            return [R("bank", ob), R("bank", ob + 1)]

        def g8(ap):
            return ap.rearrange("p (h g) -> p h g", h=2)

        def Oview(b, ncols, ob):
            return bank[ob + b][:, 0:4 * ncols].rearrange("p (g c) -> p g c", g=4)

        hrr = [0]

        def build_table(dst, hk, off, add_tri, hbufs, tres):
            base = (8 * hk) * FLEN + FOFF + off * 128 - 127
            hrr[0] = (hrr[0] + 1) % len(hbufs)
            Htmp, hres = hbufs[hrr[0]]
            P.dma("sp", DMA(Htmp, dap(Fd, base, [[1, 128], [FLEN, 8], [1, 128]])), msem("sp"),
                  reads=[R("Fd")], writes=[hres])
            for half in range(2):
                mb = misc_bank()
                P.op("pe", MM(bank[mb][:], J32[:], Htmp[:, 4 * half:4 * half + 4, :].rearrange("p g j -> p (g j)"), True, True),
                     reads=[R("J32"), hres], writes=[R("bank", mb)])
                dv = dst[:, 4 * half:4 * half + 4, :]
                if add_tri:
                    P.op("dve", TT(dv, bank[mb][:].rearrange("p (g j) -> p g j", g=4),
                                   tri[:].unsqueeze(1).to_broadcast([128, 4, 128]), ALU.add),
                         reads=[R("bank", mb), R("tri")], writes=[tres])
                else:
                    P.op("act", ACP(dv, bank[mb][:].rearrange("p (g j) -> p g j", g=4)),
                         reads=[R("bank", mb)], writes=[tres])

        def outproj(qb, aT, nk, kc0, wtile):
            for nh in range(2):
                mb = misc_bank()
                for kc in range(nk):
                    P.op("pe", MM(bank[mb][:], aT[:, kc * 128:(kc + 1) * 128], wtile[:, kc0 + kc, nh * 512:(nh + 1) * 512],
                                  kc == 0, kc == nk - 1),
                         reads=[R("attnT"), R("w_o")], writes=[R("bank", mb)])
                P.op("dve", TT(h[:, qb, nh * 512:(nh + 1) * 512], bank[mb][:], h[:, qb, nh * 512:(nh + 1) * 512], ALU.add),
                     reads=[R("bank", mb), R("h", qb)], writes=[R("h", qb)])

        def load_wo(src2d):
            srcv = src2d.rearrange("(k p) n -> p k n", p=128)
            for kc in range(KC):
                P.dma("pool", DMA(w_o[:, kc, :], srcv[:, kc, :]), sem_wo, writes=[R("w_o")], cont=(kc > 0))

        sem_wo = P.dsem("wo")
        sem_wd = P.dsem("wd")
        sem_out = [P.dsem("o0"), P.dsem("o1")]

        def prefetch_ffn(layer):
            wgu = I["ffn_w_gu"][layer]
            wdv = I["ffn_w_down"][layer].rearrange("(f p) n -> p f n", p=128)
            for fc in range(2):
                pre_w[("gu", layer, fc)] = load_w256(wgu, [(0, 128 * fc, 128), (128, DFF + 128 * fc, 128)])
            for fl in range(11):
                P.dma("pool", DMA(wd[:, fl, :], wdv[:, fl, :]), sem_wd, writes=[R("wd")], cont=(fl > 0))
            pre_w[("wd", layer)] = True

        def wkeep():
            return [R("wbuf", 0), R("wbuf", 1), R("wd")]

        def ffn(layer, fuse_norm=None):
            wgu = I["ffn_w_gu"][layer]
            wdn = I["ffn_w_down"][layer]
            wdv = wdn.rearrange("(f p) n -> p f n", p=128)
            sgr = [0]
            for half in range(2):
                if not (half == 0 and ("wd", layer) in pre_w):
                    for fl in range(11):
                        P.dma("pool", DMA(wd[:, fl, :], wdv[:, 11 * half + fl, :]), sem_wd, writes=[R("wd")], cont=(fl > 0))
                else:
                    pre_w.pop(("wd", layer))
                for fl in range(11):
                    fc = 11 * half + fl
                    if ("gu", layer, fc) in pre_w:
                        wi = pre_w.pop(("gu", layer, fc))
                    else:
                        wi = load_w256(wgu, [(0, 128 * fc, 128), (128, DFF + 128 * fc, 128)])
                    for tc in range(4):
                        bg = 2 * (tc % 2)
                        bu = bg + 1
                        for kc in range(KC):
                            P.op("pe", MM(bank[bg][:], wbuf[wi][:, kc, 0:128], hnT[:, kc, tc * 512:(tc + 1) * 512], kc == 0, kc == KC - 1),
                                 reads=[R("wbuf", wi)] + hn_reads(tc), writes=[R("bank", bg)])
                        for kc in range(KC):
                            P.op("pe", MM(bank[bu][:], wbuf[wi][:, kc, 128:256], hnT[:, kc, tc * 512:(tc + 1) * 512], kc == 0, kc == KC - 1),
                                 reads=[R("wbuf", wi)] + hn_reads(tc), writes=[R("bank", bu)])
                        si = sgr[0]
                        sgr[0] ^= 1
                        P.op("act", ACTF(sg[si], bank[bg][:], AF.Silu), reads=[R("bank", bg)], writes=[R("sg", si)])
                        P.op("dve", TT(actT[:, fl, tc * 512:(tc + 1) * 512], sg[si], bank[bu][:], ALU.mult),
                             reads=[R("sg", si), R("bank", bu)], writes=[R("actT", tc)])
                if half == 0:
                    for fc in (11, 12):
                        pre_w[("gu", layer, fc)] = load_w256(wgu, [(0, 128 * fc, 128), (128, DFF + 128 * fc, 128)])
                for t in range(NT):
                    for nh in range(2):
                        mb = 4 + (2 * t + nh) % 4
                        for fl in range(11):
                            P.op("pe", MM(bank[mb][:], actT[:, fl, t * 128:(t + 1) * 128], wd[:, fl, nh * 512:(nh + 1) * 512], fl == 0, fl == 10),
                                 reads=[R("actT", t // 4), R("wd")], writes=[R("bank", mb)])
                        P.op("dve", TT(h[:, t, nh * 512:(nh + 1) * 512], bank[mb][:], h[:, t, nh * 512:(nh + 1) * 512], ALU.add),
                             reads=[R("bank", mb), R("h", t)], writes=[R("h", t)])
                    if half == 1 and fuse_norm is not None and t % 4 == 3:
                        norm_tiles(list(range(t - 3, t + 1)), fuse_norm == "out", (junk2, hn_bf2, ostage2))

        def layer_A():
            W = I["a_w_qkv"]
            b = I["a_b_qkv"]
            for two in range(2):
                P.dma("sp", DMA(bq[64 * two:64 * two + 64, :], dap(b, 512 * two, [[1, 64], [64, 8]]), slow=True), msem("sp"), writes=[R("bq")])
            P.dma("sp", DMA(bk[:], dap(b, 1024, [[1, 128], [1, 1]]), slow=True), msem("sp"), writes=[R("bk")])
            P.dma("sp", DMA(bv_bc[:], dap(b, 1152, [[0, 128], [1, 128]])), msem("sp"), writes=[R("bv")])
            q_proj(W, True, "qA")
            wi = load_w256(W, [(0, 1024, 256)])

            P.op("pool", MSET(kT1[64:128, :], 0.0), writes=[R("kTz", 0)])
            P.op("pool", MSET(kT2[0:64, :], 0.0), writes=[R("kTz", 1)])

            def ev_k(tc, mb):
                P.op("dve", TS(kT1[0:64, tc * 512:(tc + 1) * 512], bank[mb][0:64, :], bk[0:64, 0:1], None, ALU.add),
                     reads=[R("bank", mb), R("bk")], writes=[R("kT")])
                P.op("dve", TS(kT2[64:128, tc * 512:(tc + 1) * 512], bank[mb][64:128, :], bk[64:128, 0:1], None, ALU.add),
                     reads=[R("bank", mb), R("bk")], writes=[R("kT")])
            proj_feat(wi, 0, ev_k, "k")

            def ev_v(t, mb):
                P.op("dve", TT(V1[:, t, :, 0:64], bank[mb][:, 0:128].rearrange("p (k d) -> p k d", k=2),
                               bv_bc[:].rearrange("p (k d) -> p k d", k=2), ALU.add),
                     reads=[R("bank", mb), R("bv")], writes=[R("V")])
            proj_tok(wi, 128, 128, ev_v)
            P.op("pool", MSET(V1[:, :, :, 64], 1.0), writes=[R("V")])
            P.barrier()
            P.dma("sp", DMA(gbc[:], dap(I["a_b_o"], 0, [[0, 128], [1, D]])), msem("sp"), writes=[R("gbc")])
            P.dma("sp", DMA(esink[:], dap(I["a_sinks"], 0, [[0, 128], [1, 16]])), msem("sp"), writes=[R("esink")])
            P.op("act", ACTF(esink[:], esink[:], AF.Exp), reads=[R("esink")], writes=[R("esink")])
            for hk in range(2):
                build_table(tabs[:, hk], hk, 0, False, ((Htmp, R("Htmp")), (tabX[:], R("tabX"))), R("tab", hk))
                build_table(tabs[:, 2 + hk], hk, 1, True, ((Htmp, R("Htmp")), (tabX[:], R("tabX"))), R("tab", 2 + hk))
            load_wo(I["a_w_o"])
            def epiA(hk, qb, ob):
                P.op("dve", TT(g8(den[:]), O2(65, ob)[:, :, :, 64], g8(esink[:, 8 * hk:8 * hk + 8]), ALU.add),
                     reads=Ores(ob) + [R("esink")], writes=[R("den")])
                P.op("dve", lambda e: e.reciprocal(out=rd[:], in_=den[:]), reads=[R("den")], writes=[R("rd")])
                c0 = 8 * hk * 64
                P.op("dve", TT(attn_bf[:, c0:c0 + 512].rearrange("p (h g d) -> p h g d", h=2, g=4), O2(65, ob)[:, :, :, 0:64],
                               g8(rd[:]).unsqueeze(3).to_broadcast([128, 2, 4, 64]), ALU.mult),
                     reads=Ores(ob) + [R("rd")], writes=[R("attn_bf")])
                if hk == 1:
                    P.op("pool", TT(h[:, qb, :], h[:, qb, :], gbc[:], ALU.add), reads=[R("h", qb), R("gbc")], writes=[R("h", qb)])

                    def opA(qb=qb):
                        mb = misc_bank()
                        bb_ = bank_bf(mb)
                        for kc in range(KC):
                            P.op("pe", TR(bb_[:, kc * 128:(kc + 1) * 128], attn_bf[:, kc * 128:(kc + 1) * 128], ident[:]),
                                 reads=[R("attn_bf"), R("ident")], writes=[R("bank", mb)])
                        P.op("act", ACP(attnT, bb_), reads=[R("bank", mb)], writes=[R("attnT")])
                        outproj(qb, attnT, KC, 0, w_o)
                    defer(3, opA)

            for qb in range(NT):
                for hk in range(2):
                    kbs = [kb for kb in (qb - 1, qb) if kb >= 0]
                    for i, kb in enumerate(kbs):
                        tab = tabs[:, hk] if kb == qb else tabs[:, 2 + hk]
                        tres_ = R("tab", hk) if kb == qb else R("tab", 2 + hk)
                        lastu = (i == len(kbs) - 1)
                        if i == 0:
                            ob_ = new_branch()
                        unit(hk, qb, kT1 if hk == 0 else kT2, kb * 128, tab, False, V1[:, kb, hk, :], 65, i == 0, lastu, tabres=tres_,
                             post=(lambda hk=hk, qb=qb, ob_=ob_: epiA(hk, qb, ob_)) if lastu else None, ob=ob_)
            flush_pipe()
            P.barrier()

        def layer_B():
            W = I["b_w_in"]
            for which in range(2):
                w1v = I["b_cmp_w1"][which].rearrange("(l d) m -> d l m", d=64)
                for lq in range(4):
                    P.dma("pool", DMA(w1sb[64 * which:64 * which + 64, 8 * lq:8 * lq + 8, :], w1v[:, 8 * lq:8 * lq + 8, :]), sem_wo,
                          writes=[R("w1sb")], cont=not (which == 0 and lq == 0))
                P.dma("sp", DMA(posT[64 * which:64 * which + 64, :], I["b_cmp_pos"][which].rearrange("l d -> d l"), slow=True),
                      msem("sp"), writes=[R("posT")])
            w2k = I["b_cmp_w2"][0].rearrange("(c p) d -> p c d", p=128)
            for two in range(2):
                P.dma("pool", DMA(w2dup[:, :, 64 * two:64 * two + 64], w2k), msem("pool"), writes=[R("w2k", two)])
            P.dma("pool", DMA(w2v[:], I["b_cmp_w2"][1].rearrange("(c p) d -> p c d", p=128)), msem("pool"), writes=[R("w2v")])
            q_proj(W, False, "qB")
            P.op("pool", MSET(cmpT[:, :, 2048:2064], 0.0), writes=[R("cmpT", 0), R("cmpT", 1)])
            wi = load_w256(W, [(0, 1024, 64), (64, 1152, 64), (128, 1088, 64), (192, 1216, 64)])
            for hk_ in range(2):
                def ev_c(tc, mb, hk_=hk_):
                    P.op("act", ACP(cmpT[:, hk_, tc * 512:(tc + 1) * 512], bank[mb][:]), reads=[R("bank", mb)], writes=[R("cmpT", hk_)])
                proj_feat(wi, 128 * hk_, ev_c, "c")
            for (c0, kTd, Vd, nm) in ((1280, kT1, V1, "s"), (1536, kT2, V2, "w")):
                wi = load_w256(W, [(0, c0, 256)])

                def ev_k(tc, mb, kTd=kTd, nm=nm):
                    P.op("act", ACP(kTd[:, tc * 512:(tc + 1) * 512], bank[mb][:]), reads=[R("bank", mb)], writes=[R("kT", nm)])

                def ev_v(t, mb, Vd=Vd, nm=nm):
                    P.op("dve", CP(Vd[:, t, :, 0:64], bank[mb][:, 0:128].rearrange("p (k d) -> p k d", k=2)),
                         reads=[R("bank", mb)], writes=[R("V", nm)])
                proj_feat(wi, 0, ev_k, "k")
                proj_tok(wi, 128, 128, ev_v)
                P.op("pool", MSET(Vd[:, :, :, 64], 1.0), writes=[R("V", nm)])
            wi = load_w256(W, [(0, 1792, 48)])

            def ev_g(t, mb):
                P.op("act", ACTF(gates[:, t, :], bank[mb][:, 0:48], AF.Sigmoid), reads=[R("bank", mb)], writes=[R("gates")])
            proj_tok(wi, 0, 48, ev_g)
            P.barrier(keep=[R("w1sb"), R("posT"), R("w2k", 0), R("w2k", 1), R("w2v")])
            P.op("pool", MSET(kcT2[:], 0.0), writes=[R("kcT2")])
            for hk in range(2):
                for a_ in range(2):
                    P.op("dve", TT(blk[:, :, 16 * a_:16 * a_ + 16], cmpT[:, hk, 16 * a_:16 * a_ + 2048].rearrange("p (c r) -> p c r", r=16),
                                   posT[:, 16 * a_:16 * a_ + 16].unsqueeze(1).to_broadcast([128, 128, 16]), ALU.add),
                         reads=[R("cmpT", hk), R("posT")], writes=[R("blk")])
                gb = [misc_bank(), misc_bank()]
                for l in range(32):
                    for which in range(2):
                        P.op("pe", MM(bank[gb[which]][:, 0:256], blk[64 * which:64 * which + 64, :, l],
                                      w1sb[64 * which:64 * which + 64, l, :], l == 0, l == 31),
                             reads=[R("blk"), R("w1sb")], writes=[R("bank", gb[which])])
                for which in range(2):
                    mb = gb[which]
                    P.op("act", ACP(xs_g, bank[mb][:, 0:256]), reads=[R("bank", mb)], writes=[R("xs")])
                    P.op("dve", TT(t1_g, xs_g, xs_g, ALU.mult), reads=[R("xs")], writes=[R("t1")])
                    P.op("dve", TS(t1_g, t1_g, 0.044715, 1.0, ALU.mult, ALU.add), reads=[R("t1")], writes=[R("t1")])
                    P.op("dve", TT(t1_g, t1_g, xs_g, ALU.mult), reads=[R("t1"), R("xs")], writes=[R("t1")])
                    P.op("act", ACTF(t1_g, t1_g, AF.Sigmoid, scale=1.5957691216057308), reads=[R("t1")], writes=[R("t1")])
                    P.op("dve", TT(gl_bf, xs_g, t1_g, ALU.mult), reads=[R("t1"), R("xs")], writes=[R("gl")])
                    bb_ = bank_bf(mb)
                    for ch in range(2):
                        P.op("pe", TR(bb_[:, ch * 128:(ch + 1) * 128], gl_bf[:, ch * 128:(ch + 1) * 128], J_bf[:]),
                             reads=[R("gl"), R("J_bf")], writes=[R("bank", mb)])
                    P.op("act", ACP(GT, bb_[:, 0:256]), reads=[R("bank", mb)], writes=[R("GT")])
                    if which == 0:
                        for ch in range(2):
                            P.op("pe", MM(bank[mb][:, 0:128], w2dup[:, ch, :], GT[:, ch * 128:(ch + 1) * 128], ch == 0, ch == 1),
                                 reads=[R("GT"), R("w2k", 0), R("w2k", 1)], writes=[R("bank", mb)])
                        P.op("dve", CP(kcT2[64 * hk:64 * hk + 64, hk, :], bank[mb][64 * hk:64 * hk + 64, 0:128]),
                             reads=[R("bank", mb)], writes=[R("kcT2")])
                    else:
                        for ch in range(2):
                            P.op("pe", MM(bank[mb][:, 0:64], GT[:, ch * 128:(ch + 1) * 128], w2v[:, ch, :], ch == 0, ch == 1),
                                 reads=[R("GT"), R("w2v")], writes=[R("bank", mb)])
                        P.op("dve", CP(Vc[:, hk, 0:64], bank[mb][:, 0:64]), reads=[R("bank", mb)], writes=[R("Vc")])
            P.op("pool", MSET(Vc[:, :, 64:65], 1.0), writes=[R("Vc")])
            for hk in range(2):
                P.op("pool", CP(Vc[:, hk, 65:97], ovrev[:]), reads=[R("ovrev")], writes=[R("Vc")])
            P.barrier()
            P.dma("pool", DMA(E_bf[0:32, :], I["c_E"]), msem("pool"), writes=[R("E")])
            P.op("pool", MSET(E_bf[32:64, :], 0.0), writes=[R("E")])
            P.op("pool", MSET(E_bf[64:128, :], 0.0), writes=[R("E")])
            P.op("pool", MSET(MBT[:], 0.0), writes=[R("MBT")])
            gview = gates[:].rearrange("p t (h b) -> p t h b", b=3)
            for hk in range(2):
                if hk == 0:
                    for i_, kTx in enumerate((kT1, kT2)):
                        P.dma("sp", DMA(ksave[i_], kTx[64:128, :]), msem("sp"), reads=[R("kT")], writes=[R("ksave")])
                    for kTx in (kT1, kT2):
                        P.op("pool", MSET(kTx[64:128, :], 0.0), writes=[R("kT")])
                else:
                    for kTx in (kT1, kT2):
                        P.op("pool", MSET(kTx[0:64, :], 0.0), writes=[R("kT")])
                    for i_, kTx in enumerate((kT1, kT2)):
                        P.dma("sp", DMA(kTx[64:128, :], ksave[i_]), msem("sp"), reads=[R("ksave")], writes=[R("kT")])
                tabc = gbc[:].rearrange("p (g j) -> p g j", g=8)
                for qb in range(NT):
                    def pre_c(hk=hk, qb=qb):
                        if qb < 8:
                            hb = ((Htmp, R("Htmp")), (tabX[:], R("tabX"))) if qb < 4 else ((Htmp, R("Htmp")),)
                            build_table(tabs[:, qb], hk, qb, False, hb, R("tab", qb))
                        if qb == 4:
                            P.op("dve", TT(tabX[:], tabs[:, 4], tri[:].unsqueeze(1).to_broadcast([128, 8, 128]), ALU.add),
                                 reads=[R("tab", 4), R("tri")], writes=[R("tabX")])
                        if hk == 0 and qb == 0:
                            load_wo(I["b_w_o"])
                        P.dma("sp", DMA(tabc, dap(Fd, (8 * hk) * FLEN + qb * 128, [[16, 128], [FLEN, 8], [1, 128]])), msem("sp"),
                              reads=[R("Fd")], writes=[R("gbc")])
                    ob_ = new_branch()
                    unit(hk, qb, kcT2[:, hk, :], 0, tabc, False, Vc[:, hk, :], 97, True, True, tabres=R("gbc"),
                         kres=R("kcT2"), vres=R("Vc"), pre=pre_c, post=(lambda hk=hk, qb=qb, ob_=ob_: epi_cmp(hk, qb, ob_)), ob=ob_)
                    kbs = [kb for kb in range(qb - 4, qb + 1) if kb >= 0]
                    for i, kb in enumerate(kbs):
                        off = qb - kb
                        tab = tabX[:] if off == 4 else tabs[:, off]
                        rt = R("tabX") if off == 4 else R("tab", off)
                        lastu = (i == len(kbs) - 1)
                        if i == 0:
                            ob_ = new_branch()
                        unit(hk, qb, kT2, kb * 128, tab, False, V2[:, kb, hk, :], 65, i == 0, lastu, tabres=rt,
                             post=(lambda hk=hk, qb=qb, ob_=ob_: branch_epilogue(hk, qb, 2, False, ob_)) if lastu else None, ob=ob_)
                    for kb in range(qb + 1):
                        off = qb - kb
                        em = kb if qb >= 8 else None
                        lastu = (kb == qb)
                        if kb == 0:
                            ob_ = new_branch()
                        po = (lambda hk=hk, qb=qb, ob_=ob_: (branch_epilogue(hk, qb, 1, True, ob_), defer(6, lambda: outproj_B(hk, qb)))) if lastu else None
                        if off <= 7:
                            unit(hk, qb, kT1, kb * 128, tabs[:, off], False, V1[:, kb, hk, :], 65, kb == 0, lastu, emask_kb=em, post=po, ob=ob_,
                                 tabres=R("tab", off))
                        else:
                            unit(hk, qb, kT1, kb * 128, None, True, V1[:, kb, hk, :], 65, kb == 0, lastu, emask_kb=em, post=po, ob=ob_)
                flush_pipe()
            P.barrier()

        def outproj_B(hk, qb):
            mb = misc_bank()
            bb_ = bank_bf(mb)
            for kc in range(4):
                P.op("pe", TR(bb_[:, kc * 128:(kc + 1) * 128], attn_bf1[:, kc * 128:(kc + 1) * 128], ident[:]),
                     reads=[R("attn_bf"), R("ident")], writes=[R("bank", mb)])
            P.op("act", ACP(attnT1, bb_[:, 0:512]), reads=[R("bank", mb)], writes=[R("attnT")])
            outproj(qb, attnT1, 4, 4 * hk, w_o)

        def epi_cmp(hk, qb, ob):
            gview = gates[:].rearrange("p t (h b) -> p t h b", b=3)
            P.op("dve", TS(g8(den[:]), O2(97, ob)[:, :, :, 64], 1e-30, None, ALU.max), reads=Ores(ob), writes=[R("den")])
            P.op("dve", lambda e: e.reciprocal(out=rd[:], in_=den[:]), reads=[R("den")], writes=[R("rd")])
            P.op("dve", TT(fac[:], rd[:], gview[:, qb, 8 * hk:8 * hk + 8, 0], ALU.mult), reads=[R("rd"), R("gates")], writes=[R("fac")])
            P.op("dve", TT(o_acc.rearrange("p (h g) d -> p h g d", h=2), O2(97, ob)[:, :, :, 0:64],
                           g8(fac[:]).unsqueeze(3).to_broadcast([128, 2, 4, 64]), ALU.mult),
                 reads=Ores(ob) + [R("fac")], writes=[R("o_acc")])
            if qb >= 8:
                P.op("dve", TT(impn[:].rearrange("p (h g) j -> p h g j", h=2), O2(97, ob)[:, :, :, 65:97],
                               g8(rd[:]).unsqueeze(3).to_broadcast([128, 2, 4, 32]), ALU.mult),
                     reads=Ores(ob) + [R("rd")], writes=[R("impn")])
                P.op("dve", lambda e: e.tensor_reduce(out=imp[:], in_=impn[:].rearrange("p g j -> p j g"), axis=AX.X, op=ALU.add),
                     reads=[R("impn")], writes=[R("imp")])
                P.op("dve", TT(sc[:], imp[:], selc[:, qb - 8, 0:32], ALU.mult), reads=[R("imp"), R("selc")], writes=[R("sc")])
                P.op("dve", TT(sc[:], sc[:], selc[:, qb - 8, 32:64], ALU.add), reads=[R("sc"), R("selc")], writes=[R("sc")])
                P.op("dve", lambda e: e.max(out=m8[:, 0:8], in_=sc[:]), reads=[R("sc")], writes=[R("m8")])
                P.op("dve", lambda e: e.match_replace(out=sc2[:], in_to_replace=m8[:, 0:8], in_values=sc[:], imm_value=NEG),
                     reads=[R("sc"), R("m8")], writes=[R("sc2")])
                P.op("dve", lambda e: e.max(out=m8[:, 8:16], in_=sc2[:]), reads=[R("sc2")], writes=[R("m8")])
                P.op("dve", TS(sc2[:], sc[:], m8[:, 15:16], None, ALU.is_ge), reads=[R("sc"), R("m8")], writes=[R("sc2")])
                P.op("dve", TS(mb_bf[:], sc2[:], -1.0, 30000.0, ALU.add, ALU.mult), reads=[R("sc2")], writes=[R("mb_bf")])
                def mbt_part():
                    mb = misc_bank()
                    bb_ = bank_bf(mb)
                    P.op("pe", TR(bb_[0:32, 0:128], mb_bf[:, 0:32], ident[:]), reads=[R("mb_bf"), R("ident")], writes=[R("bank", mb)])
                    for rr in range(4):
                        P.op("act", ACP(MBT[0:32, rr * 128:(rr + 1) * 128], bb_[0:32, 0:128]), reads=[R("bank", mb)], writes=[R("MBT")])
                defer(5, mbt_part)

        tmpm = P.sb("tmpm", [128, 8, 64], F32)

        def branch_epilogue(hk, qb, br, is_last, ob):
            gview = gates[:].rearrange("p t (h b) -> p t h b", b=3)
            P.op("dve", lambda e: e.reciprocal(out=g8(rd[:]), in_=O2(65, ob)[:, :, :, 64]), reads=Ores(ob), writes=[R("rd")])
            P.op("dve", TT(fac[:], rd[:], gview[:, qb, 8 * hk:8 * hk + 8, br], ALU.mult), reads=[R("rd"), R("gates")], writes=[R("fac")])
            P.op("dve", TT(tmpm[:].rearrange("p (h g) d -> p h g d", h=2), O2(65, ob)[:, :, :, 0:64],
                           g8(fac[:]).unsqueeze(3).to_broadcast([128, 2, 4, 64]), ALU.mult),
                 reads=Ores(ob) + [R("fac")], writes=[R("tmpm")])
            if is_last:
                P.op("pool", TT(attn_bf1.rearrange("p (g d) -> p g d", g=8), o_acc, tmpm[:], ALU.add),
                     reads=[R("o_acc"), R("tmpm")], writes=[R("attn_bf")])
            else:
                P.op("pool", TT(o_acc, o_acc, tmpm[:], ALU.add), reads=[R("o_acc"), R("tmpm")], writes=[R("o_acc")])

        nl = len(layer_ids)
        qsrc = {0: (I["a_w_qkv"], "qA"), 1: (I["b_w_in"], "qB")}
        for k_, li in enumerate(layer_ids):
            if k_ == 0:
                prefetch_q(*qsrc[li % 2])
                norm_phase(I["attn_norm"][li:li + 1, :])
                P.barrier(keep=wkeep())
            if li % 2 == 0:
                layer_A()
            else:
                layer_B()
            prefetch_ffn(li)
            norm_phase(I["ffn_norm"][li:li + 1, :])
            P.barrier(keep=wkeep())
            if k_ + 1 < nl:
                load_gain(I["attn_norm"][layer_ids[k_ + 1]:layer_ids[k_ + 1] + 1, :])
                ffn(li, fuse_norm="hnT")
                prefetch_q(*qsrc[layer_ids[k_ + 1] % 2])
            elif final:
                load_gain(I["final_norm"][0:1, :])
                ffn(li, fuse_norm="out")
            else:
                ffn(li)
            P.barrier(keep=wkeep())
        if not final:
            for t in range(NT):
                P.dma("sp", DMA(out[t * 128:(t + 1) * 128, :], h[:, t, :]), sem_out[t % 2], reads=[R("h", t)])
        P.barrier()
        stats = P.emit()
        stats["sbuf_left"] = nc.sbuf_bytes_remaining
    return nc, stats


FUSED = True


def _prep_shared(inputs):
    f = lambda a: np.ascontiguousarray(np.asarray(a, dtype=np.float32))
    sh = {
        "rel_table": f(inputs["rel_table"]), "attn_norm": f(inputs["attn_norm"]), "ffn_norm": f(inputs["ffn_norm"]),
        "final_norm": f(inputs["final_norm"]).reshape(1, D),
        "a_w_qkv": f(inputs["a_w_qkv"])[0], "a_b_qkv": f(inputs["a_b_qkv"]).reshape(1, 1280),
        "a_sinks": f(inputs["a_sinks"]).reshape(1, 16), "a_w_o": f(inputs["a_w_o"])[0],
        "a_b_o": f(inputs["a_b_o"]).reshape(1, D), "b_w_in": f(inputs["b_w_in"])[0],
        "b_cmp_pos": f(inputs["b_cmp_pos"])[0], "b_cmp_w1": f(inputs["b_cmp_w1"])[0],
        "b_cmp_w2": f(inputs["b_cmp_w2"])[0], "b_w_o": f(inputs["b_w_o"])[0],
        "ffn_w_gu": f(inputs["ffn_w_gu"]), "ffn_w_down": f(inputs["ffn_w_down"]),
    }
    sh.update(host_consts())
    return sh


def run_prog(layer_ids, final, xs, shared):
    nc, stats = build(layer_ids, final)
    in_maps = []
    for xb in xs:
        m = dict(shared)
        m["x"] = np.ascontiguousarray(xb, dtype=np.float32)
        in_maps.append(m)
    res = run_bass_kernel_spmd(nc, in_maps, core_ids=list(range(len(xs))))
    return [np.asarray(r["out"]) for r in res.results]


def kernel(**inputs):
    x = np.asarray(inputs["x"], dtype=np.float32)
    shared = _prep_shared(inputs)
    xs = [x[b] for b in range(x.shape[0])]
    if FUSED:
        outs = run_prog([0, 1], True, xs, shared)
    else:
        hs = run_prog([0], False, xs, shared)
        outs = run_prog([1], True, hs, shared)
    return np.stack(outs, axis=0).astype(np.float32)
```

```python
import math
from contextlib import ExitStack
import numpy as np
import concourse.bass as bass
import concourse.mybir as mybir
from concourse.bass_utils import run_bass_kernel_spmd

F32 = mybir.dt.float32
BF16 = mybir.dt.bfloat16
AF = mybir.ActivationFunctionType
ALU = mybir.AluOpType
AX = mybir.AxisListType

S = 2048
D = 1024
NT = 16
KC = 8
DFF = 2816
FOFF = 2063
FLEN = 4608
NEG = -1e30


class Res:
    __slots__ = ("name", "w", "r")

    def __init__(self, name=""):
        self.name = name
        self.w = None
        self.r = {}


class DmaSem:
    def __init__(self, handle, name):
        self.handle = handle
        self.name = name
        self.count = 0


class Queue:
    def __init__(self, name, eng, sem):
        self.name = name
        self.eng = eng
        self.sem = sem
        self.ops = []
        self.seen = {}


class Prog:
    def __init__(self, nc, stack):
        self.nc = nc
        self.stack = stack
        self.q = {}
        for name, eng in (("pe", nc.tensor), ("act", nc.scalar), ("dve", nc.vector),
                          ("pool", nc.gpsimd), ("sp", nc.sync)):
            sem = stack.enter_context(nc.semaphore("s_" + name))
            self.q[name] = Queue(name, eng, sem)
        self.n_dsem = 0
        self.pending = []
        self.rcache = {}

    def R(self, *key):
        r = self.rcache.get(key)
        if r is None:
            r = Res(str(key))
            self.rcache[key] = r
        return r

    def dsem(self, name=None):
        self.n_dsem += 1
        h = self.stack.enter_context(self.nc.semaphore("d_%d" % self.n_dsem))
        return DmaSem(h, name or "d%d" % self.n_dsem)

    def sb(self, name, shape, dtype):
        return self.stack.enter_context(self.nc.sbuf_tensor(name, list(shape), dtype))

    def ps(self, name, shape, dtype):
        return self.stack.enter_context(self.nc.psum_tensor(name, list(shape), dtype))

    def _need(self, q, toks):
        waits = []
        for (key, idx) in toks:
            if key is q and q.name == "pe":
                continue
            if q.seen.get(key, -1) >= idx:
                continue
            q.seen[key] = idx
            waits.append((key, idx))
            if isinstance(key, Queue):
                key.ops[idx]["mark"] = True
        return waits

    def _deps(self, q, reads, writes):
        toks = []
        for r in reads:
            if r.w is not None:
                toks.append(r.w)
        for w in writes:
            if w.w is not None:
                toks.append(w.w)
            toks.extend(w.r.values())
        return self._need(q, toks)

    def op(self, qname, fn, reads=(), writes=()):
        q = self.q[qname]
        waits = self._deps(q, reads, writes)
        idx = len(q.ops)
        q.ops.append({"fn": fn, "waits": waits, "mark": False, "dsem": None})
        tok = (q, idx)
        for r in reads:
            r.r[q] = tok
        for w in writes:
            w.w = tok
            w.r = {}
        return tok

    def dma(self, qname, fn, sem, reads=(), writes=(), cont=False):
        q = self.q[qname]
        toks = []
        for r in reads:
            if r.w is not None:
                toks.append(r.w)
        for w in writes:
            if w.w is not None:
                toks.append(w.w)
            toks.extend(w.r.values())
        toks = [t for t in toks if t[0] is not sem]
        if not cont and sem.count > 0:
            toks.append((sem, sem.count))
        waits = self._need(q, toks)
        q.ops.append({"fn": fn, "waits": waits, "mark": False, "dsem": sem})
        sem.count += 16
        tok = (sem, sem.count)
        for r in reads:
            r.r[sem] = tok
        for w in writes:
            w.w = tok
            w.r = {}
        self.pending.append(tok)
        return tok

    def wait_tok(self, qname, toks):
        q = self.q[qname]
        waits = self._need(q, toks)
        if waits:
            q.ops.append({"fn": None, "waits": waits, "mark": False, "dsem": None})

    def barrier(self, keep=()):
        keep_toks = set()
        for r in keep:
            if r.w is not None:
                keep_toks.add(r.w)
        last = []
        for q in self.q.values():
            for i in range(len(q.ops) - 1, -1, -1):
                if q.ops[i]["fn"] is not None and q.ops[i]["dsem"] is None:
                    last.append((q, i))
                    break
        kept_sem = {}
        for (key, val) in keep_toks:
            if isinstance(key, DmaSem):
                kept_sem[key] = max(kept_sem.get(key, 0), val)
        dl = {}
        still = []
        for (sem, val) in self.pending:
            if sem in kept_sem and val <= kept_sem[sem]:
                still.append((sem, val))
                continue
            dl[sem] = max(dl.get(sem, 0), val)
        dtoks = list(dl.items())
        self.pending = still
        for qn in self.q:
            self.wait_tok(qn, last + dtoks)
        kept = {k: v for k, v in self.rcache.items() if v in keep}
        self.rcache = kept

    def emit(self):
        for q in self.q.values():
            c = 0
            for o in q.ops:
                if o["mark"]:
                    c += 1
                o["cnt"] = c
        stats = {}
        with self.nc.Block() as block:
            def run(q):
                def body(eng):
                    nw = 0
                    for o in q.ops:
                        for (key, idx) in o["waits"]:
                            if isinstance(key, Queue):
                                eng.wait_ge(key.sem, key.ops[idx]["cnt"])
                            else:
                                eng.wait_ge(key.handle, idx)
                            nw += 1
                        if o["fn"] is None:
                            continue
                        ins = o["fn"](eng)
                        if o["dsem"] is not None:
                            ins.then_inc(o["dsem"].handle, 16)
                        elif o["mark"]:
                            ins.then_inc(q.sem, 1)
                    stats[q.name] = (len(q.ops), nw)
                return body
            block.tensor(run(self.q["pe"]))
            block.scalar(run(self.q["act"]))
            block.vector(run(self.q["dve"]))
            block.gpsimd(run(self.q["pool"]))
            block.sync(run(self.q["sp"]))
        return stats


def MM(out, lhsT, rhs, start, stop):
    return lambda e: e.matmul(out, lhsT=lhsT, rhs=rhs, start=start, stop=stop, skip_group_check=True)


def TR(out, in_, ident):
    return lambda e: e.transpose(out=out, in_=in_, identity=ident)


def ACTF(out, in_, func, **kw):
    return lambda e: e.activation(out=out, in_=in_, func=func, **kw)


def TT(out, in0, in1, op):
    return lambda e: e.tensor_tensor(out=out, in0=in0, in1=in1, op=op)


def TS(out, in0, s1, s2, op0, op1=None):
    if op1 is None:
        return lambda e: e.tensor_scalar(out=out, in0=in0, scalar1=s1, scalar2=None, op0=op0)
    return lambda e: e.tensor_scalar(out=out, in0=in0, scalar1=s1, scalar2=s2, op0=op0, op1=op1)


def STT(out, in0, scalar, in1, op0, op1):
    return lambda e: e.scalar_tensor_tensor(out=out, in0=in0, scalar=scalar, in1=in1, op0=op0, op1=op1)


def CP(out, in_):
    return lambda e: e.tensor_copy(out=out, in_=in_)


def ACP(out, in_):
    return lambda e: e.copy(out=out, in_=in_)


def DMA(out, in_, slow=False):
    if slow:
        return lambda e: e.dma_start(out=out, in_=in_, allow_slow_non_contiguous=True)
    return lambda e: e.dma_start(out=out, in_=in_)


def MSET(ap, v):
    return lambda e: e.memset(ap, v)


def _t5_bucket_np(dist):
    n = np.maximum(dist, 0)
    nf = np.maximum(n, 1).astype(np.float32)
    v = (np.log(nf / np.float32(16)) / np.float32(math.log(1024 / 16)) * np.float32(16)).astype(np.float32)
    log_b = 16 + v.astype(np.int32)
    return np.where(n < 16, n, np.minimum(log_b, 31))


def host_consts():
    c = {}
    c["c_ident"] = np.eye(128, dtype=np.float32)
    c["c_J"] = np.ascontiguousarray(np.eye(128, dtype=np.float32)[::-1])
    oh = np.zeros((33, FLEN), np.float32)
    idx = np.arange(FLEN)
    d = idx - FOFF
    valid = (d >= 0) & (d < S)
    b = _t5_bucket_np(np.where(valid, d, 0))
    oh[b[valid], idx[valid]] = 1.0
    oh[32, idx[~valid]] = 1.0
    c["c_onehot"] = oh
    E = np.zeros((32, S), np.float32)
    E[np.arange(S) // 64, np.arange(S)] = 1.0
    c["c_E"] = E
    p = np.arange(128)[:, None]
    j = np.arange(128)[None, :]
    c["c_tri"] = np.where(j >= p, np.float32(NEG), np.float32(0)).astype(np.float32)
    selc = np.zeros((128, 8, 64), np.float32)
    for qi in range(8):
        qb = 8 + qi
        t = qb * 128 + np.arange(128)
        cur = t // 64
        for jb in range(32):
            f0 = (jb == 0)
            f1 = (cur - jb == 1)
            f2 = (cur - jb == 0)
            forced = f0 | f1 | f2
            keep = (~forced) & (jb <= cur)
            fn = np.where(f2, 3e6, np.where(f1, 2e6, np.where(f0, 1e6, np.where(jb > cur, -1.0, 0.0))))
            selc[:, qi, jb] = keep.astype(np.float32)
            selc[:, qi, 32 + jb] = fn
    c["c_selc"] = selc
    ov = np.zeros((128, 32), np.float32)
    for pp in range(128):
        cc = 127 - pp
        if cc > 126:
            continue
        for jb in range(32):
            if 4 * jb - 1 <= cc <= 4 * jb + 3:
                ov[pp, jb] = 1.0
    c["c_ovrev"] = ov
    return c


INPUT_SHAPES = {
    "x": [S, D], "rel_table": [32, 16], "attn_norm": [2, D], "ffn_norm": [2, D], "final_norm": [1, D],
    "a_w_qkv": [D, 1280], "a_b_qkv": [1, 1280], "a_sinks": [1, 16], "a_w_o": [D, D], "a_b_o": [1, D],
    "b_w_in": [D, 1840], "b_cmp_pos": [2, 32, 64], "b_cmp_w1": [2, 2048, 256], "b_cmp_w2": [2, 256, 64],
    "b_w_o": [D, D], "ffn_w_gu": [2, D, 2 * DFF], "ffn_w_down": [2, DFF, D],
    "c_ident": [128, 128], "c_J": [128, 128], "c_onehot": [33, FLEN], "c_E": [32, S], "c_tri": [128, 128],
    "c_selc": [128, 8, 64], "c_ovrev": [128, 32],
}


def build(layer_ids, final):
    nc = bass.Bass("TRN2", target_bir_lowering=False)
    I = {k: nc.dram_tensor(k, v, F32, kind="ExternalInput").ap() for k, v in INPUT_SHAPES.items()}
    out = nc.dram_tensor("out", [S, D], F32, kind="ExternalOutput").ap()
    Fd = nc.dram_tensor("Fd", [16, FLEN], F32, kind="Internal").ap()
    ksave = nc.dram_tensor("ksave", [2, 64, S], BF16, kind="Internal").ap()

    def dap(ap, off, pat):
        return bass.AP(ap.tensor, off, pat)

    with ExitStack() as st:
        P = Prog(nc, st)
        R = P.R
        h = P.sb("h", [128, NT, D], F32)
        arenaA = P.sb("arenaA", [128, 16384], BF16)
        arenaB = P.sb("arenaB", [128, 43008], BF16)
        wbuf = [P.sb("wbuf%d" % i, [128, KC, 256], BF16) for i in range(2)]
        tabX = P.sb("tabX", [128, 8, 128], F32)
        gbc = P.sb("gbc", [128, D], F32)
        selc = P.sb("selc", [128, 8, 64], F32)
        ident = P.sb("ident", [128, 128], BF16)
        J_bf = P.sb("J_bf", [128, 128], BF16)
        J32 = P.sb("J32", [128, 128], F32)
        tri = P.sb("tri", [128, 128], F32)
        ovrev = P.sb("ovrev", [128, 32], BF16)
        relaug = P.sb("relaug", [33, 16], F32)
        ssq = P.sb("ssq", [128, NT], F32)
        rstd = P.sb("rstd", [128, NT], F32)
        bq = P.sb("bq", [128, 8], F32)
        bk = P.sb("bk", [128, 1], F32)
        bv_bc = P.sb("bv_bc", [128, 128], F32)
        esink = P.sb("esink", [128, 16], F32)
        b31 = P.sb("b31", [128, 16], F32)
        den = P.sb("den", [128, 8], F32)
        rd = P.sb("rd", [128, 8], F32)
        fac = P.sb("fac", [128, 8], F32)
        impn = P.sb("impn", [128, 8, 32], F32)
        imp = P.sb("imp", [128, 32], F32)
        sc = P.sb("sc", [128, 32], F32)
        sc2 = P.sb("sc2", [128, 32], F32)
        m8 = P.sb("m8", [128, 16], F32)
        mb_bf = P.sb("mb_bf", [128, 32], BF16)
        MBT = P.sb("MBT", [128, 512], BF16)
        Vc = P.sb("Vc", [128, 2, 97], BF16)
        kcT2 = P.sb("kcT2", [128, 2, 128], BF16)
        posT = P.sb("posT", [128, 32], F32)
        w2dup = P.sb("w2dup", [128, 2, 128], BF16)
        w2v = P.sb("w2v", [128, 2, 64], BF16)

        bank = [P.ps("bank%d" % i, [128, 512], F32) for i in range(4)]
        Obuf = [P.ps("Obuf%d" % i, [128, 1024], F32) for i in range(2)]
        for i_ in range(2):
            bank += [Obuf[i_][:, 0:512], Obuf[i_][:, 512:1024]]

        def bank_bf(i):
            return bank[i][:].bitcast(BF16)

        misc_rr = [0]

        att_mode = [False]
        srr = [0]

        def next_sbank():
            srr[0] = (srr[0] + 1) % 4
            return srr[0]

        def misc_bank():
            if att_mode[0]:
                return next_sbank()
            misc_rr[0] ^= 1
            return 6 + misc_rr[0]

        obase = [4]

        def new_branch():
            obase[0] = 10 - obase[0]
            return obase[0]

        def carve(arena, boff, nbytes, dtype, pattern=None, **kw):
            v = arena[:, boff // 2:(boff + nbytes) // 2]
            if dtype is F32:
                v = v.bitcast(F32)
            if pattern:
                v = v.rearrange(pattern, **kw)
            return v

        hnT = carve(arenaA, 0, 32768, BF16, "p (k t) -> p k t", k=KC)
        tabs = carve(arenaA, 0, 32768, F32, "p (s g j) -> p s g j", s=8, g=8)
        blk = carve(arenaA, 0, 8192, BF16, "p (c l) -> p c l", c=128)
        xs_g = carve(arenaA, 8192, 1024, F32)
        t1_g = carve(arenaA, 9216, 1024, F32)
        gl_bf = carve(arenaA, 10240, 512, BF16)
        GT = carve(arenaA, 10752, 512, BF16)

        qTp = carve(arenaB, 0, 32768, BF16, "p (g t) -> p g t", g=8)
        kT1 = carve(arenaB, 32768, 4096, BF16)
        kT2 = carve(arenaB, 36864, 4096, BF16)
        V1 = carve(arenaB, 40960, 4160, BF16, "p (t k c) -> p t k c", t=NT, k=2)
        V2 = carve(arenaB, 45120, 4160, BF16, "p (t k c) -> p t k c", t=NT, k=2)
        gates = carve(arenaB, 49280, 3072, F32, "p (t c) -> p t c", t=NT)
        w_o = carve(arenaB, 52352, 16384, BF16, "p (k n) -> p k n", k=KC)
        w1sb = carve(arenaB, 52352, 16384, BF16, "p (l m) -> p l m", l=32)
        WK = 68736
        E_bf = carve(arenaB, WK + 4096, 4096, BF16)
        PT = [carve(arenaB, WK + 8192 + 1024 * i, 1024, BF16) for i in range(4)]
        o_acc = carve(arenaB, WK + 12288, 2048, F32, "p (g d) -> p g d", g=8)
        attn_bf = carve(arenaB, WK + 12288, 2048, BF16)
        attn_bf1 = carve(arenaB, WK + 14336, 1024, BF16)
        attnT = carve(arenaB, WK + 14336, 2048, BF16)
        attnT1 = carve(arenaB, WK + 15360, 1024, BF16)
        Htmp = carve(arenaB, WK, 4096, F32, "p (g j) -> p g j", g=8)
        cmpT = carve(arenaB, WK, 16512, F32, "p (k t) -> p k t", k=2)
        hn_bf = [carve(arenaB, 2048 * i, 2048, BF16) for i in range(2)]
        junk = carve(arenaB, 4096, 2048, BF16)
        ostage = [carve(arenaB, 8192 + 4096 * i, 4096, F32) for i in range(2)]
        hn_bf2 = [carve(arenaB, 71680 + 2048 * i, 2048, BF16) for i in range(2)]
        junk2 = carve(arenaB, 75776, 2048, BF16)
        ostage2 = [carve(arenaB, 77824 + 4096 * i, 4096, F32) for i in range(2)]
        oh_sb = carve(arenaB, 16384, 18432, F32)
        Fsb = carve(arenaB, 34816, 18432, F32)
        actT = carve(arenaB, 0, 45056, BF16, "p (f t) -> p f t", f=11)
        wd = carve(arenaB, 45056, 22528, BF16, "p (f n) -> p f n", f=11)
        sg = [carve(arenaB, 67584 + 2048 * i, 2048, F32) for i in range(2)]

        sem_w = [P.dsem("w0"), P.dsem("w1")]
        sem_misc = {"sp": [P.dsem("ms%d" % i) for i in range(6)], "pool": [P.dsem("mp%d" % i) for i in range(4)]}
        mrr = {"sp": 0, "pool": 0}

        def msem(qn):
            mrr[qn] = (mrr[qn] + 1) % len(sem_misc[qn])
            return sem_misc[qn][mrr[qn]]

        wrr = [0]

        def next_wbuf():
            i = wrr[0]
            wrr[0] ^= 1
            return i

        sem_x = [P.dsem("x%d" % i) for i in range(4)]
        xv = I["x"].rearrange("(t p) d -> p t d", p=128)
        for gi in range(4):
            for t in range(4 * gi, 4 * gi + 4):
                P.dma("sp", DMA(h[:, t, :], xv[:, t, :]), sem_x[gi], writes=[R("h", t)], cont=(t % 4 != 0))
            for t in range(4 * gi, 4 * gi + 4):
                R("h", t).w = (sem_x[gi], sem_x[gi].count)
        P.dma("pool", DMA(ident[:], I["c_ident"]), msem("pool"), writes=[R("ident")])
        P.dma("pool", DMA(J_bf[:], I["c_J"]), msem("pool"), writes=[R("J_bf")])
        P.dma("sp", DMA(J32[:], I["c_J"]), msem("sp"), writes=[R("J32")])
        P.dma("sp", DMA(tri[:], I["c_tri"]), msem("sp"), writes=[R("tri")])
        P.dma("sp", DMA(selc[:], I["c_selc"]), msem("sp"), writes=[R("selc")])
        P.dma("pool", DMA(ovrev[:], I["c_ovrev"]), msem("pool"), writes=[R("ovrev")])
        P.op("pool", MSET(relaug[32:33, :], NEG), writes=[R("relaug")])
        P.dma("sp", DMA(relaug[0:32, :], I["rel_table"]), msem("sp"), writes=[R("relaug")])
        P.dma("sp", DMA(oh_sb[0:33, :], I["c_onehot"]), msem("sp"), writes=[R("oh")])
        P.dma("sp", DMA(b31[:], dap(I["rel_table"], 31 * 16, [[0, 128], [1, 16]])), msem("sp"), writes=[R("b31")])
        for c in range(FLEN // 512):
            bi = c % 2
            P.op("pe", MM(bank[bi][0:16, :], relaug[0:33, 0:16], oh_sb[0:33, c * 512:(c + 1) * 512], True, True),
                 reads=[R("relaug"), R("oh")], writes=[R("bank", bi)])
            P.op("act", ACP(Fsb[0:16, c * 512:(c + 1) * 512], bank[bi][0:16, :]), reads=[R("bank", bi)], writes=[R("Fsb")])
        sem_F = P.dsem("F")
        P.dma("sp", DMA(Fd, Fsb[0:16, :]), sem_F, reads=[R("Fsb")], writes=[R("Fd")])

        def load_gain(gain_ap_row):
            P.dma("sp", DMA(gbc[:], dap(gain_ap_row, gain_ap_row.offset, [[0, 128], [1, D]])), msem("sp"), writes=[R("gbc")])

        def norm_tiles(tiles, to_out, tmp):
            junk_, hn_, os_ = tmp
            t0_, t1_ = tiles[0], tiles[-1] + 1
            for i in tiles:
                P.op("act", ACTF(junk_, h[:, i, :], AF.Square, accum_out=ssq[:, i:i + 1]),
                     reads=[R("h", i)], writes=[R("ssq", i), R("junk")])
            sr = [R("ssq", i) for i in tiles]
            P.op("dve", TS(rstd[:, t0_:t1_], ssq[:, t0_:t1_], 1.0 / D, 1e-6, ALU.mult, ALU.add), reads=sr, writes=[R("rstd", t0_)])
            P.op("act", ACTF(rstd[:, t0_:t1_], rstd[:, t0_:t1_], AF.Sqrt), reads=[R("rstd", t0_)], writes=[R("rstd", t0_)])
            P.op("dve", lambda e: e.reciprocal(out=rstd[:, t0_:t1_], in_=rstd[:, t0_:t1_]), reads=[R("rstd", t0_)], writes=[R("rstd", t0_)])
            for i in tiles:
                sl = i % 2
                if to_out:
                    P.op("dve", STT(os_[sl], h[:, i, :], rstd[:, i:i + 1], gbc[:], ALU.mult, ALU.mult),
                         reads=[R("h", i), R("rstd", t0_), R("gbc")], writes=[R("ostage", sl)])
                    P.dma("sp", DMA(out[i * 128:(i + 1) * 128, :], os_[sl]), sem_out[sl], reads=[R("ostage", sl)])
                    continue
                P.op("dve", STT(hn_[sl], h[:, i, :], rstd[:, i:i + 1], gbc[:], ALU.mult, ALU.mult),
                     reads=[R("h", i), R("rstd", t0_), R("gbc")], writes=[R("hn_bf", sl)])
                mb = misc_bank()
                bb = bank_bf(mb)
                for kc in range(KC):
                    P.op("pe", TR(bb[:, kc * 128:(kc + 1) * 128], hn_[sl][:, kc * 128:(kc + 1) * 128], ident[:]),
                         reads=[R("hn_bf", sl), R("ident")], writes=[R("bank", mb)])
                P.op("act", ACP(hnT[:, :, i * 128:(i + 1) * 128], bb.rearrange("p (k t) -> p k t", k=KC)),
                     reads=[R("bank", mb)], writes=[R("hnT", i)])

        def norm_phase(gain_ap_row, to_out=False):
            load_gain(gain_ap_row)
            norm_tiles(list(range(NT)), to_out, (junk, hn_bf, ostage))

        def load_w256(src2d, col_specs):
            wi = next_wbuf()
            srcv = src2d.rearrange("(k p) n -> p k n", p=128)
            for ci, (d0, s0, n) in enumerate(col_specs):
                P.dma("pool", DMA(wbuf[wi][:, :, d0:d0 + n], srcv[:, :, s0:s0 + n]), sem_w[wi], writes=[R("wbuf", wi)], cont=(ci > 0))
            return wi

        pre_w = {}

        def load_qpair_chunk(src2d, c, key=None):
            if key is not None and key in pre_w:
                return pre_w.pop(key)
            wi = next_wbuf()
            srcv = src2d.rearrange("(k p) n -> p k n", p=128)
            dst5 = wbuf[wi][:].rearrange("p k (pl two d) -> p k pl two d", pl=2, two=2)
            for two in range(2):
                s = srcv[:, :, two * 512 + 128 * c: two * 512 + 128 * c + 128].rearrange("p k (pl d) -> p k pl d", pl=2)
                for kc in range(KC):
                    P.dma("pool", DMA(dst5[:, kc, :, two, :], s[:, kc, :, :]), sem_w[wi], writes=[R("wbuf", wi)], cont=not (two == 0 and kc == 0))
            return wi

        def hn_reads(tc):
            return [R("hnT", t) for t in range(4 * tc, 4 * tc + 4)]

        def proj_feat(wi, c0, evac, tag):
            for tc in range(4):
                mb = misc_bank()
                for kc in range(KC):
                    P.op("pe", MM(bank[mb][:], wbuf[wi][:, kc, c0:c0 + 128], hnT[:, kc, tc * 512:(tc + 1) * 512],
                                  kc == 0, kc == KC - 1),
                         reads=[R("wbuf", wi)] + hn_reads(tc), writes=[R("bank", mb)])
                evac(tc, mb)

        def proj_tok(wi, c0, n, evac):
            for t in range(NT):
                mb = misc_bank()
                for kc in range(KC):
                    P.op("pe", MM(bank[mb][:, 0:n], hnT[:, kc, t * 128:(t + 1) * 128], wbuf[wi][:, kc, c0:c0 + n],
                                  kc == 0, kc == KC - 1),
                         reads=[R("wbuf", wi), R("hnT", t)], writes=[R("bank", mb)])
                evac(t, mb)

        def prefetch_q(src2d, name):
            for c in range(2):
                pre_w[(name, c)] = load_qpair_chunk(src2d, c)

        def q_proj(src2d, has_bias, name):
            for c in range(4):
                wi = load_qpair_chunk(src2d, c, key=(name, c))
                for pl in range(2):
                    g = 2 * c + pl

                    def ev(tc, mb, g=g):
                        if has_bias:
                            P.op("dve", TS(qTp[:, g, tc * 512:(tc + 1) * 512], bank[mb][:], bq[:, g:g + 1], 0.125, ALU.add, ALU.mult),
                                 reads=[R("bank", mb), R("bq")], writes=[R("qTp", g, tc)])
                        else:
                            P.op("dve", TS(qTp[:, g, tc * 512:(tc + 1) * 512], bank[mb][:], 0.125, None, ALU.mult),
                                 reads=[R("bank", mb)], writes=[R("qTp", g, tc)])
                    proj_feat(wi, pl * 128, ev, "q")

        pipe = []

        def unit(hk, qb, kT, kcol0, tab4, const_bias, Vrhs, ncols, first, last, emask_kb=None, tabres=None,
                 kres=None, vres=None, pre=None, post=None, ob=4):
            pipe.append(dict(hk=hk, qb=qb, kT=kT, kcol0=kcol0, tab4=tab4, const_bias=const_bias, Vrhs=Vrhs, ncols=ncols,
                             first=first, last=last, emask_kb=emask_kb, tabres=tabres, kres=kres, vres=vres, pre=pre, post=post, ob=ob))

        SKEW = 3

        def emit_S(u, half, sl):
            hk, qb = u["hk"], u["qb"]
            if half == 0 and u["pre"] is not None:
                u["pre"]()
            qrd = [R("qTp", g, qb // 4) for g in range(4 * half, 4 * half + 4)]
            bi = next_sbank()
            rhs = qTp[:, 4 * half:4 * half + 4, qb * 128:(qb + 1) * 128]
            P.op("pe", MM(bank[bi][:], u["kT"][:, u["kcol0"]:u["kcol0"] + 128], rhs, True, u["emask_kb"] is None),
                 reads=[u["kres"] or R("kT")] + qrd, writes=[R("bank", bi)])
            if u["emask_kb"] is not None:
                ek = u["emask_kb"]
                P.op("pe", MM(bank[bi][:], E_bf[:, ek * 128:(ek + 1) * 128], MBT[:], False, True),
                     reads=[R("E"), R("MBT")], writes=[R("bank", bi)])
            bv3 = bank[bi][:].rearrange("p (g j) -> p g j", g=4)
            if u["const_bias"]:
                in1 = b31[:, 8 * hk + 4 * half: 8 * hk + 4 * half + 4].unsqueeze(2).to_broadcast([128, 4, 128])
                P.op("dve", TT(bv3, bv3, in1, ALU.add), reads=[R("bank", bi), R("b31")], writes=[R("bank", bi)])
            else:
                P.op("dve", TT(bv3, bv3, u["tab4"][:, 4 * half:4 * half + 4, :], ALU.add),
                     reads=[R("bank", bi), u["tabres"] or R("tab")], writes=[R("bank", bi)])
            P.op("act", ACTF(PT[sl], bank[bi][:], AF.Exp), reads=[R("bank", bi)], writes=[R("PT", sl)])

        def emit_PV(u, half, sl):
            ncols = u["ncols"]
            ob = u["ob"] + half
            for g4 in range(4):
                oap = bank[ob][:, g4 * ncols:(g4 + 1) * ncols]
                P.op("pe", MM(oap, PT[sl][:, g4 * 128:(g4 + 1) * 128], u["Vrhs"], u["first"] and (g4 == 0), u["last"]),
                     reads=[R("PT", sl), u["vres"] or R("V")], writes=[R("bank", ob)])
            if half == 1 and u["post"] is not None:
                u["post"]()

        deferred = []
        cur_i = [0]

        def defer(k, fn):
            deferred.append((cur_i[0] + k, fn))

        def run_deferred(upto):
            keep_ = []
            for (due, fn) in list(deferred):
                if due <= upto:
                    deferred.remove((due, fn))
                    fn()
            return

        def flush_pipe():
            att_mode[0] = True
            hu = [(u, half) for u in pipe for half in range(2)]
            n = len(hu)
            for i in range(n + SKEW):
                cur_i[0] = i
                run_deferred(i)
                if i < n:
                    emit_S(hu[i][0], hu[i][1], i % 4)
                if i >= SKEW:
                    j = i - SKEW
                    emit_PV(hu[j][0], hu[j][1], j % 4)
            while deferred:
                cur_i[0] += 1
                run_deferred(cur_i[0])
            del pipe[:]
            att_mode[0] = False

        def O2(ncols, ob):
            return Obuf[(ob - 4) // 2][:].rearrange("p (h c) -> p h c", h=2)[:, :, 0:4 * ncols].rearrange("p h (g c) -> p h g c", g=4)

        def Ores(ob):
            return [R("bank", ob), R("bank", ob + 1)]

        def g8(ap):
            return ap.rearrange("p (h g) -> p h g", h=2)

        def Oview(b, ncols, ob):
            return bank[ob + b][:, 0:4 * ncols].rearrange("p (g c) -> p g c", g=4)

        hrr = [0]

        def build_table(dst, hk, off, add_tri, hbufs, tres):
            base = (8 * hk) * FLEN + FOFF + off * 128 - 127
            hrr[0] = (hrr[0] + 1) % len(hbufs)
            Htmp, hres = hbufs[hrr[0]]
            P.dma("sp", DMA(Htmp, dap(Fd, base, [[1, 128], [FLEN, 8], [1, 128]])), msem("sp"),
                  reads=[R("Fd")], writes=[hres])
            for half in range(2):
                mb = misc_bank()
                P.op("pe", MM(bank[mb][:], J32[:], Htmp[:, 4 * half:4 * half + 4, :].rearrange("p g j -> p (g j)"), True, True),
                     reads=[R("J32"), hres], writes=[R("bank", mb)])
                dv = dst[:, 4 * half:4 * half + 4, :]
                if add_tri:
                    P.op("dve", TT(dv, bank[mb][:].rearrange("p (g j) -> p g j", g=4),
                                   tri[:].unsqueeze(1).to_broadcast([128, 4, 128]), ALU.add),
                         reads=[R("bank", mb), R("tri")], writes=[tres])
                else:
                    P.op("act", ACP(dv, bank[mb][:].rearrange("p (g j) -> p g j", g=4)),
                         reads=[R("bank", mb)], writes=[tres])

        def outproj(qb, aT, nk, kc0, wtile):
            for nh in range(2):
                mb = misc_bank()
                for kc in range(nk):
                    P.op("pe", MM(bank[mb][:], aT[:, kc * 128:(kc + 1) * 128], wtile[:, kc0 + kc, nh * 512:(nh + 1) * 512],
                                  kc == 0, kc == nk - 1),
                         reads=[R("attnT"), R("w_o")], writes=[R("bank", mb)])
                P.op("dve", TT(h[:, qb, nh * 512:(nh + 1) * 512], bank[mb][:], h[:, qb, nh * 512:(nh + 1) * 512], ALU.add),
                     reads=[R("bank", mb), R("h", qb)], writes=[R("h", qb)])

        def load_wo(src2d):
            srcv = src2d.rearrange("(k p) n -> p k n", p=128)
            for kc in range(KC):
                P.dma("pool", DMA(w_o[:, kc, :], srcv[:, kc, :]), sem_wo, writes=[R("w_o")], cont=(kc > 0))

        sem_wo = P.dsem("wo")
        sem_wd = P.dsem("wd")
        sem_out = [P.dsem("o0"), P.dsem("o1")]

        def prefetch_ffn(layer):
            wgu = I["ffn_w_gu"][layer]
            wdv = I["ffn_w_down"][layer].rearrange("(f p) n -> p f n", p=128)
            for fc in range(2):
                pre_w[("gu", layer, fc)] = load_w256(wgu, [(0, 128 * fc, 128), (128, DFF + 128 * fc, 128)])
            for fl in range(11):
                P.dma("pool", DMA(wd[:, fl, :], wdv[:, fl, :]), sem_wd, writes=[R("wd")], cont=(fl > 0))
            pre_w[("wd", layer)] = True

        def wkeep():
            return [R("wbuf", 0), R("wbuf", 1), R("wd")]

        def ffn(layer, fuse_norm=None):
            wgu = I["ffn_w_gu"][layer]
            wdn = I["ffn_w_down"][layer]
            wdv = wdn.rearrange("(f p) n -> p f n", p=128)
            sgr = [0]
            for half in range(2):
                if not (half == 0 and ("wd", layer) in pre_w):
                    for fl in range(11):
                        P.dma("pool", DMA(wd[:, fl, :], wdv[:, 11 * half + fl, :]), sem_wd, writes=[R("wd")], cont=(fl > 0))
                else:
                    pre_w.pop(("wd", layer))
                for fl in range(11):
                    fc = 11 * half + fl
                    if ("gu", layer, fc) in pre_w:
                        wi = pre_w.pop(("gu", layer, fc))
                    else:
                        wi = load_w256(wgu, [(0, 128 * fc, 128), (128, DFF + 128 * fc, 128)])
                    for tc in range(4):
                        bg = 2 * (tc % 2)
                        bu = bg + 1
                        for kc in range(KC):
                            P.op("pe", MM(bank[bg][:], wbuf[wi][:, kc, 0:128], hnT[:, kc, tc * 512:(tc + 1) * 512], kc == 0, kc == KC - 1),
                                 reads=[R("wbuf", wi)] + hn_reads(tc), writes=[R("bank", bg)])
                        for kc in range(KC):
                            P.op("pe", MM(bank[bu][:], wbuf[wi][:, kc, 128:256], hnT[:, kc, tc * 512:(tc + 1) * 512], kc == 0, kc == KC - 1),
                                 reads=[R("wbuf", wi)] + hn_reads(tc), writes=[R("bank", bu)])
                        si = sgr[0]
                        sgr[0] ^= 1
                        P.op("act", ACTF(sg[si], bank[bg][:], AF.Silu), reads=[R("bank", bg)], writes=[R("sg", si)])
                        P.op("dve", TT(actT[:, fl, tc * 512:(tc + 1) * 512], sg[si], bank[bu][:], ALU.mult),
                             reads=[R("sg", si), R("bank", bu)], writes=[R("actT", tc)])
                if half == 0:
                    for fc in (11, 12):
                        pre_w[("gu", layer, fc)] = load_w256(wgu, [(0, 128 * fc, 128), (128, DFF + 128 * fc, 128)])
                for t in range(NT):
                    for nh in range(2):
                        mb = 4 + (2 * t + nh) % 4
                        for fl in range(11):
                            P.op("pe", MM(bank[mb][:], actT[:, fl, t * 128:(t + 1) * 128], wd[:, fl, nh * 512:(nh + 1) * 512], fl == 0, fl == 10),
                                 reads=[R("actT", t // 4), R("wd")], writes=[R("bank", mb)])
                        P.op("dve", TT(h[:, t, nh * 512:(nh + 1) * 512], bank[mb][:], h[:, t, nh * 512:(nh + 1) * 512], ALU.add),
                             reads=[R("bank", mb), R("h", t)], writes=[R("h", t)])
                    if half == 1 and fuse_norm is not None and t % 4 == 3:
                        norm_tiles(list(range(t - 3, t + 1)), fuse_norm == "out", (junk2, hn_bf2, ostage2))

        def layer_A():
            W = I["a_w_qkv"]
            b = I["a_b_qkv"]
            for two in range(2):
                P.dma("sp", DMA(bq[64 * two:64 * two + 64, :], dap(b, 512 * two, [[1, 64], [64, 8]]), slow=True), msem("sp"), writes=[R("bq")])
            P.dma("sp", DMA(bk[:], dap(b, 1024, [[1, 128], [1, 1]]), slow=True), msem("sp"), writes=[R("bk")])
            P.dma("sp", DMA(bv_bc[:], dap(b, 1152, [[0, 128], [1, 128]])), msem("sp"), writes=[R("bv")])
            q_proj(W, True, "qA")
            wi = load_w256(W, [(0, 1024, 256)])

            P.op("pool", MSET(kT1[64:128, :], 0.0), writes=[R("kTz", 0)])
            P.op("pool", MSET(kT2[0:64, :], 0.0), writes=[R("kTz", 1)])

            def ev_k(tc, mb):
                P.op("dve", TS(kT1[0:64, tc * 512:(tc + 1) * 512], bank[mb][0:64, :], bk[0:64, 0:1], None, ALU.add),
                     reads=[R("bank", mb), R("bk")], writes=[R("kT")])
                P.op("dve", TS(kT2[64:128, tc * 512:(tc + 1) * 512], bank[mb][64:128, :], bk[64:128, 0:1], None, ALU.add),
                     reads=[R("bank", mb), R("bk")], writes=[R("kT")])
            proj_feat(wi, 0, ev_k, "k")

            def ev_v(t, mb):
                P.op("dve", TT(V1[:, t, :, 0:64], bank[mb][:, 0:128].rearrange("p (k d) -> p k d", k=2),
                               bv_bc[:].rearrange("p (k d) -> p k d", k=2), ALU.add),
                     reads=[R("bank", mb), R("bv")], writes=[R("V")])
            proj_tok(wi, 128, 128, ev_v)
            P.op("pool", MSET(V1[:, :, :, 64], 1.0), writes=[R("V")])
            P.barrier()
            P.dma("sp", DMA(gbc[:], dap(I["a_b_o"], 0, [[0, 128], [1, D]])), msem("sp"), writes=[R("gbc")])
            P.dma("sp", DMA(esink[:], dap(I["a_sinks"], 0, [[0, 128], [1, 16]])), msem("sp"), writes=[R("esink")])
            P.op("act", ACTF(esink[:], esink[:], AF.Exp), reads=[R("esink")], writes=[R("esink")])
            for hk in range(2):
                build_table(tabs[:, hk], hk, 0, False, ((Htmp, R("Htmp")), (tabX[:], R("tabX"))), R("tab", hk))
                build_table(tabs[:, 2 + hk], hk, 1, True, ((Htmp, R("Htmp")), (tabX[:], R("tabX"))), R("tab", 2 + hk))
            load_wo(I["a_w_o"])
            def epiA(hk, qb, ob):
                P.op("dve", TT(g8(den[:]), O2(65, ob)[:, :, :, 64], g8(esink[:, 8 * hk:8 * hk + 8]), ALU.add),
                     reads=Ores(ob) + [R("esink")], writes=[R("den")])
                P.op("dve", lambda e: e.reciprocal(out=rd[:], in_=den[:]), reads=[R("den")], writes=[R("rd")])
                c0 = 8 * hk * 64
                P.op("dve", TT(attn_bf[:, c0:c0 + 512].rearrange("p (h g d) -> p h g d", h=2, g=4), O2(65, ob)[:, :, :, 0:64],
                               g8(rd[:]).unsqueeze(3).to_broadcast([128, 2, 4, 64]), ALU.mult),
                     reads=Ores(ob) + [R("rd")], writes=[R("attn_bf")])
                if hk == 1:
                    P.op("pool", TT(h[:, qb, :], h[:, qb, :], gbc[:], ALU.add), reads=[R("h", qb), R("gbc")], writes=[R("h", qb)])

                    def opA(qb=qb):
                        mb = misc_bank()
                        bb_ = bank_bf(mb)
                        for kc in range(KC):
                            P.op("pe", TR(bb_[:, kc * 128:(kc + 1) * 128], attn_bf[:, kc * 128:(kc + 1) * 128], ident[:]),
                                 reads=[R("attn_bf"), R("ident")], writes=[R("bank", mb)])
                        P.op("act", ACP(attnT, bb_), reads=[R("bank", mb)], writes=[R("attnT")])
                        outproj(qb, attnT, KC, 0, w_o)
                    defer(3, opA)

            for qb in range(NT):
                for hk in range(2):
                    kbs = [kb for kb in (qb - 1, qb) if kb >= 0]
                    for i, kb in enumerate(kbs):
                        tab = tabs[:, hk] if kb == qb else tabs[:, 2 + hk]
                        tres_ = R("tab", hk) if kb == qb else R("tab", 2 + hk)
                        lastu = (i == len(kbs) - 1)
                        if i == 0:
                            ob_ = new_branch()
                        unit(hk, qb, kT1 if hk == 0 else kT2, kb * 128, tab, False, V1[:, kb, hk, :], 65, i == 0, lastu, tabres=tres_,
                             post=(lambda hk=hk, qb=qb, ob_=ob_: epiA(hk, qb, ob_)) if lastu else None, ob=ob_)
            flush_pipe()
            P.barrier()

        def layer_B():
            W = I["b_w_in"]
            for which in range(2):
                w1v = I["b_cmp_w1"][which].rearrange("(l d) m -> d l m", d=64)
                for lq in range(4):
                    P.dma("pool", DMA(w1sb[64 * which:64 * which + 64, 8 * lq:8 * lq + 8, :], w1v[:, 8 * lq:8 * lq + 8, :]), sem_wo,
                          writes=[R("w1sb")], cont=not (which == 0 and lq == 0))
                P.dma("sp", DMA(posT[64 * which:64 * which + 64, :], I["b_cmp_pos"][which].rearrange("l d -> d l"), slow=True),
                      msem("sp"), writes=[R("posT")])
            w2k = I["b_cmp_w2"][0].rearrange("(c p) d -> p c d", p=128)
            for two in range(2):
                P.dma("pool", DMA(w2dup[:, :, 64 * two:64 * two + 64], w2k), msem("pool"), writes=[R("w2k", two)])
            P.dma("pool", DMA(w2v[:], I["b_cmp_w2"][1].rearrange("(c p) d -> p c d", p=128)), msem("pool"), writes=[R("w2v")])
            q_proj(W, False, "qB")
            P.op("pool", MSET(cmpT[:, :, 2048:2064], 0.0), writes=[R("cmpT", 0), R("cmpT", 1)])
            wi = load_w256(W, [(0, 1024, 64), (64, 1152, 64), (128, 1088, 64), (192, 1216, 64)])
            for hk_ in range(2):
                def ev_c(tc, mb, hk_=hk_):
                    P.op("act", ACP(cmpT[:, hk_, tc * 512:(tc + 1) * 512], bank[mb][:]), reads=[R("bank", mb)], writes=[R("cmpT", hk_)])
                proj_feat(wi, 128 * hk_, ev_c, "c")
            for (c0, kTd, Vd, nm) in ((1280, kT1, V1, "s"), (1536, kT2, V2, "w")):
                wi = load_w256(W, [(0, c0, 256)])

                def ev_k(tc, mb, kTd=kTd, nm=nm):
                    P.op("act", ACP(kTd[:, tc * 512:(tc + 1) * 512], bank[mb][:]), reads=[R("bank", mb)], writes=[R("kT", nm)])

                def ev_v(t, mb, Vd=Vd, nm=nm):
                    P.op("dve", CP(Vd[:, t, :, 0:64], bank[mb][:, 0:128].rearrange("p (k d) -> p k d", k=2)),
                         reads=[R("bank", mb)], writes=[R("V", nm)])
                proj_feat(wi, 0, ev_k, "k")
                proj_tok(wi, 128, 128, ev_v)
                P.op("pool", MSET(Vd[:, :, :, 64], 1.0), writes=[R("V", nm)])
            wi = load_w256(W, [(0, 1792, 48)])

            def ev_g(t, mb):
                P.op("act", ACTF(gates[:, t, :], bank[mb][:, 0:48], AF.Sigmoid), reads=[R("bank", mb)], writes=[R("gates")])
            proj_tok(wi, 0, 48, ev_g)
            P.barrier(keep=[R("w1sb"), R("posT"), R("w2k", 0), R("w2k", 1), R("w2v")])
            P.op("pool", MSET(kcT2[:], 0.0), writes=[R("kcT2")])
            for hk in range(2):
                for a_ in range(2):
                    P.op("dve", TT(blk[:, :, 16 * a_:16 * a_ + 16], cmpT[:, hk, 16 * a_:16 * a_ + 2048].rearrange("p (c r) -> p c r", r=16),
                                   posT[:, 16 * a_:16 * a_ + 16].unsqueeze(1).to_broadcast([128, 128, 16]), ALU.add),
                         reads=[R("cmpT", hk), R("posT")], writes=[R("blk")])
                gb = [misc_bank(), misc_bank()]
                for l in range(32):
                    for which in range(2):
                        P.op("pe", MM(bank[gb[which]][:, 0:256], blk[64 * which:64 * which + 64, :, l],
                                      w1sb[64 * which:64 * which + 64, l, :], l == 0, l == 31),
                             reads=[R("blk"), R("w1sb")], writes=[R("bank", gb[which])])
                for which in range(2):
                    mb = gb[which]
                    P.op("act", ACP(xs_g, bank[mb][:, 0:256]), reads=[R("bank", mb)], writes=[R("xs")])
                    P.op("dve", TT(t1_g, xs_g, xs_g, ALU.mult), reads=[R("xs")], writes=[R("t1")])
                    P.op("dve", TS(t1_g, t1_g, 0.044715, 1.0, ALU.mult, ALU.add), reads=[R("t1")], writes=[R("t1")])
                    P.op("dve", TT(t1_g, t1_g, xs_g, ALU.mult), reads=[R("t1"), R("xs")], writes=[R("t1")])
                    P.op("act", ACTF(t1_g, t1_g, AF.Sigmoid, scale=1.5957691216057308), reads=[R("t1")], writes=[R("t1")])
                    P.op("dve", TT(gl_bf, xs_g, t1_g, ALU.mult), reads=[R("t1"), R("xs")], writes=[R("gl")])
                    bb_ = bank_bf(mb)
                    for ch in range(2):
                        P.op("pe", TR(bb_[:, ch * 128:(ch + 1) * 128], gl_bf[:, ch * 128:(ch + 1) * 128], J_bf[:]),
                             reads=[R("gl"), R("J_bf")], writes=[R("bank", mb)])
                    P.op("act", ACP(GT, bb_[:, 0:256]), reads=[R("bank", mb)], writes=[R("GT")])
                    if which == 0:
                        for ch in range(2):
                            P.op("pe", MM(bank[mb][:, 0:128], w2dup[:, ch, :], GT[:, ch * 128:(ch + 1) * 128], ch == 0, ch == 1),
                                 reads=[R("GT"), R("w2k", 0), R("w2k", 1)], writes=[R("bank", mb)])
                        P.op("dve", CP(kcT2[64 * hk:64 * hk + 64, hk, :], bank[mb][64 * hk:64 * hk + 64, 0:128]),
                             reads=[R("bank", mb)], writes=[R("kcT2")])
                    else:
                        for ch in range(2):
                            P.op("pe", MM(bank[mb][:, 0:64], GT[:, ch * 128:(ch + 1) * 128], w2v[:, ch, :], ch == 0, ch == 1),
                                 reads=[R("GT"), R("w2v")], writes=[R("bank", mb)])
                        P.op("dve", CP(Vc[:, hk, 0:64], bank[mb][:, 0:64]), reads=[R("bank", mb)], writes=[R("Vc")])
            P.op("pool", MSET(Vc[:, :, 64:65], 1.0), writes=[R("Vc")])
            for hk in range(2):
                P.op("pool", CP(Vc[:, hk, 65:97], ovrev[:]), reads=[R("ovrev")], writes=[R("Vc")])
            P.barrier()
            P.dma("pool", DMA(E_bf[0:32, :], I["c_E"]), msem("pool"), writes=[R("E")])
            P.op("pool", MSET(E_bf[32:64, :], 0.0), writes=[R("E")])
            P.op("pool", MSET(E_bf[64:128, :], 0.0), writes=[R("E")])
            P.op("pool", MSET(MBT[:], 0.0), writes=[R("MBT")])
            gview = gates[:].rearrange("p t (h b) -> p t h b", b=3)
            for hk in range(2):
                if hk == 0:
                    for i_, kTx in enumerate((kT1, kT2)):
                        P.dma("sp", DMA(ksave[i_], kTx[64:128, :]), msem("sp"), reads=[R("kT")], writes=[R("ksave")])
                    for kTx in (kT1, kT2):
                        P.op("pool", MSET(kTx[64:128, :], 0.0), writes=[R("kT")])
                else:
                    for kTx in (kT1, kT2):
                        P.op("pool", MSET(kTx[0:64, :], 0.0), writes=[R("kT")])
                    for i_, kTx in enumerate((kT1, kT2)):
                        P.dma("sp", DMA(kTx[64:128, :], ksave[i_]), msem("sp"), reads=[R("ksave")], writes=[R("kT")])
                tabc = gbc[:].rearrange("p (g j) -> p g j", g=8)
                for qb in range(NT):
                    def pre_c(hk=hk, qb=qb):
                        if qb < 8:
                            hb = ((Htmp, R("Htmp")), (tabX[:], R("tabX"))) if qb < 4 else ((Htmp, R("Htmp")),)
                            build_table(tabs[:, qb], hk, qb, False, hb, R("tab", qb))
                        if qb == 4:
                            P.op("dve", TT(tabX[:], tabs[:, 4], tri[:].unsqueeze(1).to_broadcast([128, 8, 128]), ALU.add),
                                 reads=[R("tab", 4), R("tri")], writes=[R("tabX")])
                        if hk == 0 and qb == 0:
                            load_wo(I["b_w_o"])
                        P.dma("sp", DMA(tabc, dap(Fd, (8 * hk) * FLEN + qb * 128, [[16, 128], [FLEN, 8], [1, 128]])), msem("sp"),
                              reads=[R("Fd")], writes=[R("gbc")])
                    ob_ = new_branch()
                    unit(hk, qb, kcT2[:, hk, :], 0, tabc, False, Vc[:, hk, :], 97, True, True, tabres=R("gbc"),
                         kres=R("kcT2"), vres=R("Vc"), pre=pre_c, post=(lambda hk=hk, qb=qb, ob_=ob_: epi_cmp(hk, qb, ob_)), ob=ob_)
                    kbs = [kb for kb in range(qb - 4, qb + 1) if kb >= 0]
                    for i, kb in enumerate(kbs):
                        off = qb - kb
                        tab = tabX[:] if off == 4 else tabs[:, off]
                        rt = R("tabX") if off == 4 else R("tab", off)
                        lastu = (i == len(kbs) - 1)
                        if i == 0:
                            ob_ = new_branch()
                        unit(hk, qb, kT2, kb * 128, tab, False, V2[:, kb, hk, :], 65, i == 0, lastu, tabres=rt,
                             post=(lambda hk=hk, qb=qb, ob_=ob_: branch_epilogue(hk, qb, 2, False, ob_)) if lastu else None, ob=ob_)
                    for kb in range(qb + 1):
                        off = qb - kb
                        em = kb if qb >= 8 else None
                        lastu = (kb == qb)
                        if kb == 0:
                            ob_ = new_branch()
                        po = (lambda hk=hk, qb=qb, ob_=ob_: (branch_epilogue(hk, qb, 1, True, ob_), defer(6, lambda: outproj_B(hk, qb)))) if lastu else None
                        if off <= 7:
                            unit(hk, qb, kT1, kb * 128, tabs[:, off], False, V1[:, kb, hk, :], 65, kb == 0, lastu, emask_kb=em, post=po, ob=ob_,
                                 tabres=R("tab", off))
                        else:
                            unit(hk, qb, kT1, kb * 128, None, True, V1[:, kb, hk, :], 65, kb == 0, lastu, emask_kb=em, post=po, ob=ob_)
                flush_pipe()
            P.barrier()

        def outproj_B(hk, qb):
            mb = misc_bank()
            bb_ = bank_bf(mb)
            for kc in range(4):
                P.op("pe", TR(bb_[:, kc * 128:(kc + 1) * 128], attn_bf1[:, kc * 128:(kc + 1) * 128], ident[:]),
                     reads=[R("attn_bf"), R("ident")], writes=[R("bank", mb)])
            P.op("act", ACP(attnT1, bb_[:, 0:512]), reads=[R("bank", mb)], writes=[R("attnT")])
            outproj(qb, attnT1, 4, 4 * hk, w_o)

        def epi_cmp(hk, qb, ob):
            gview = gates[:].rearrange("p t (h b) -> p t h b", b=3)
            P.op("dve", TS(g8(den[:]), O2(97, ob)[:, :, :, 64], 1e-30, None, ALU.max), reads=Ores(ob), writes=[R("den")])
            P.op("dve", lambda e: e.reciprocal(out=rd[:], in_=den[:]), reads=[R("den")], writes=[R("rd")])
            P.op("dve", TT(fac[:], rd[:], gview[:, qb, 8 * hk:8 * hk + 8, 0], ALU.mult), reads=[R("rd"), R("gates")], writes=[R("fac")])
            P.op("dve", TT(o_acc.rearrange("p (h g) d -> p h g d", h=2), O2(97, ob)[:, :, :, 0:64],
                           g8(fac[:]).unsqueeze(3).to_broadcast([128, 2, 4, 64]), ALU.mult),
                 reads=Ores(ob) + [R("fac")], writes=[R("o_acc")])
            if qb >= 8:
                P.op("dve", TT(impn[:].rearrange("p (h g) j -> p h g j", h=2), O2(97, ob)[:, :, :, 65:97],
                               g8(rd[:]).unsqueeze(3).to_broadcast([128, 2, 4, 32]), ALU.mult),
                     reads=Ores(ob) + [R("rd")], writes=[R("impn")])
                P.op("dve", lambda e: e.tensor_reduce(out=imp[:], in_=impn[:].rearrange("p g j -> p j g"), axis=AX.X, op=ALU.add),
                     reads=[R("impn")], writes=[R("imp")])
                P.op("dve", TT(sc[:], imp[:], selc[:, qb - 8, 0:32], ALU.mult), reads=[R("imp"), R("selc")], writes=[R("sc")])
                P.op("dve", TT(sc[:], sc[:], selc[:, qb - 8, 32:64], ALU.add), reads=[R("sc"), R("selc")], writes=[R("sc")])
                P.op("dve", lambda e: e.max(out=m8[:, 0:8], in_=sc[:]), reads=[R("sc")], writes=[R("m8")])
                P.op("dve", lambda e: e.match_replace(out=sc2[:], in_to_replace=m8[:, 0:8], in_values=sc[:], imm_value=NEG),
                     reads=[R("sc"), R("m8")], writes=[R("sc2")])
                P.op("dve", lambda e: e.max(out=m8[:, 8:16], in_=sc2[:]), reads=[R("sc2")], writes=[R("m8")])
                P.op("dve", TS(mb_bf[:], sc[:], m8[:, 15:16], -30000.0, ALU.is_lt, ALU.mult), reads=[R("sc"), R("m8")], writes=[R("mb_bf")])
                def mbt_part():
                    mb = misc_bank()
                    bb_ = bank_bf(mb)
                    P.op("pe", TR(bb_[0:32, 0:128], mb_bf[:, 0:32], ident[:]), reads=[R("mb_bf"), R("ident")], writes=[R("bank", mb)])
                    for rr in range(4):
                        P.op("act", ACP(MBT[0:32, rr * 128:(rr + 1) * 128], bb_[0:32, 0:128]), reads=[R("bank", mb)], writes=[R("MBT")])
                defer(5, mbt_part)

        tmpm = P.sb("tmpm", [128, 8, 64], F32)

        def branch_epilogue(hk, qb, br, is_last, ob):
            gview = gates[:].rearrange("p t (h b) -> p t h b", b=3)
            P.op("dve", lambda e: e.reciprocal(out=g8(rd[:]), in_=O2(65, ob)[:, :, :, 64]), reads=Ores(ob), writes=[R("rd")])
            P.op("dve", TT(fac[:], rd[:], gview[:, qb, 8 * hk:8 * hk + 8, br], ALU.mult), reads=[R("rd"), R("gates")], writes=[R("fac")])
            P.op("dve", TT(tmpm[:].rearrange("p (h g) d -> p h g d", h=2), O2(65, ob)[:, :, :, 0:64],
                           g8(fac[:]).unsqueeze(3).to_broadcast([128, 2, 4, 64]), ALU.mult),
                 reads=Ores(ob) + [R("fac")], writes=[R("tmpm")])
            if is_last:
                P.op("pool", TT(attn_bf1.rearrange("p (g d) -> p g d", g=8), o_acc, tmpm[:], ALU.add),
                     reads=[R("o_acc"), R("tmpm")], writes=[R("attn_bf")])
            else:
                P.op("pool", TT(o_acc, o_acc, tmpm[:], ALU.add), reads=[R("o_acc"), R("tmpm")], writes=[R("o_acc")])

        nl = len(layer_ids)
        qsrc = {0: (I["a_w_qkv"], "qA"), 1: (I["b_w_in"], "qB")}
        for k_, li in enumerate(layer_ids):
            if k_ == 0:
                prefetch_q(*qsrc[li % 2])
                norm_phase(I["attn_norm"][li:li + 1, :])
                P.barrier(keep=wkeep())
            if li % 2 == 0:
                layer_A()
            else:
                layer_B()
            prefetch_ffn(li)
            norm_phase(I["ffn_norm"][li:li + 1, :])
            P.barrier(keep=wkeep())
            if k_ + 1 < nl:
                load_gain(I["attn_norm"][layer_ids[k_ + 1]:layer_ids[k_ + 1] + 1, :])
                ffn(li, fuse_norm="hnT")
                prefetch_q(*qsrc[layer_ids[k_ + 1] % 2])
            elif final:
                load_gain(I["final_norm"][0:1, :])
                ffn(li, fuse_norm="out")
            else:
                ffn(li)
            P.barrier(keep=wkeep())
        if not final:
            for t in range(NT):
                P.dma("sp", DMA(out[t * 128:(t + 1) * 128, :], h[:, t, :]), sem_out[t % 2], reads=[R("h", t)])
        P.barrier()
        stats = P.emit()
        stats["sbuf_left"] = nc.sbuf_bytes_remaining
    return nc, stats


FUSED = True


def _prep_shared(inputs):
    f = lambda a: np.ascontiguousarray(np.asarray(a, dtype=np.float32))
    sh = {
        "rel_table": f(inputs["rel_table"]), "attn_norm": f(inputs["attn_norm"]), "ffn_norm": f(inputs["ffn_norm"]),
        "final_norm": f(inputs["final_norm"]).reshape(1, D),
        "a_w_qkv": f(inputs["a_w_qkv"])[0], "a_b_qkv": f(inputs["a_b_qkv"]).reshape(1, 1280),
        "a_sinks": f(inputs["a_sinks"]).reshape(1, 16), "a_w_o": f(inputs["a_w_o"])[0],
        "a_b_o": f(inputs["a_b_o"]).reshape(1, D), "b_w_in": f(inputs["b_w_in"])[0],
        "b_cmp_pos": f(inputs["b_cmp_pos"])[0], "b_cmp_w1": f(inputs["b_cmp_w1"])[0],
        "b_cmp_w2": f(inputs["b_cmp_w2"])[0], "b_w_o": f(inputs["b_w_o"])[0],
        "ffn_w_gu": f(inputs["ffn_w_gu"]), "ffn_w_down": f(inputs["ffn_w_down"]),
    }
    sh.update(host_consts())
    return sh


def run_prog(layer_ids, final, xs, shared):
    nc, stats = build(layer_ids, final)
    in_maps = []
    for xb in xs:
        m = dict(shared)
        m["x"] = np.ascontiguousarray(xb, dtype=np.float32)
        in_maps.append(m)
    res = run_bass_kernel_spmd(nc, in_maps, core_ids=list(range(len(xs))))
    return [np.asarray(r["out"]) for r in res.results]


def kernel(**inputs):
    x = np.asarray(inputs["x"], dtype=np.float32)
    shared = _prep_shared(inputs)
    xs = [x[b] for b in range(x.shape[0])]
    if FUSED:
        outs = run_prog([0, 1], True, xs, shared)
    else:
        hs = run_prog([0], False, xs, shared)
        outs = run_prog([1], True, hs, shared)
    return np.stack(outs, axis=0).astype(np.float32)
```

```python
import math
from contextlib import ExitStack
import numpy as np
import concourse.bass as bass
import concourse.mybir as mybir
from concourse.bass_utils import run_bass_kernel_spmd

F32 = mybir.dt.float32
BF16 = mybir.dt.bfloat16
AF = mybir.ActivationFunctionType
ALU = mybir.AluOpType
AX = mybir.AxisListType

S = 2048
D = 1024
NT = 16
KC = 8
DFF = 2816
FOFF = 2063
FLEN = 4608
NEG = -1e30


class Res:
    __slots__ = ("name", "w", "r")

    def __init__(self, name=""):
        self.name = name
        self.w = None
        self.r = {}


class DmaSem:
    def __init__(self, handle, name):
        self.handle = handle
        self.name = name
        self.count = 0


class Queue:
    def __init__(self, name, eng, sem):
        self.name = name
        self.eng = eng
        self.sem = sem
        self.ops = []
        self.seen = {}


class Prog:
    def __init__(self, nc, stack):
        self.nc = nc
        self.stack = stack
        self.q = {}
        for name, eng in (("pe", nc.tensor), ("act", nc.scalar), ("dve", nc.vector),
                          ("pool", nc.gpsimd), ("sp", nc.sync)):
            sem = stack.enter_context(nc.semaphore("s_" + name))
            self.q[name] = Queue(name, eng, sem)
        self.n_dsem = 0
        self.pending = []
        self.rcache = {}

    def R(self, *key):
        r = self.rcache.get(key)
        if r is None:
            r = Res(str(key))
            self.rcache[key] = r
        return r

    def dsem(self, name=None):
        self.n_dsem += 1
        h = self.stack.enter_context(self.nc.semaphore("d_%d" % self.n_dsem))
        return DmaSem(h, name or "d%d" % self.n_dsem)

    def sb(self, name, shape, dtype):
        return self.stack.enter_context(self.nc.sbuf_tensor(name, list(shape), dtype))

    def ps(self, name, shape, dtype):
        return self.stack.enter_context(self.nc.psum_tensor(name, list(shape), dtype))

    def _need(self, q, toks):
        waits = []
        for (key, idx) in toks:
            if key is q and q.name == "pe":
                continue
            if q.seen.get(key, -1) >= idx:
                continue
            q.seen[key] = idx
            waits.append((key, idx))
            if isinstance(key, Queue):
                key.ops[idx]["mark"] = True
        return waits

    def _deps(self, q, reads, writes):
        toks = []
        for r in reads:
            if r.w is not None:
                toks.append(r.w)
        for w in writes:
            if w.w is not None:
                toks.append(w.w)
            toks.extend(w.r.values())
        return self._need(q, toks)

    def op(self, qname, fn, reads=(), writes=()):
        q = self.q[qname]
        waits = self._deps(q, reads, writes)
        idx = len(q.ops)
        q.ops.append({"fn": fn, "waits": waits, "mark": False, "dsem": None})
        tok = (q, idx)
        for r in reads:
            r.r[q] = tok
        for w in writes:
            w.w = tok
            w.r = {}
        return tok

    def dma(self, qname, fn, sem, reads=(), writes=(), cont=False):
        q = self.q[qname]
        toks = []
        for r in reads:
            if r.w is not None:
                toks.append(r.w)
        for w in writes:
            if w.w is not None:
                toks.append(w.w)
            toks.extend(w.r.values())
        toks = [t for t in toks if t[0] is not sem]
        if not cont and sem.count > 0:
            toks.append((sem, sem.count))
        waits = self._need(q, toks)
        q.ops.append({"fn": fn, "waits": waits, "mark": False, "dsem": sem})
        sem.count += 16
        tok = (sem, sem.count)
        for r in reads:
            r.r[sem] = tok
        for w in writes:
            w.w = tok
            w.r = {}
        self.pending.append(tok)
        return tok

    def wait_tok(self, qname, toks):
        q = self.q[qname]
        waits = self._need(q, toks)
        if waits:
            q.ops.append({"fn": None, "waits": waits, "mark": False, "dsem": None})

    def barrier(self, keep=()):
        keep_toks = set()
        for r in keep:
            if r.w is not None:
                keep_toks.add(r.w)
        last = []
        for q in self.q.values():
            for i in range(len(q.ops) - 1, -1, -1):
                if q.ops[i]["fn"] is not None and q.ops[i]["dsem"] is None:
                    last.append((q, i))
                    break
        kept_sem = {}
        for (key, val) in keep_toks:
            if isinstance(key, DmaSem):
                kept_sem[key] = max(kept_sem.get(key, 0), val)
        dl = {}
        still = []
        for (sem, val) in self.pending:
            if sem in kept_sem and val <= kept_sem[sem]:
                still.append((sem, val))
                continue
            dl[sem] = max(dl.get(sem, 0), val)
        dtoks = list(dl.items())
        self.pending = still
        for qn in self.q:
            self.wait_tok(qn, last + dtoks)
        kept = {k: v for k, v in self.rcache.items() if v in keep}
        self.rcache = kept

    def emit(self):
        for q in self.q.values():
            c = 0
            for o in q.ops:
                if o["mark"]:
                    c += 1
                o["cnt"] = c
        stats = {}
        with self.nc.Block() as block:
            def run(q):
                def body(eng):
                    nw = 0
                    for o in q.ops:
                        waits = o["waits"]
                        attach = None
                        if o["fn"] is not None and o["dsem"] is None and waits:
                            attach = waits[-1]
                            waits = waits[:-1]
                        for (key, idx) in waits:
                            if isinstance(key, Queue):
                                eng.wait_ge(key.sem, key.ops[idx]["cnt"])
                            else:
                                eng.wait_ge(key.handle, idx)
                            nw += 1
                        if o["fn"] is None:
                            continue
                        ins = o["fn"](eng)
                        if attach is not None:
                            key, idx = attach
                            if isinstance(key, Queue):
                                ins._wait_ge(key.sem, key.ops[idx]["cnt"])
                            else:
                                ins._wait_ge(key.handle, idx)
                        if o["dsem"] is not None:
                            ins.then_inc(o["dsem"].handle, 16)
                        elif o["mark"]:
                            ins.then_inc(q.sem, 1)
                    stats[q.name] = (len(q.ops), nw)
                return body
            block.tensor(run(self.q["pe"]))
            block.scalar(run(self.q["act"]))
            block.vector(run(self.q["dve"]))
            block.gpsimd(run(self.q["pool"]))
            block.sync(run(self.q["sp"]))
        return stats


def MM(out, lhsT, rhs, start, stop):
    return lambda e: e.matmul(out, lhsT=lhsT, rhs=rhs, start=start, stop=stop, skip_group_check=True)


def TR(out, in_, ident):
    return lambda e: e.transpose(out=out, in_=in_, identity=ident)


def ACTF(out, in_, func, **kw):
    return lambda e: e.activation(out=out, in_=in_, func=func, **kw)


def TT(out, in0, in1, op):
    return lambda e: e.tensor_tensor(out=out, in0=in0, in1=in1, op=op)


def TS(out, in0, s1, s2, op0, op1=None):
    if op1 is None:
        return lambda e: e.tensor_scalar(out=out, in0=in0, scalar1=s1, scalar2=None, op0=op0)
    return lambda e: e.tensor_scalar(out=out, in0=in0, scalar1=s1, scalar2=s2, op0=op0, op1=op1)


def STT(out, in0, scalar, in1, op0, op1):
    return lambda e: e.scalar_tensor_tensor(out=out, in0=in0, scalar=scalar, in1=in1, op0=op0, op1=op1)


def CP(out, in_):
    return lambda e: e.tensor_copy(out=out, in_=in_)


def ACP(out, in_):
    return lambda e: e.copy(out=out, in_=in_)


def DMA(out, in_, slow=False):
    if slow:
        return lambda e: e.dma_start(out=out, in_=in_, allow_slow_non_contiguous=True)
    return lambda e: e.dma_start(out=out, in_=in_)


def MSET(ap, v):
    return lambda e: e.memset(ap, v)


def _t5_bucket_np(dist):
    n = np.maximum(dist, 0)
    nf = np.maximum(n, 1).astype(np.float32)
    v = (np.log(nf / np.float32(16)) / np.float32(math.log(1024 / 16)) * np.float32(16)).astype(np.float32)
    log_b = 16 + v.astype(np.int32)
    return np.where(n < 16, n, np.minimum(log_b, 31))


def host_consts():
    c = {}
    c["c_ident"] = np.eye(128, dtype=np.float32)
    c["c_J"] = np.ascontiguousarray(np.eye(128, dtype=np.float32)[::-1])
    oh = np.zeros((33, FLEN), np.float32)
    idx = np.arange(FLEN)
    d = idx - FOFF
    valid = (d >= 0) & (d < S)
    b = _t5_bucket_np(np.where(valid, d, 0))
    oh[b[valid], idx[valid]] = 1.0
    oh[32, idx[~valid]] = 1.0
    c["c_onehot"] = oh
    E = np.zeros((32, S), np.float32)
    E[np.arange(S) // 64, np.arange(S)] = 1.0
    c["c_E"] = E
    p = np.arange(128)[:, None]
    j = np.arange(128)[None, :]
    c["c_tri"] = np.where(j >= p, np.float32(NEG), np.float32(0)).astype(np.float32)
    selc = np.zeros((128, 8, 64), np.float32)
    for qi in range(8):
        qb = 8 + qi
        t = qb * 128 + np.arange(128)
        cur = t // 64
        for jb in range(32):
            f0 = (jb == 0)
            f1 = (cur - jb == 1)
            f2 = (cur - jb == 0)
            forced = f0 | f1 | f2
            keep = (~forced) & (jb <= cur)
            fn = np.where(f2, 3e6, np.where(f1, 2e6, np.where(f0, 1e6, np.where(jb > cur, -1.0, 0.0))))
            selc[:, qi, jb] = keep.astype(np.float32)
            selc[:, qi, 32 + jb] = fn
    c["c_selc"] = selc
    ov = np.zeros((128, 32), np.float32)
    for pp in range(128):
        cc = 127 - pp
        if cc > 126:
            continue
        for jb in range(32):
            if 4 * jb - 1 <= cc <= 4 * jb + 3:
                ov[pp, jb] = 1.0
    c["c_ovrev"] = ov
    return c


INPUT_SHAPES = {
    "x": [S, D], "rel_table": [32, 16], "attn_norm": [2, D], "ffn_norm": [2, D], "final_norm": [1, D],
    "a_w_qkv": [D, 1280], "a_b_qkv": [1, 1280], "a_sinks": [1, 16], "a_w_o": [D, D], "a_b_o": [1, D],
    "b_w_in": [D, 1840], "b_cmp_pos": [2, 32, 64], "b_cmp_w1": [2, 2048, 256], "b_cmp_w2": [2, 256, 64],
    "b_w_o": [D, D], "ffn_w_gu": [2, D, 2 * DFF], "ffn_w_down": [2, DFF, D],
    "c_ident": [128, 128], "c_J": [128, 128], "c_onehot": [33, FLEN], "c_E": [32, S], "c_tri": [128, 128],
    "c_selc": [128, 8, 64], "c_ovrev": [128, 32],
}


def build(layer_ids, final):
    nc = bass.Bass("TRN2", target_bir_lowering=False)
    I = {k: nc.dram_tensor(k, v, F32, kind="ExternalInput").ap() for k, v in INPUT_SHAPES.items()}
    out = nc.dram_tensor("out", [S, D], F32, kind="ExternalOutput").ap()
    Fd = nc.dram_tensor("Fd", [16, FLEN], F32, kind="Internal").ap()
    ksave = nc.dram_tensor("ksave", [2, 64, S], BF16, kind="Internal").ap()

    def dap(ap, off, pat):
        return bass.AP(ap.tensor, off, pat)

    with ExitStack() as st:
        P = Prog(nc, st)
        R = P.R
        h = P.sb("h", [128, NT, D], F32)
        arenaA = P.sb("arenaA", [128, 16384], BF16)
        arenaB = P.sb("arenaB", [128, 43008], BF16)
        wbuf = [P.sb("wbuf%d" % i, [128, KC, 256], BF16) for i in range(2)]
        tabX = P.sb("tabX", [128, 8, 128], F32)
        gbc = P.sb("gbc", [128, D], F32)
        selc = P.sb("selc", [128, 8, 64], F32)
        ident = P.sb("ident", [128, 128], BF16)
        J_bf = P.sb("J_bf", [128, 128], BF16)
        J32 = P.sb("J32", [128, 128], F32)
        tri = P.sb("tri", [128, 128], F32)
        ovrev = P.sb("ovrev", [128, 32], BF16)
        relaug = P.sb("relaug", [33, 16], F32)
        ssq = P.sb("ssq", [128, NT], F32)
        rstd = P.sb("rstd", [128, NT], F32)
        bq = P.sb("bq", [128, 8], F32)
        bk = P.sb("bk", [128, 1], F32)
        bv_bc = P.sb("bv_bc", [128, 128], F32)
        esink = P.sb("esink", [128, 16], F32)
        b31 = P.sb("b31", [128, 16], F32)
        den = P.sb("den", [128, 8], F32)
        rd = P.sb("rd", [128, 8], F32)
        fac = P.sb("fac", [128, 8], F32)
        impn = P.sb("impn", [128, 8, 32], F32)
        imp = P.sb("imp", [128, 32], F32)
        sc = P.sb("sc", [128, 32], F32)
        sc2 = P.sb("sc2", [128, 32], F32)
        m8 = P.sb("m8", [128, 16], F32)
        mb_bf = P.sb("mb_bf", [128, 32], BF16)
        MBT = P.sb("MBT", [128, 512], BF16)
        Vc = P.sb("Vc", [128, 2, 97], BF16)
        kcT2 = P.sb("kcT2", [128, 2, 128], BF16)
        posT = P.sb("posT", [128, 32], F32)
        w2dup = P.sb("w2dup", [128, 2, 128], BF16)
        w2v = P.sb("w2v", [128, 2, 64], BF16)

        bank = [P.ps("bank%d" % i, [128, 512], F32) for i in range(4)]
        Obuf = [P.ps("Obuf%d" % i, [128, 1024], F32) for i in range(2)]
        for i_ in range(2):
            bank += [Obuf[i_][:, 0:512], Obuf[i_][:, 512:1024]]

        def bank_bf(i):
            return bank[i][:].bitcast(BF16)

        misc_rr = [0]

        att_mode = [False]
        srr = [0]

        def next_sbank():
            srr[0] = (srr[0] + 1) % 4
            return srr[0]

        def misc_bank():
            if att_mode[0]:
                return next_sbank()
            misc_rr[0] ^= 1
            return 6 + misc_rr[0]

        obase = [4]

        def new_branch():
            obase[0] = 10 - obase[0]
            return obase[0]

        def carve(arena, boff, nbytes, dtype, pattern=None, **kw):
            v = arena[:, boff // 2:(boff + nbytes) // 2]
            if dtype is F32:
                v = v.bitcast(F32)
            if pattern:
                v = v.rearrange(pattern, **kw)
            return v

        hnT = carve(arenaA, 0, 32768, BF16, "p (k t) -> p k t", k=KC)
        tabs = carve(arenaA, 0, 32768, F32, "p (s g j) -> p s g j", s=8, g=8)
        blk = carve(arenaA, 0, 8192, BF16, "p (c l) -> p c l", c=128)
        xs_g = carve(arenaA, 8192, 1024, F32)
        t1_g = carve(arenaA, 9216, 1024, F32)
        gl_bf = carve(arenaA, 10240, 512, BF16)
        GT = carve(arenaA, 10752, 512, BF16)

        qTp = carve(arenaB, 0, 32768, BF16, "p (g t) -> p g t", g=8)
        kT1 = carve(arenaB, 32768, 4096, BF16)
        kT2 = carve(arenaB, 36864, 4096, BF16)
        V1 = carve(arenaB, 40960, 4160, BF16, "p (t k c) -> p t k c", t=NT, k=2)
        V2 = carve(arenaB, 45120, 4160, BF16, "p (t k c) -> p t k c", t=NT, k=2)
        gates = carve(arenaB, 49280, 3072, F32, "p (t c) -> p t c", t=NT)
        w_o = carve(arenaB, 52352, 16384, BF16, "p (k n) -> p k n", k=KC)
        w1sb = carve(arenaB, 52352, 16384, BF16, "p (l m) -> p l m", l=32)
        WK = 68736
        E_bf = carve(arenaB, WK + 4096, 4096, BF16)
        PT = [carve(arenaB, WK + 8192 + 1024 * i, 1024, BF16) for i in range(4)]
        o_acc = carve(arenaB, WK + 12288, 2048, F32, "p (g d) -> p g d", g=8)
        attn_bf = carve(arenaB, WK + 12288, 2048, BF16)
        attn_bf1 = carve(arenaB, WK + 14336, 1024, BF16)
        attnT = carve(arenaB, WK + 14336, 2048, BF16)
        attnT1 = carve(arenaB, WK + 15360, 1024, BF16)
        Htmp = carve(arenaB, WK, 4096, F32, "p (g j) -> p g j", g=8)
        cmpT = carve(arenaB, WK, 16512, F32, "p (k t) -> p k t", k=2)
        hn_bf = [carve(arenaB, 2048 * i, 2048, BF16) for i in range(2)]
        junk = carve(arenaB, 4096, 2048, BF16)
        ostage = [carve(arenaB, 8192 + 4096 * i, 4096, F32) for i in range(2)]
        hn_bf2 = [carve(arenaB, 71680 + 2048 * i, 2048, BF16) for i in range(2)]
        junk2 = carve(arenaB, 75776, 2048, BF16)
        ostage2 = [carve(arenaB, 77824 + 4096 * i, 4096, F32) for i in range(2)]
        oh_sb = carve(arenaB, 0, 18432, F32)
        Fsb = carve(arenaB, 18432, 18432, F32)
        actT = carve(arenaB, 0, 45056, BF16, "p (f t) -> p f t", f=11)
        wd = carve(arenaB, 45056, 22528, BF16, "p (f n) -> p f n", f=11)
        sg = [carve(arenaB, 67584 + 2048 * i, 2048, F32) for i in range(2)]

        sem_w = [P.dsem("w0"), P.dsem("w1")]
        sem_misc = {"sp": [P.dsem("ms%d" % i) for i in range(6)], "pool": [P.dsem("mp%d" % i) for i in range(4)]}
        mrr = {"sp": 0, "pool": 0}

        def msem(qn):
            mrr[qn] = (mrr[qn] + 1) % len(sem_misc[qn])
            return sem_misc[qn][mrr[qn]]

        wrr = [0]

        def next_wbuf():
            i = wrr[0]
            wrr[0] ^= 1
            return i

        sem_x = [P.dsem("x%d" % i) for i in range(4)]
        xv = I["x"].rearrange("(t p) d -> p t d", p=128)
        for gi in range(4):
            for t in range(4 * gi, 4 * gi + 4):
                P.dma("sp", DMA(h[:, t, :], xv[:, t, :]), sem_x[gi], writes=[R("h", t)], cont=(t % 4 != 0))
            for t in range(4 * gi, 4 * gi + 4):
                R("h", t).w = (sem_x[gi], sem_x[gi].count)
        P.dma("pool", DMA(ident[:], I["c_ident"]), msem("pool"), writes=[R("ident")])
        P.dma("pool", DMA(J_bf[:], I["c_J"]), msem("pool"), writes=[R("J_bf")])
        P.dma("sp", DMA(J32[:], I["c_J"]), msem("sp"), writes=[R("J32")])
        P.dma("sp", DMA(tri[:], I["c_tri"]), msem("sp"), writes=[R("tri")])
        P.dma("sp", DMA(selc[:], I["c_selc"]), msem("sp"), writes=[R("selc")])
        P.dma("pool", DMA(ovrev[:], I["c_ovrev"]), msem("pool"), writes=[R("ovrev")])
        P.op("pool", MSET(relaug[32:33, :], NEG), writes=[R("relaug")])
        P.dma("sp", DMA(relaug[0:32, :], I["rel_table"]), msem("sp"), writes=[R("relaug")])
        P.dma("sp", DMA(oh_sb[0:33, :], I["c_onehot"]), msem("sp"), writes=[R("oh")])
        P.dma("sp", DMA(b31[:], dap(I["rel_table"], 31 * 16, [[0, 128], [1, 16]])), msem("sp"), writes=[R("b31")])
        for c in range(FLEN // 512):
            bi = c % 2
            P.op("pe", MM(bank[bi][0:16, :], relaug[0:33, 0:16], oh_sb[0:33, c * 512:(c + 1) * 512], True, True),
                 reads=[R("relaug"), R("oh")], writes=[R("bank", bi)])
            P.op("act", ACP(Fsb[0:16, c * 512:(c + 1) * 512], bank[bi][0:16, :]), reads=[R("bank", bi)], writes=[R("Fsb")])
        sem_F = P.dsem("F")
        P.dma("sp", DMA(Fd, Fsb[0:16, :]), sem_F, reads=[R("Fsb")], writes=[R("Fd")])
        keepers = lambda: [R("ident"), R("J_bf"), R("J32"), R("tri"), R("E"), R("selc"), R("ovrev"), R("b31"), R("Fd")]
        P.barrier()
        for t in range(NT):
            R("h", t)

        def load_gain(gain_ap_row):
            P.dma("sp", DMA(gbc[:], dap(gain_ap_row, gain_ap_row.offset, [[0, 128], [1, D]])), msem("sp"), writes=[R("gbc")])

        def norm_tiles(tiles, to_out, tmp):
            junk_, hn_, os_ = tmp
            t0_, t1_ = tiles[0], tiles[-1] + 1
            for i in tiles:
                P.op("act", ACTF(junk_, h[:, i, :], AF.Square, accum_out=ssq[:, i:i + 1]),
                     reads=[R("h", i)], writes=[R("ssq", i), R("junk")])
            sr = [R("ssq", i) for i in tiles]
            P.op("dve", TS(rstd[:, t0_:t1_], ssq[:, t0_:t1_], 1.0 / D, 1e-6, ALU.mult, ALU.add), reads=sr, writes=[R("rstd", t0_)])
            P.op("act", ACTF(rstd[:, t0_:t1_], rstd[:, t0_:t1_], AF.Sqrt), reads=[R("rstd", t0_)], writes=[R("rstd", t0_)])
            P.op("dve", lambda e: e.reciprocal(out=rstd[:, t0_:t1_], in_=rstd[:, t0_:t1_]), reads=[R("rstd", t0_)], writes=[R("rstd", t0_)])
            for i in tiles:
                sl = i % 2
                if to_out:
                    P.op("dve", STT(os_[sl], h[:, i, :], rstd[:, i:i + 1], gbc[:], ALU.mult, ALU.mult),
                         reads=[R("h", i), R("rstd", t0_), R("gbc")], writes=[R("ostage", sl)])
                    P.dma("sp", DMA(out[i * 128:(i + 1) * 128, :], os_[sl]), sem_out[sl], reads=[R("ostage", sl)])
                    continue
                P.op("dve", STT(hn_[sl], h[:, i, :], rstd[:, i:i + 1], gbc[:], ALU.mult, ALU.mult),
                     reads=[R("h", i), R("rstd", t0_), R("gbc")], writes=[R("hn_bf", sl)])
                mb = misc_bank()
                bb = bank_bf(mb)
                for kc in range(KC):
                    P.op("pe", TR(bb[:, kc * 128:(kc + 1) * 128], hn_[sl][:, kc * 128:(kc + 1) * 128], ident[:]),
                         reads=[R("hn_bf", sl), R("ident")], writes=[R("bank", mb)])
                P.op("act", ACP(hnT[:, :, i * 128:(i + 1) * 128], bb.rearrange("p (k t) -> p k t", k=KC)),
                     reads=[R("bank", mb)], writes=[R("hnT", i)])

        def norm_phase(gain_ap_row, to_out=False):
            load_gain(gain_ap_row)
            norm_tiles(list(range(NT)), to_out, (junk, hn_bf, ostage))

        def load_w256(src2d, col_specs):
            wi = next_wbuf()
            srcv = src2d.rearrange("(k p) n -> p k n", p=128)
            for ci, (d0, s0, n) in enumerate(col_specs):
                P.dma("pool", DMA(wbuf[wi][:, :, d0:d0 + n], srcv[:, :, s0:s0 + n]), sem_w[wi], writes=[R("wbuf", wi)], cont=(ci > 0))
            return wi

        pre_w = {}

        def load_qpair_chunk(src2d, c, key=None):
            if key is not None and key in pre_w:
                return pre_w.pop(key)
            wi = next_wbuf()
            srcv = src2d.rearrange("(k p) n -> p k n", p=128)
            dst5 = wbuf[wi][:].rearrange("p k (pl two d) -> p k pl two d", pl=2, two=2)
            for two in range(2):
                s = srcv[:, :, two * 512 + 128 * c: two * 512 + 128 * c + 128].rearrange("p k (pl d) -> p k pl d", pl=2)
                for kc in range(KC):
                    P.dma("pool", DMA(dst5[:, kc, :, two, :], s[:, kc, :, :]), sem_w[wi], writes=[R("wbuf", wi)], cont=not (two == 0 and kc == 0))
            return wi

        def hn_reads(tc):
            return [R("hnT", t) for t in range(4 * tc, 4 * tc + 4)]

        def proj_feat(wi, c0, evac, tag):
            for tc in range(4):
                mb = misc_bank()
                for kc in range(KC):
                    P.op("pe", MM(bank[mb][:], wbuf[wi][:, kc, c0:c0 + 128], hnT[:, kc, tc * 512:(tc + 1) * 512],
                                  kc == 0, kc == KC - 1),
                         reads=[R("wbuf", wi)] + hn_reads(tc), writes=[R("bank", mb)])
                evac(tc, mb)

        def proj_tok(wi, c0, n, evac):
            for t in range(NT):
                mb = misc_bank()
                for kc in range(KC):
                    P.op("pe", MM(bank[mb][:, 0:n], hnT[:, kc, t * 128:(t + 1) * 128], wbuf[wi][:, kc, c0:c0 + n],
                                  kc == 0, kc == KC - 1),
                         reads=[R("wbuf", wi), R("hnT", t)], writes=[R("bank", mb)])
                evac(t, mb)

        def prefetch_q(src2d, name):
            for c in range(2):
                pre_w[(name, c)] = load_qpair_chunk(src2d, c)

        def q_proj(src2d, has_bias, name):
            for c in range(4):
                wi = load_qpair_chunk(src2d, c, key=(name, c))
                for pl in range(2):
                    g = 2 * c + pl

                    def ev(tc, mb, g=g):
                        if has_bias:
                            P.op("dve", TS(qTp[:, g, tc * 512:(tc + 1) * 512], bank[mb][:], bq[:, g:g + 1], 0.125, ALU.add, ALU.mult),
                                 reads=[R("bank", mb), R("bq")], writes=[R("qTp", g, tc)])
                        else:
                            P.op("dve", TS(qTp[:, g, tc * 512:(tc + 1) * 512], bank[mb][:], 0.125, None, ALU.mult),
                                 reads=[R("bank", mb)], writes=[R("qTp", g, tc)])
                    proj_feat(wi, pl * 128, ev, "q")

        pipe = []

        def unit(hk, qb, kT, kcol0, tab4, const_bias, Vrhs, ncols, first, last, emask_kb=None, tabres=None,
                 kres=None, vres=None, pre=None, post=None, ob=4):
            pipe.append(dict(hk=hk, qb=qb, kT=kT, kcol0=kcol0, tab4=tab4, const_bias=const_bias, Vrhs=Vrhs, ncols=ncols,
                             first=first, last=last, emask_kb=emask_kb, tabres=tabres, kres=kres, vres=vres, pre=pre, post=post, ob=ob))

        SKEW = 3

        def emit_S(u, half, sl):
            hk, qb = u["hk"], u["qb"]
            if half == 0 and u["pre"] is not None:
                u["pre"]()
            qrd = [R("qTp", g, qb // 4) for g in range(4 * half, 4 * half + 4)]
            bi = next_sbank()
            rhs = qTp[:, 4 * half:4 * half + 4, qb * 128:(qb + 1) * 128]
            P.op("pe", MM(bank[bi][:], u["kT"][:, u["kcol0"]:u["kcol0"] + 128], rhs, True, u["emask_kb"] is None),
                 reads=[u["kres"] or R("kT")] + qrd, writes=[R("bank", bi)])
            if u["emask_kb"] is not None:
                ek = u["emask_kb"]
                P.op("pe", MM(bank[bi][:], E_bf[:, ek * 128:(ek + 1) * 128], MBT[:], False, True),
                     reads=[R("E"), R("MBT")], writes=[R("bank", bi)])
            bv3 = bank[bi][:].rearrange("p (g j) -> p g j", g=4)
            if u["const_bias"]:
                in1 = b31[:, 8 * hk + 4 * half: 8 * hk + 4 * half + 4].unsqueeze(2).to_broadcast([128, 4, 128])
                P.op("dve", TT(bv3, bv3, in1, ALU.add), reads=[R("bank", bi), R("b31")], writes=[R("bank", bi)])
            else:
                P.op("dve", TT(bv3, bv3, u["tab4"][:, 4 * half:4 * half + 4, :], ALU.add),
                     reads=[R("bank", bi), u["tabres"] or R("tab")], writes=[R("bank", bi)])
            P.op("act", ACTF(PT[sl], bank[bi][:], AF.Exp), reads=[R("bank", bi)], writes=[R("PT", sl)])

        def emit_PV(u, half, sl):
            ncols = u["ncols"]
            ob = u["ob"] + half
            for g4 in range(4):
                oap = bank[ob][:, g4 * ncols:(g4 + 1) * ncols]
                P.op("pe", MM(oap, PT[sl][:, g4 * 128:(g4 + 1) * 128], u["Vrhs"], u["first"] and (g4 == 0), u["last"]),
                     reads=[R("PT", sl), u["vres"] or R("V")], writes=[R("bank", ob)])
            if half == 1 and u["post"] is not None:
                u["post"]()

        deferred = []
        cur_i = [0]

        def defer(k, fn):
            deferred.append((cur_i[0] + k, fn))

        def run_deferred(upto):
            keep_ = []
            for (due, fn) in list(deferred):
                if due <= upto:
                    deferred.remove((due, fn))
                    fn()
            return

        def flush_pipe():
            att_mode[0] = True
            hu = [(u, half) for u in pipe for half in range(2)]
            n = len(hu)
            for i in range(n + SKEW):
                cur_i[0] = i
                run_deferred(i)
                if i < n:
                    emit_S(hu[i][0], hu[i][1], i % 4)
                if i >= SKEW:
                    j = i - SKEW
                    emit_PV(hu[j][0], hu[j][1], j % 4)
            while deferred:
                cur_i[0] += 1
                run_deferred(cur_i[0])
            del pipe[:]
            att_mode[0] = False

        def O2(ncols, ob):
            return Obuf[(ob - 4) // 2][:].rearrange("p (h c) -> p h c", h=2)[:, :, 0:4 * ncols].rearrange("p h (g c) -> p h g c", g=4)

        def Ores(ob):
            return [R("bank", ob), R("bank", ob + 1)]

        def g8(ap):
            return ap.rearrange("p (h g) -> p h g", h=2)

        def Oview(b, ncols, ob):
            return bank[ob + b][:, 0:4 * ncols].rearrange("p (g c) -> p g c", g=4)

        hrr = [0]

        def build_table(dst, hk, off, add_tri, hbufs, tres):
            base = (8 * hk) * FLEN + FOFF + off * 128 - 127
            hrr[0] = (hrr[0] + 1) % len(hbufs)
            Htmp, hres = hbufs[hrr[0]]
            P.dma("sp", DMA(Htmp, dap(Fd, base, [[1, 128], [FLEN, 8], [1, 128]])), msem("sp"),
                  reads=[R("Fd")], writes=[hres])
            for half in range(2):
                mb = misc_bank()
                P.op("pe", MM(bank[mb][:], J32[:], Htmp[:, 4 * half:4 * half + 4, :].rearrange("p g j -> p (g j)"), True, True),
                     reads=[R("J32"), hres], writes=[R("bank", mb)])
                dv = dst[:, 4 * half:4 * half + 4, :]
                if add_tri:
                    P.op("dve", TT(dv, bank[mb][:].rearrange("p (g j) -> p g j", g=4),
                                   tri[:].unsqueeze(1).to_broadcast([128, 4, 128]), ALU.add),
                         reads=[R("bank", mb), R("tri")], writes=[tres])
                else:
                    P.op("act", ACP(dv, bank[mb][:].rearrange("p (g j) -> p g j", g=4)),
                         reads=[R("bank", mb)], writes=[tres])

        def outproj(qb, aT, nk, kc0, wtile):
            for nh in range(2):
                mb = misc_bank()
                for kc in range(nk):
                    P.op("pe", MM(bank[mb][:], aT[:, kc * 128:(kc + 1) * 128], wtile[:, kc0 + kc, nh * 512:(nh + 1) * 512],
                                  kc == 0, kc == nk - 1),
                         reads=[R("attnT"), R("w_o")], writes=[R("bank", mb)])
                P.op("dve", TT(h[:, qb, nh * 512:(nh + 1) * 512], bank[mb][:], h[:, qb, nh * 512:(nh + 1) * 512], ALU.add),
                     reads=[R("bank", mb), R("h", qb)], writes=[R("h", qb)])

        def load_wo(src2d):
            srcv = src2d.rearrange("(k p) n -> p k n", p=128)
            for kc in range(KC):
                P.dma("pool", DMA(w_o[:, kc, :], srcv[:, kc, :]), sem_wo, writes=[R("w_o")], cont=(kc > 0))

        sem_wo = P.dsem("wo")
        sem_wd = P.dsem("wd")
        sem_out = [P.dsem("o0"), P.dsem("o1")]

        def prefetch_ffn(layer):
            wgu = I["ffn_w_gu"][layer]
            wdv = I["ffn_w_down"][layer].rearrange("(f p) n -> p f n", p=128)
            for fc in range(2):
                pre_w[("gu", layer, fc)] = load_w256(wgu, [(0, 128 * fc, 128), (128, DFF + 128 * fc, 128)])
            for fl in range(11):
                P.dma("pool", DMA(wd[:, fl, :], wdv[:, fl, :]), sem_wd, writes=[R("wd")], cont=(fl > 0))
            pre_w[("wd", layer)] = True

        def wkeep():
            return [R("wbuf", 0), R("wbuf", 1), R("wd")]

        def ffn(layer, fuse_norm=None):
            wgu = I["ffn_w_gu"][layer]
            wdn = I["ffn_w_down"][layer]
            wdv = wdn.rearrange("(f p) n -> p f n", p=128)
            sgr = [0]
            for half in range(2):
                if not (half == 0 and ("wd", layer) in pre_w):
                    for fl in range(11):
                        P.dma("pool", DMA(wd[:, fl, :], wdv[:, 11 * half + fl, :]), sem_wd, writes=[R("wd")], cont=(fl > 0))
                else:
                    pre_w.pop(("wd", layer))
                for fl in range(11):
                    fc = 11 * half + fl
                    if ("gu", layer, fc) in pre_w:
                        wi = pre_w.pop(("gu", layer, fc))
                    else:
                        wi = load_w256(wgu, [(0, 128 * fc, 128), (128, DFF + 128 * fc, 128)])
                    for tc in range(4):
                        bg = 2 * (tc % 2)
                        bu = bg + 1
                        for kc in range(KC):
                            P.op("pe", MM(bank[bg][:], wbuf[wi][:, kc, 0:128], hnT[:, kc, tc * 512:(tc + 1) * 512], kc == 0, kc == KC - 1),
                                 reads=[R("wbuf", wi)] + hn_reads(tc), writes=[R("bank", bg)])
                        for kc in range(KC):
                            P.op("pe", MM(bank[bu][:], wbuf[wi][:, kc, 128:256], hnT[:, kc, tc * 512:(tc + 1) * 512], kc == 0, kc == KC - 1),
                                 reads=[R("wbuf", wi)] + hn_reads(tc), writes=[R("bank", bu)])
                        si = sgr[0]
                        sgr[0] ^= 1
                        P.op("act", ACTF(sg[si], bank[bg][:], AF.Silu), reads=[R("bank", bg)], writes=[R("sg", si)])
                        P.op("dve", TT(actT[:, fl, tc * 512:(tc + 1) * 512], sg[si], bank[bu][:], ALU.mult),
                             reads=[R("sg", si), R("bank", bu)], writes=[R("actT", tc)])
                if half == 0:
                    for fc in (11, 12):
                        pre_w[("gu", layer, fc)] = load_w256(wgu, [(0, 128 * fc, 128), (128, DFF + 128 * fc, 128)])
                for t in range(NT):
                    for nh in range(2):
                        mb = 4 + (2 * t + nh) % 4
                        for fl in range(11):
                            P.op("pe", MM(bank[mb][:], actT[:, fl, t * 128:(t + 1) * 128], wd[:, fl, nh * 512:(nh + 1) * 512], fl == 0, fl == 10),
                                 reads=[R("actT", t // 4), R("wd")], writes=[R("bank", mb)])
                        P.op("dve", TT(h[:, t, nh * 512:(nh + 1) * 512], bank[mb][:], h[:, t, nh * 512:(nh + 1) * 512], ALU.add),
                             reads=[R("bank", mb), R("h", t)], writes=[R("h", t)])
                    if half == 1 and fuse_norm is not None and t % 4 == 3:
                        norm_tiles(list(range(t - 3, t + 1)), fuse_norm == "out", (junk2, hn_bf2, ostage2))

        def layer_A():
            W = I["a_w_qkv"]
            b = I["a_b_qkv"]
            for two in range(2):
                P.dma("sp", DMA(bq[64 * two:64 * two + 64, :], dap(b, 512 * two, [[1, 64], [64, 8]]), slow=True), msem("sp"), writes=[R("bq")])
            P.dma("sp", DMA(bk[:], dap(b, 1024, [[1, 128], [1, 1]]), slow=True), msem("sp"), writes=[R("bk")])
            P.dma("sp", DMA(bv_bc[:], dap(b, 1152, [[0, 128], [1, 128]])), msem("sp"), writes=[R("bv")])
            q_proj(W, True, "qA")
            wi = load_w256(W, [(0, 1024, 256)])

            P.op("pool", MSET(kT1[64:128, :], 0.0), writes=[R("kTz", 0)])
            P.op("pool", MSET(kT2[0:64, :], 0.0), writes=[R("kTz", 1)])

            def ev_k(tc, mb):
                P.op("dve", TS(kT1[0:64, tc * 512:(tc + 1) * 512], bank[mb][0:64, :], bk[0:64, 0:1], None, ALU.add),
                     reads=[R("bank", mb), R("bk")], writes=[R("kT")])
                P.op("dve", TS(kT2[64:128, tc * 512:(tc + 1) * 512], bank[mb][64:128, :], bk[64:128, 0:1], None, ALU.add),
                     reads=[R("bank", mb), R("bk")], writes=[R("kT")])
            proj_feat(wi, 0, ev_k, "k")

            def ev_v(t, mb):
                P.op("dve", TT(V1[:, t, :, 0:64], bank[mb][:, 0:128].rearrange("p (k d) -> p k d", k=2),
                               bv_bc[:].rearrange("p (k d) -> p k d", k=2), ALU.add),
                     reads=[R("bank", mb), R("bv")], writes=[R("V")])
            proj_tok(wi, 128, 128, ev_v)
            P.op("pool", MSET(V1[:, :, :, 64], 1.0), writes=[R("V")])
            P.barrier()
            P.dma("sp", DMA(gbc[:], dap(I["a_b_o"], 0, [[0, 128], [1, D]])), msem("sp"), writes=[R("gbc")])
            P.dma("sp", DMA(esink[:], dap(I["a_sinks"], 0, [[0, 128], [1, 16]])), msem("sp"), writes=[R("esink")])
            P.op("act", ACTF(esink[:], esink[:], AF.Exp), reads=[R("esink")], writes=[R("esink")])
            for hk in range(2):
                build_table(tabs[:, hk], hk, 0, False, ((Htmp, R("Htmp")), (tabX[:], R("tabX"))), R("tab", hk))
                build_table(tabs[:, 2 + hk], hk, 1, True, ((Htmp, R("Htmp")), (tabX[:], R("tabX"))), R("tab", 2 + hk))
            load_wo(I["a_w_o"])
            def epiA(hk, qb, ob):
                P.op("dve", TT(g8(den[:]), O2(65, ob)[:, :, :, 64], g8(esink[:, 8 * hk:8 * hk + 8]), ALU.add),
                     reads=Ores(ob) + [R("esink")], writes=[R("den")])
                P.op("dve", lambda e: e.reciprocal(out=rd[:], in_=den[:]), reads=[R("den")], writes=[R("rd")])
                c0 = 8 * hk * 64
                P.op("dve", TT(attn_bf[:, c0:c0 + 512].rearrange("p (h g d) -> p h g d", h=2, g=4), O2(65, ob)[:, :, :, 0:64],
                               g8(rd[:]).unsqueeze(3).to_broadcast([128, 2, 4, 64]), ALU.mult),
                     reads=Ores(ob) + [R("rd")], writes=[R("attn_bf")])
                if hk == 1:
                    P.op("pool", TT(h[:, qb, :], h[:, qb, :], gbc[:], ALU.add), reads=[R("h", qb), R("gbc")], writes=[R("h", qb)])

                    def opA(qb=qb):
                        mb = misc_bank()
                        bb_ = bank_bf(mb)
                        for kc in range(KC):
                            P.op("pe", TR(bb_[:, kc * 128:(kc + 1) * 128], attn_bf[:, kc * 128:(kc + 1) * 128], ident[:]),
                                 reads=[R("attn_bf"), R("ident")], writes=[R("bank", mb)])
                        P.op("act", ACP(attnT, bb_), reads=[R("bank", mb)], writes=[R("attnT")])
                        outproj(qb, attnT, KC, 0, w_o)
                    defer(3, opA)

            for qb in range(NT):
                for hk in range(2):
                    kbs = [kb for kb in (qb - 1, qb) if kb >= 0]
                    for i, kb in enumerate(kbs):
                        tab = tabs[:, hk] if kb == qb else tabs[:, 2 + hk]
                        tres_ = R("tab", hk) if kb == qb else R("tab", 2 + hk)
                        lastu = (i == len(kbs) - 1)
                        if i == 0:
                            ob_ = new_branch()
                        unit(hk, qb, kT1 if hk == 0 else kT2, kb * 128, tab, False, V1[:, kb, hk, :], 65, i == 0, lastu, tabres=tres_,
                             post=(lambda hk=hk, qb=qb, ob_=ob_: epiA(hk, qb, ob_)) if lastu else None, ob=ob_)
            flush_pipe()
            P.barrier()

        def layer_B():
            W = I["b_w_in"]
            for which in range(2):
                w1v = I["b_cmp_w1"][which].rearrange("(l d) m -> d l m", d=64)
                for lq in range(4):
                    P.dma("pool", DMA(w1sb[64 * which:64 * which + 64, 8 * lq:8 * lq + 8, :], w1v[:, 8 * lq:8 * lq + 8, :]), sem_wo,
                          writes=[R("w1sb")], cont=not (which == 0 and lq == 0))
                P.dma("sp", DMA(posT[64 * which:64 * which + 64, :], I["b_cmp_pos"][which].rearrange("l d -> d l"), slow=True),
                      msem("sp"), writes=[R("posT")])
            w2k = I["b_cmp_w2"][0].rearrange("(c p) d -> p c d", p=128)
            for two in range(2):
                P.dma("pool", DMA(w2dup[:, :, 64 * two:64 * two + 64], w2k), msem("pool"), writes=[R("w2k", two)])
            P.dma("pool", DMA(w2v[:], I["b_cmp_w2"][1].rearrange("(c p) d -> p c d", p=128)), msem("pool"), writes=[R("w2v")])
            q_proj(W, False, "qB")
            P.op("pool", MSET(cmpT[:, :, 2048:2064], 0.0), writes=[R("cmpT", 0), R("cmpT", 1)])
            wi = load_w256(W, [(0, 1024, 64), (64, 1152, 64), (128, 1088, 64), (192, 1216, 64)])
            for hk_ in range(2):
                def ev_c(tc, mb, hk_=hk_):
                    P.op("act", ACP(cmpT[:, hk_, tc * 512:(tc + 1) * 512], bank[mb][:]), reads=[R("bank", mb)], writes=[R("cmpT", hk_)])
                proj_feat(wi, 128 * hk_, ev_c, "c")
            for (c0, kTd, Vd, nm) in ((1280, kT1, V1, "s"), (1536, kT2, V2, "w")):
                wi = load_w256(W, [(0, c0, 256)])

                def ev_k(tc, mb, kTd=kTd, nm=nm):
                    P.op("act", ACP(kTd[:, tc * 512:(tc + 1) * 512], bank[mb][:]), reads=[R("bank", mb)], writes=[R("kT", nm)])

                def ev_v(t, mb, Vd=Vd, nm=nm):
                    P.op("dve", CP(Vd[:, t, :, 0:64], bank[mb][:, 0:128].rearrange("p (k d) -> p k d", k=2)),
                         reads=[R("bank", mb)], writes=[R("V", nm)])
                proj_feat(wi, 0, ev_k, "k")
                proj_tok(wi, 128, 128, ev_v)
                P.op("pool", MSET(Vd[:, :, :, 64], 1.0), writes=[R("V", nm)])
            wi = load_w256(W, [(0, 1792, 48)])

            def ev_g(t, mb):
                P.op("act", ACTF(gates[:, t, :], bank[mb][:, 0:48], AF.Sigmoid), reads=[R("bank", mb)], writes=[R("gates")])
            proj_tok(wi, 0, 48, ev_g)
            P.barrier(keep=[R("w1sb"), R("posT"), R("w2k", 0), R("w2k", 1), R("w2v")])
            P.op("pool", MSET(kcT2[:], 0.0), writes=[R("kcT2")])
            for hk in range(2):
                for a_ in range(2):
                    P.op("dve", TT(blk[:, :, 16 * a_:16 * a_ + 16], cmpT[:, hk, 16 * a_:16 * a_ + 2048].rearrange("p (c r) -> p c r", r=16),
                                   posT[:, 16 * a_:16 * a_ + 16].unsqueeze(1).to_broadcast([128, 128, 16]), ALU.add),
                         reads=[R("cmpT", hk), R("posT")], writes=[R("blk")])
                gb = [misc_bank(), misc_bank()]
                for l in range(32):
                    for which in range(2):
                        P.op("pe", MM(bank[gb[which]][:, 0:256], blk[64 * which:64 * which + 64, :, l],
                                      w1sb[64 * which:64 * which + 64, l, :], l == 0, l == 31),
                             reads=[R("blk"), R("w1sb")], writes=[R("bank", gb[which])])
                for which in range(2):
                    mb = gb[which]
                    P.op("act", ACP(xs_g, bank[mb][:, 0:256]), reads=[R("bank", mb)], writes=[R("xs")])
                    P.op("dve", TT(t1_g, xs_g, xs_g, ALU.mult), reads=[R("xs")], writes=[R("t1")])
                    P.op("dve", TS(t1_g, t1_g, 0.044715, 1.0, ALU.mult, ALU.add), reads=[R("t1")], writes=[R("t1")])
                    P.op("dve", TT(t1_g, t1_g, xs_g, ALU.mult), reads=[R("t1"), R("xs")], writes=[R("t1")])
                    P.op("act", ACTF(t1_g, t1_g, AF.Sigmoid, scale=1.5957691216057308), reads=[R("t1")], writes=[R("t1")])
                    P.op("dve", TT(gl_bf, xs_g, t1_g, ALU.mult), reads=[R("t1"), R("xs")], writes=[R("gl")])
                    bb_ = bank_bf(mb)
                    for ch in range(2):
                        P.op("pe", TR(bb_[:, ch * 128:(ch + 1) * 128], gl_bf[:, ch * 128:(ch + 1) * 128], J_bf[:]),
                             reads=[R("gl"), R("J_bf")], writes=[R("bank", mb)])
                    P.op("act", ACP(GT, bb_[:, 0:256]), reads=[R("bank", mb)], writes=[R("GT")])
                    if which == 0:
                        for ch in range(2):
                            P.op("pe", MM(bank[mb][:, 0:128], w2dup[:, ch, :], GT[:, ch * 128:(ch + 1) * 128], ch == 0, ch == 1),
                                 reads=[R("GT"), R("w2k", 0), R("w2k", 1)], writes=[R("bank", mb)])
                        P.op("dve", CP(kcT2[64 * hk:64 * hk + 64, hk, :], bank[mb][64 * hk:64 * hk + 64, 0:128]),
                             reads=[R("bank", mb)], writes=[R("kcT2")])
                    else:
                        for ch in range(2):
                            P.op("pe", MM(bank[mb][:, 0:64], GT[:, ch * 128:(ch + 1) * 128], w2v[:, ch, :], ch == 0, ch == 1),
                                 reads=[R("GT"), R("w2v")], writes=[R("bank", mb)])
                        P.op("dve", CP(Vc[:, hk, 0:64], bank[mb][:, 0:64]), reads=[R("bank", mb)], writes=[R("Vc")])
            P.op("pool", MSET(Vc[:, :, 64:65], 1.0), writes=[R("Vc")])
            for hk in range(2):
                P.op("pool", CP(Vc[:, hk, 65:97], ovrev[:]), reads=[R("ovrev")], writes=[R("Vc")])
            P.barrier()
            P.dma("pool", DMA(E_bf[0:32, :], I["c_E"]), msem("pool"), writes=[R("E")])
            P.op("pool", MSET(E_bf[32:64, :], 0.0), writes=[R("E")])
            P.op("pool", MSET(E_bf[64:128, :], 0.0), writes=[R("E")])
            P.op("pool", MSET(MBT[:], 0.0), writes=[R("MBT")])
            gview = gates[:].rearrange("p t (h b) -> p t h b", b=3)
            for hk in range(2):
                if hk == 0:
                    for i_, kTx in enumerate((kT1, kT2)):
                        P.dma("sp", DMA(ksave[i_], kTx[64:128, :]), msem("sp"), reads=[R("kT")], writes=[R("ksave")])
                    for kTx in (kT1, kT2):
                        P.op("pool", MSET(kTx[64:128, :], 0.0), writes=[R("kT")])
                else:
                    for kTx in (kT1, kT2):
                        P.op("pool", MSET(kTx[0:64, :], 0.0), writes=[R("kT")])
                    for i_, kTx in enumerate((kT1, kT2)):
                        P.dma("sp", DMA(kTx[64:128, :], ksave[i_]), msem("sp"), reads=[R("ksave")], writes=[R("kT")])
                tabc = gbc[:].rearrange("p (g j) -> p g j", g=8)
                for qb in range(NT):
                    def pre_c(hk=hk, qb=qb):
                        if qb < 8:
                            hb = ((Htmp, R("Htmp")), (tabX[:], R("tabX"))) if qb < 4 else ((Htmp, R("Htmp")),)
                            build_table(tabs[:, qb], hk, qb, False, hb, R("tab", qb))
                        if qb == 4:
                            P.op("dve", TT(tabX[:], tabs[:, 4], tri[:].unsqueeze(1).to_broadcast([128, 8, 128]), ALU.add),
                                 reads=[R("tab", 4), R("tri")], writes=[R("tabX")])
                        if hk == 0 and qb == 0:
                            load_wo(I["b_w_o"])
                        P.dma("sp", DMA(tabc, dap(Fd, (8 * hk) * FLEN + qb * 128, [[16, 128], [FLEN, 8], [1, 128]])), msem("sp"),
                              reads=[R("Fd")], writes=[R("gbc")])
                    ob_ = new_branch()
                    unit(hk, qb, kcT2[:, hk, :], 0, tabc, False, Vc[:, hk, :], 97, True, True, tabres=R("gbc"),
                         kres=R("kcT2"), vres=R("Vc"), pre=pre_c, post=(lambda hk=hk, qb=qb, ob_=ob_: epi_cmp(hk, qb, ob_)), ob=ob_)
                    kbs = [kb for kb in range(qb - 4, qb + 1) if kb >= 0]
                    for i, kb in enumerate(kbs):
                        off = qb - kb
                        tab = tabX[:] if off == 4 else tabs[:, off]
                        rt = R("tabX") if off == 4 else R("tab", off)
                        lastu = (i == len(kbs) - 1)
                        if i == 0:
                            ob_ = new_branch()
                        unit(hk, qb, kT2, kb * 128, tab, False, V2[:, kb, hk, :], 65, i == 0, lastu, tabres=rt,
                             post=(lambda hk=hk, qb=qb, ob_=ob_: branch_epilogue(hk, qb, 2, False, ob_)) if lastu else None, ob=ob_)
                    for kb in range(qb + 1):
                        off = qb - kb
                        em = kb if qb >= 8 else None
                        lastu = (kb == qb)
                        if kb == 0:
                            ob_ = new_branch()
                        po = (lambda hk=hk, qb=qb, ob_=ob_: (branch_epilogue(hk, qb, 1, True, ob_), defer(6, lambda: outproj_B(hk, qb)))) if lastu else None
                        if off <= 7:
                            unit(hk, qb, kT1, kb * 128, tabs[:, off], False, V1[:, kb, hk, :], 65, kb == 0, lastu, emask_kb=em, post=po, ob=ob_,
                                 tabres=R("tab", off))
                        else:
                            unit(hk, qb, kT1, kb * 128, None, True, V1[:, kb, hk, :], 65, kb == 0, lastu, emask_kb=em, post=po, ob=ob_)
                flush_pipe()
            P.barrier()

        def outproj_B(hk, qb):
            mb = misc_bank()
            bb_ = bank_bf(mb)
            for kc in range(4):
                P.op("pe", TR(bb_[:, kc * 128:(kc + 1) * 128], attn_bf1[:, kc * 128:(kc + 1) * 128], ident[:]),
                     reads=[R("attn_bf"), R("ident")], writes=[R("bank", mb)])
            P.op("act", ACP(attnT1, bb_[:, 0:512]), reads=[R("bank", mb)], writes=[R("attnT")])
            outproj(qb, attnT1, 4, 4 * hk, w_o)

        def epi_cmp(hk, qb, ob):
            gview = gates[:].rearrange("p t (h b) -> p t h b", b=3)
            P.op("dve", TS(g8(den[:]), O2(97, ob)[:, :, :, 64], 1e-30, None, ALU.max), reads=Ores(ob), writes=[R("den")])
            P.op("dve", lambda e: e.reciprocal(out=rd[:], in_=den[:]), reads=[R("den")], writes=[R("rd")])
            P.op("dve", TT(fac[:], rd[:], gview[:, qb, 8 * hk:8 * hk + 8, 0], ALU.mult), reads=[R("rd"), R("gates")], writes=[R("fac")])
            P.op("dve", TT(o_acc.rearrange("p (h g) d -> p h g d", h=2), O2(97, ob)[:, :, :, 0:64],
                           g8(fac[:]).unsqueeze(3).to_broadcast([128, 2, 4, 64]), ALU.mult),
                 reads=Ores(ob) + [R("fac")], writes=[R("o_acc")])
            if qb >= 8:
                P.op("dve", TT(impn[:].rearrange("p (h g) j -> p h g j", h=2), O2(97, ob)[:, :, :, 65:97],
                               g8(rd[:]).unsqueeze(3).to_broadcast([128, 2, 4, 32]), ALU.mult),
                     reads=Ores(ob) + [R("rd")], writes=[R("impn")])
                P.op("dve", lambda e: e.tensor_reduce(out=imp[:], in_=impn[:].rearrange("p g j -> p j g"), axis=AX.X, op=ALU.add),
                     reads=[R("impn")], writes=[R("imp")])
                P.op("dve", TT(sc[:], imp[:], selc[:, qb - 8, 0:32], ALU.mult), reads=[R("imp"), R("selc")], writes=[R("sc")])
                P.op("dve", TT(sc[:], sc[:], selc[:, qb - 8, 32:64], ALU.add), reads=[R("sc"), R("selc")], writes=[R("sc")])
                P.op("dve", lambda e: e.max(out=m8[:, 0:8], in_=sc[:]), reads=[R("sc")], writes=[R("m8")])
                P.op("dve", lambda e: e.match_replace(out=sc2[:], in_to_replace=m8[:, 0:8], in_values=sc[:], imm_value=NEG),
                     reads=[R("sc"), R("m8")], writes=[R("sc2")])
                P.op("dve", lambda e: e.max(out=m8[:, 8:16], in_=sc2[:]), reads=[R("sc2")], writes=[R("m8")])
                P.op("dve", TS(sc2[:], sc[:], m8[:, 15:16], None, ALU.is_ge), reads=[R("sc"), R("m8")], writes=[R("sc2")])
                P.op("dve", TS(mb_bf[:], sc2[:], -1.0, 30000.0, ALU.add, ALU.mult), reads=[R("sc2")], writes=[R("mb_bf")])
                def mbt_part():
                    mb = misc_bank()
                    bb_ = bank_bf(mb)
                    P.op("pe", TR(bb_[0:32, 0:128], mb_bf[:, 0:32], ident[:]), reads=[R("mb_bf"), R("ident")], writes=[R("bank", mb)])
                    for rr in range(4):
                        P.op("act", ACP(MBT[0:32, rr * 128:(rr + 1) * 128], bb_[0:32, 0:128]), reads=[R("bank", mb)], writes=[R("MBT")])
                defer(5, mbt_part)

        tmpm = P.sb("tmpm", [128, 8, 64], F32)

        def branch_epilogue(hk, qb, br, is_last, ob):
            gview = gates[:].rearrange("p t (h b) -> p t h b", b=3)
            P.op("dve", lambda e: e.reciprocal(out=g8(rd[:]), in_=O2(65, ob)[:, :, :, 64]), reads=Ores(ob), writes=[R("rd")])
            P.op("dve", TT(fac[:], rd[:], gview[:, qb, 8 * hk:8 * hk + 8, br], ALU.mult), reads=[R("rd"), R("gates")], writes=[R("fac")])
            P.op("dve", TT(tmpm[:].rearrange("p (h g) d -> p h g d", h=2), O2(65, ob)[:, :, :, 0:64],
                           g8(fac[:]).unsqueeze(3).to_broadcast([128, 2, 4, 64]), ALU.mult),
                 reads=Ores(ob) + [R("fac")], writes=[R("tmpm")])
            if is_last:
                P.op("pool", TT(attn_bf1.rearrange("p (g d) -> p g d", g=8), o_acc, tmpm[:], ALU.add),
                     reads=[R("o_acc"), R("tmpm")], writes=[R("attn_bf")])
            else:
                P.op("pool", TT(o_acc, o_acc, tmpm[:], ALU.add), reads=[R("o_acc"), R("tmpm")], writes=[R("o_acc")])

        nl = len(layer_ids)
        qsrc = {0: (I["a_w_qkv"], "qA"), 1: (I["b_w_in"], "qB")}
        for k_, li in enumerate(layer_ids):
            if k_ == 0:
                prefetch_q(*qsrc[li % 2])
                norm_phase(I["attn_norm"][li:li + 1, :])
                P.barrier(keep=wkeep())
            if li % 2 == 0:
                layer_A()
            else:
                layer_B()
            prefetch_ffn(li)
            norm_phase(I["ffn_norm"][li:li + 1, :])
            P.barrier(keep=wkeep())
            if k_ + 1 < nl:
                load_gain(I["attn_norm"][layer_ids[k_ + 1]:layer_ids[k_ + 1] + 1, :])
                ffn(li, fuse_norm="hnT")
                prefetch_q(*qsrc[layer_ids[k_ + 1] % 2])
            elif final:
                load_gain(I["final_norm"][0:1, :])
                ffn(li, fuse_norm="out")
            else:
                ffn(li)
            P.barrier(keep=wkeep())
        if not final:
            for t in range(NT):
                P.dma("sp", DMA(out[t * 128:(t + 1) * 128, :], h[:, t, :]), sem_out[t % 2], reads=[R("h", t)])
        P.barrier()
        stats = P.emit()
        stats["sbuf_left"] = nc.sbuf_bytes_remaining
    return nc, stats


FUSED = True


def _prep_shared(inputs):
    f = lambda a: np.ascontiguousarray(np.asarray(a, dtype=np.float32))
    sh = {
        "rel_table": f(inputs["rel_table"]), "attn_norm": f(inputs["attn_norm"]), "ffn_norm": f(inputs["ffn_norm"]),
        "final_norm": f(inputs["final_norm"]).reshape(1, D),
        "a_w_qkv": f(inputs["a_w_qkv"])[0], "a_b_qkv": f(inputs["a_b_qkv"]).reshape(1, 1280),
        "a_sinks": f(inputs["a_sinks"]).reshape(1, 16), "a_w_o": f(inputs["a_w_o"])[0],
        "a_b_o": f(inputs["a_b_o"]).reshape(1, D), "b_w_in": f(inputs["b_w_in"])[0],
        "b_cmp_pos": f(inputs["b_cmp_pos"])[0], "b_cmp_w1": f(inputs["b_cmp_w1"])[0],
        "b_cmp_w2": f(inputs["b_cmp_w2"])[0], "b_w_o": f(inputs["b_w_o"])[0],
        "ffn_w_gu": f(inputs["ffn_w_gu"]), "ffn_w_down": f(inputs["ffn_w_down"]),
    }
    sh.update(host_consts())
    return sh


def run_prog(layer_ids, final, xs, shared):
    nc, stats = build(layer_ids, final)
    in_maps = []
    for xb in xs:
        m = dict(shared)
        m["x"] = np.ascontiguousarray(xb, dtype=np.float32)
        in_maps.append(m)
    res = run_bass_kernel_spmd(nc, in_maps, core_ids=list(range(len(xs))))
    return [np.asarray(r["out"]) for r in res.results]


def kernel(**inputs):
    x = np.asarray(inputs["x"], dtype=np.float32)
    shared = _prep_shared(inputs)
    xs = [x[b] for b in range(x.shape[0])]
    if FUSED:
        outs = run_prog([0, 1], True, xs, shared)
    else:
        hs = run_prog([0], False, xs, shared)
        outs = run_prog([1], True, hs, shared)
    return np.stack(outs, axis=0).astype(np.float32)
```

```python
import math
from contextlib import ExitStack
import numpy as np
import concourse.bass as bass
import concourse.mybir as mybir
from concourse.bass_utils import run_bass_kernel_spmd

F32 = mybir.dt.float32
BF16 = mybir.dt.bfloat16
AF = mybir.ActivationFunctionType
ALU = mybir.AluOpType
AX = mybir.AxisListType

S = 2048
D = 1024
NT = 16
KC = 8
DFF = 2816
FOFF = 2063
FLEN = 4608
NEG = -1e30


class Res:
    __slots__ = ("name", "w", "r")

    def __init__(self, name=""):
        self.name = name
        self.w = None
        self.r = {}


class DmaSem:
    def __init__(self, handle, name):
        self.handle = handle
        self.name = name
        self.count = 0
        self.snaps = {}


class Queue:
    def __init__(self, name, eng, sem):
        self.name = name
        self.eng = eng
        self.sem = sem
        self.ops = []
        self.seen = {}


class Prog:
    def __init__(self, nc, stack):
        self.nc = nc
        self.stack = stack
        self.q = {}
        for name, eng in (("pe", nc.tensor), ("act", nc.scalar), ("dve", nc.vector),
                          ("pool", nc.gpsimd), ("sp", nc.sync)):
            sem = stack.enter_context(nc.semaphore("s_" + name))
            self.q[name] = Queue(name, eng, sem)
        self.n_dsem = 0
        self.pending = []
        self.rcache = {}

    def R(self, *key):
        r = self.rcache.get(key)
        if r is None:
            r = Res(str(key))
            self.rcache[key] = r
        return r

    def dsem(self, name=None):
        self.n_dsem += 1
        h = self.stack.enter_context(self.nc.semaphore("d_%d" % self.n_dsem))
        return DmaSem(h, name or "d%d" % self.n_dsem)

    def sb(self, name, shape, dtype):
        return self.stack.enter_context(self.nc.sbuf_tensor(name, list(shape), dtype))

    def ps(self, name, shape, dtype):
        return self.stack.enter_context(self.nc.psum_tensor(name, list(shape), dtype))

    def _need(self, q, toks):
        waits = []
        for (key, idx) in toks:
            if key is q and q.name == "pe":
                continue
            if q.seen.get(key, -1) >= idx:
                continue
            waits.append((key, idx))
            if isinstance(key, Queue):
                key.ops[idx]["mark"] = True
                snap = key.ops[idx]["know"]
            else:
                snap = key.snaps.get(idx, {})
            for k2, v2 in snap.items():
                if q.seen.get(k2, -1) < v2:
                    q.seen[k2] = v2
            if q.seen.get(key, -1) < idx:
                q.seen[key] = idx
        return waits

    def _deps(self, q, reads, writes):
        toks = []
        for r in reads:
            if r.w is not None:
                toks.append(r.w)
        for w in writes:
            if w.w is not None:
                toks.append(w.w)
            toks.extend(w.r.values())
        return self._need(q, toks)

    def op(self, qname, fn, reads=(), writes=()):
        q = self.q[qname]
        waits = self._deps(q, reads, writes)
        idx = len(q.ops)
        know = dict(q.seen)
        know[q] = idx
        q.ops.append({"fn": fn, "waits": waits, "mark": False, "dsem": None, "know": know})
        tok = (q, idx)
        for r in reads:
            r.r[q] = tok
        for w in writes:
            w.w = tok
            w.r = {}
        return tok

    def dma(self, qname, fn, sem, reads=(), writes=(), cont=False):
        q = self.q[qname]
        toks = []
        for r in reads:
            if r.w is not None:
                toks.append(r.w)
        for w in writes:
            if w.w is not None:
                toks.append(w.w)
            toks.extend(w.r.values())
        toks = [t for t in toks if t[0] is not sem]
        if not cont and sem.count > 0:
            toks.append((sem, sem.count))
        waits = self._need(q, toks)
        q.ops.append({"fn": fn, "waits": waits, "mark": False, "dsem": sem, "know": None})
        sem.count += 16
        tok = (sem, sem.count)
        snap = dict(q.seen)
        snap[sem] = sem.count
        sem.snaps[sem.count] = snap
        for r in reads:
            r.r[sem] = tok
        for w in writes:
            w.w = tok
            w.r = {}
        self.pending.append(tok)
        return tok

    def wait_tok(self, qname, toks):
        q = self.q[qname]
        waits = self._need(q, toks)
        if waits:
            q.ops.append({"fn": None, "waits": waits, "mark": False, "dsem": None, "know": None})

    def barrier(self, keep=()):
        keep_toks = set()
        for r in keep:
            if r.w is not None:
                keep_toks.add(r.w)
        last = []
        for q in self.q.values():
            for i in range(len(q.ops) - 1, -1, -1):
                if q.ops[i]["fn"] is not None and q.ops[i]["dsem"] is None:
                    last.append((q, i))
                    break
        kept_sem = {}
        for (key, val) in keep_toks:
            if isinstance(key, DmaSem):
                kept_sem[key] = max(kept_sem.get(key, 0), val)
        dl = {}
        still = []
        for (sem, val) in self.pending:
            if sem in kept_sem and val <= kept_sem[sem]:
                still.append((sem, val))
                continue
            dl[sem] = max(dl.get(sem, 0), val)
        dtoks = list(dl.items())
        self.pending = still
        for qn in self.q:
            self.wait_tok(qn, last + dtoks)
        kept = {k: v for k, v in self.rcache.items() if v in keep}
        self.rcache = kept

    def emit(self):
        for q in self.q.values():
            c = 0
            for o in q.ops:
                if o["mark"]:
                    c += 1
                o["cnt"] = c
        stats = {}
        with self.nc.Block() as block:
            def run(q):
                def body(eng):
                    nw = 0
                    for o in q.ops:
                        waits = o["waits"]
                        attach = None
                        if o["fn"] is not None and o["dsem"] is None and waits:
                            attach = waits[-1]
                            waits = waits[:-1]
                        for (key, idx) in waits:
                            if isinstance(key, Queue):
                                eng.wait_ge(key.sem, key.ops[idx]["cnt"])
                            else:
                                eng.wait_ge(key.handle, idx)
                            nw += 1
                        if o["fn"] is None:
                            continue
                        ins = o["fn"](eng)
                        if attach is not None:
                            key, idx = attach
                            if isinstance(key, Queue):
                                ins._wait_ge(key.sem, key.ops[idx]["cnt"])
                            else:
                                ins._wait_ge(key.handle, idx)
                        if o["dsem"] is not None:
                            ins.then_inc(o["dsem"].handle, 16)
                        elif o["mark"]:
                            ins.then_inc(q.sem, 1)
                    stats[q.name] = (len(q.ops), nw)
                return body
            block.tensor(run(self.q["pe"]))
            block.scalar(run(self.q["act"]))
            block.vector(run(self.q["dve"]))
            block.gpsimd(run(self.q["pool"]))
            block.sync(run(self.q["sp"]))
        return stats


def MM(out, lhsT, rhs, start, stop):
    return lambda e: e.matmul(out, lhsT=lhsT, rhs=rhs, start=start, stop=stop, skip_group_check=True)


def TR(out, in_, ident):
    return lambda e: e.transpose(out=out, in_=in_, identity=ident)


def ACTF(out, in_, func, **kw):
    return lambda e: e.activation(out=out, in_=in_, func=func, **kw)


def TT(out, in0, in1, op):
    return lambda e: e.tensor_tensor(out=out, in0=in0, in1=in1, op=op)


def TS(out, in0, s1, s2, op0, op1=None):
    if op1 is None:
        return lambda e: e.tensor_scalar(out=out, in0=in0, scalar1=s1, scalar2=None, op0=op0)
    return lambda e: e.tensor_scalar(out=out, in0=in0, scalar1=s1, scalar2=s2, op0=op0, op1=op1)


def STT(out, in0, scalar, in1, op0, op1):
    return lambda e: e.scalar_tensor_tensor(out=out, in0=in0, scalar=scalar, in1=in1, op0=op0, op1=op1)


def CP(out, in_):
    return lambda e: e.tensor_copy(out=out, in_=in_)


def ACP(out, in_):
    return lambda e: e.copy(out=out, in_=in_)


def DMA(out, in_, slow=False):
    if slow:
        return lambda e: e.dma_start(out=out, in_=in_, allow_slow_non_contiguous=True)
    return lambda e: e.dma_start(out=out, in_=in_)


def MSET(ap, v):
    return lambda e: e.memset(ap, v)


def _t5_bucket_np(dist):
    n = np.maximum(dist, 0)
    nf = np.maximum(n, 1).astype(np.float32)
    v = (np.log(nf / np.float32(16)) / np.float32(math.log(1024 / 16)) * np.float32(16)).astype(np.float32)
    log_b = 16 + v.astype(np.int32)
    return np.where(n < 16, n, np.minimum(log_b, 31))


def host_consts():
    c = {}
    c["c_ident"] = np.eye(128, dtype=np.float32)
    c["c_J"] = np.ascontiguousarray(np.eye(128, dtype=np.float32)[::-1])
    oh = np.zeros((33, FLEN), np.float32)
    idx = np.arange(FLEN)
    d = idx - FOFF
    valid = (d >= 0) & (d < S)
    b = _t5_bucket_np(np.where(valid, d, 0))
    oh[b[valid], idx[valid]] = 1.0
    oh[32, idx[~valid]] = 1.0
    c["c_onehot"] = oh
    E = np.zeros((32, S), np.float32)
    E[np.arange(S) // 64, np.arange(S)] = 1.0
    c["c_E"] = E
    p = np.arange(128)[:, None]
    j = np.arange(128)[None, :]
    c["c_tri"] = np.where(j >= p, np.float32(NEG), np.float32(0)).astype(np.float32)
    selc = np.zeros((128, 8, 64), np.float32)
    for qi in range(8):
        qb = 8 + qi
        t = qb * 128 + np.arange(128)
        cur = t // 64
        for jb in range(32):
            f0 = (jb == 0)
            f1 = (cur - jb == 1)
            f2 = (cur - jb == 0)
            forced = f0 | f1 | f2
            keep = (~forced) & (jb <= cur)
            fn = np.where(f2, 3e6, np.where(f1, 2e6, np.where(f0, 1e6, np.where(jb > cur, -1.0, 0.0))))
            selc[:, qi, jb] = keep.astype(np.float32)
            selc[:, qi, 32 + jb] = fn
    c["c_selc"] = selc
    ov = np.zeros((128, 32), np.float32)
    for pp in range(128):
        cc = 127 - pp
        if cc > 126:
            continue
        for jb in range(32):
            if 4 * jb - 1 <= cc <= 4 * jb + 3:
                ov[pp, jb] = 1.0
    c["c_ovrev"] = ov
    return c


INPUT_SHAPES = {
    "x": [S, D], "rel_table": [32, 16], "attn_norm": [2, D], "ffn_norm": [2, D], "final_norm": [1, D],
    "a_w_qkv": [D, 1280], "a_b_qkv": [1, 1280], "a_sinks": [1, 16], "a_w_o": [D, D], "a_b_o": [1, D],
    "b_w_in": [D, 1840], "b_cmp_pos": [2, 32, 64], "b_cmp_w1": [2, 2048, 256], "b_cmp_w2": [2, 256, 64],
    "b_w_o": [D, D], "ffn_w_gu": [2, D, 2 * DFF], "ffn_w_down": [2, DFF, D],
    "c_ident": [128, 128], "c_J": [128, 128], "c_onehot": [33, FLEN], "c_E": [32, S], "c_tri": [128, 128],
    "c_selc": [128, 8, 64], "c_ovrev": [128, 32],
}


def build(layer_ids, final):
    nc = bass.Bass("TRN2", target_bir_lowering=False)
    I = {k: nc.dram_tensor(k, v, F32, kind="ExternalInput").ap() for k, v in INPUT_SHAPES.items()}
    out = nc.dram_tensor("out", [S, D], F32, kind="ExternalOutput").ap()
    Fd = nc.dram_tensor("Fd", [16, FLEN], F32, kind="Internal").ap()
    ksave = nc.dram_tensor("ksave", [2, 64, S], BF16, kind="Internal").ap()

    def dap(ap, off, pat):
        return bass.AP(ap.tensor, off, pat)

    with ExitStack() as st:
        P = Prog(nc, st)
        R = P.R
        h = P.sb("h", [128, NT, D], F32)
        arenaA = P.sb("arenaA", [128, 16384], BF16)
        arenaB = P.sb("arenaB", [128, 43008], BF16)
        wbuf = [P.sb("wbuf%d" % i, [128, KC, 256], BF16) for i in range(2)]
        tabX = P.sb("tabX", [128, 8, 128], F32)
        gbc = P.sb("gbc", [128, D], F32)
        selc = P.sb("selc", [128, 8, 64], F32)
        ident = P.sb("ident", [128, 128], BF16)
        J_bf = P.sb("J_bf", [128, 128], BF16)
        J32 = P.sb("J32", [128, 128], F32)
        tri = P.sb("tri", [128, 128], F32)
        ovrev = P.sb("ovrev", [128, 32], BF16)
        relaug = P.sb("relaug", [33, 16], F32)
        ssq = P.sb("ssq", [128, NT], F32)
        rstd = P.sb("rstd", [128, NT], F32)
        bq = P.sb("bq", [128, 8], F32)
        bk = P.sb("bk", [128, 1], F32)
        bv_bc = P.sb("bv_bc", [128, 128], F32)
        esink = P.sb("esink", [128, 16], F32)
        b31 = P.sb("b31", [128, 16], F32)
        den = P.sb("den", [128, 8], F32)
        rd = P.sb("rd", [128, 8], F32)
        fac = P.sb("fac", [128, 8], F32)
        impn = P.sb("impn", [128, 8, 32], F32)
        imp = P.sb("imp", [128, 32], F32)
        sc = P.sb("sc", [128, 32], F32)
        sc2 = P.sb("sc2", [128, 32], F32)
        m8 = P.sb("m8", [128, 16], F32)
        mb_bf = P.sb("mb_bf", [128, 32], BF16)
        MBT = P.sb("MBT", [128, 512], BF16)
        Vc = P.sb("Vc", [128, 2, 97], BF16)
        kcT2 = P.sb("kcT2", [128, 2, 128], BF16)
        posT = P.sb("posT", [128, 32], F32)
        w2dup = P.sb("w2dup", [128, 2, 128], BF16)
        w2v = P.sb("w2v", [128, 2, 64], BF16)

        bank = [P.ps("bank%d" % i, [128, 512], F32) for i in range(4)]
        Obuf = [P.ps("Obuf%d" % i, [128, 1024], F32) for i in range(2)]
        for i_ in range(2):
            bank += [Obuf[i_][:, 0:512], Obuf[i_][:, 512:1024]]

        def bank_bf(i):
            return bank[i][:].bitcast(BF16)

        misc_rr = [0]

        att_mode = [False]
        srr = [0]

        def next_sbank():
            srr[0] = (srr[0] + 1) % 4
            return srr[0]

        def misc_bank():
            if att_mode[0]:
                return next_sbank()
            misc_rr[0] ^= 1
            return 6 + misc_rr[0]

        obase = [4]

        def new_branch():
            obase[0] = 10 - obase[0]
            return obase[0]

        def carve(arena, boff, nbytes, dtype, pattern=None, **kw):
            v = arena[:, boff // 2:(boff + nbytes) // 2]
            if dtype is F32:
                v = v.bitcast(F32)
            if pattern:
                v = v.rearrange(pattern, **kw)
            return v

        hnT = carve(arenaA, 0, 32768, BF16, "p (k t) -> p k t", k=KC)
        tabs = carve(arenaA, 0, 32768, F32, "p (s g j) -> p s g j", s=8, g=8)
        blk = carve(arenaA, 0, 8192, BF16, "p (c l) -> p c l", c=128)
        xs_g = carve(arenaA, 8192, 1024, F32)
        t1_g = carve(arenaA, 9216, 1024, F32)
        gl_bf = carve(arenaA, 10240, 512, BF16)
        GT = carve(arenaA, 10752, 512, BF16)

        qTp = carve(arenaB, 0, 32768, BF16, "p (g t) -> p g t", g=8)
        kT1 = carve(arenaB, 32768, 4096, BF16)
        kT2 = carve(arenaB, 36864, 4096, BF16)
        V1 = carve(arenaB, 40960, 4160, BF16, "p (t k c) -> p t k c", t=NT, k=2)
        V2 = carve(arenaB, 45120, 4160, BF16, "p (t k c) -> p t k c", t=NT, k=2)
        gates = carve(arenaB, 49280, 3072, F32, "p (t c) -> p t c", t=NT)
        w_o = carve(arenaB, 52352, 16384, BF16, "p (k n) -> p k n", k=KC)
        w1sb = carve(arenaB, 52352, 16384, BF16, "p (l m) -> p l m", l=32)
        WK = 68736
        E_bf = carve(arenaB, WK + 4096, 4096, BF16)
        PT = [carve(arenaB, WK + 8192 + 1024 * i, 1024, BF16) for i in range(4)]
        o_acc = carve(arenaB, WK + 12288, 2048, F32, "p (g d) -> p g d", g=8)
        attn_bf = carve(arenaB, WK + 12288, 2048, BF16)
        attn_bf1 = carve(arenaB, WK + 14336, 1024, BF16)
        attnT = carve(arenaB, WK + 14336, 2048, BF16)
        attnT1 = carve(arenaB, WK + 15360, 1024, BF16)
        Htmp = carve(arenaB, WK, 4096, F32, "p (g j) -> p g j", g=8)
        cmpT = carve(arenaB, WK, 16512, F32, "p (k t) -> p k t", k=2)
        hn_bf = [carve(arenaB, 2048 * i, 2048, BF16) for i in range(2)]
        junk = carve(arenaB, 4096, 2048, BF16)
        ostage = [carve(arenaB, 8192 + 4096 * i, 4096, F32) for i in range(2)]
        hn_bf2 = [carve(arenaB, 71680 + 2048 * i, 2048, BF16) for i in range(2)]
        junk2 = carve(arenaB, 75776, 2048, BF16)
        ostage2 = [carve(arenaB, 77824 + 4096 * i, 4096, F32) for i in range(2)]
        oh_sb = carve(arenaB, 0, 18432, F32)
        Fsb = carve(arenaB, 18432, 18432, F32)
        actT = carve(arenaB, 0, 45056, BF16, "p (f t) -> p f t", f=11)
        wd = carve(arenaB, 45056, 22528, BF16, "p (f n) -> p f n", f=11)
        sg = [carve(arenaB, 67584 + 2048 * i, 2048, F32) for i in range(2)]

        sem_w = [P.dsem("w0"), P.dsem("w1")]
        sem_misc = {"sp": [P.dsem("ms%d" % i) for i in range(6)], "pool": [P.dsem("mp%d" % i) for i in range(4)]}
        mrr = {"sp": 0, "pool": 0}

        def msem(qn):
            mrr[qn] = (mrr[qn] + 1) % len(sem_misc[qn])
            return sem_misc[qn][mrr[qn]]

        wrr = [0]

        def next_wbuf():
            i = wrr[0]
            wrr[0] ^= 1
            return i

        sem_x = [P.dsem("x%d" % i) for i in range(4)]
        xv = I["x"].rearrange("(t p) d -> p t d", p=128)
        for gi in range(4):
            for t in range(4 * gi, 4 * gi + 4):
                P.dma("sp", DMA(h[:, t, :], xv[:, t, :]), sem_x[gi], writes=[R("h", t)], cont=(t % 4 != 0))
            for t in range(4 * gi, 4 * gi + 4):
                R("h", t).w = (sem_x[gi], sem_x[gi].count)
        P.dma("pool", DMA(ident[:], I["c_ident"]), msem("pool"), writes=[R("ident")])
        P.dma("pool", DMA(J_bf[:], I["c_J"]), msem("pool"), writes=[R("J_bf")])
        P.dma("sp", DMA(J32[:], I["c_J"]), msem("sp"), writes=[R("J32")])
        P.dma("sp", DMA(tri[:], I["c_tri"]), msem("sp"), writes=[R("tri")])
        P.dma("sp", DMA(selc[:], I["c_selc"]), msem("sp"), writes=[R("selc")])
        P.dma("pool", DMA(ovrev[:], I["c_ovrev"]), msem("pool"), writes=[R("ovrev")])
        P.op("pool", MSET(relaug[32:33, :], NEG), writes=[R("relaug")])
        P.dma("sp", DMA(relaug[0:32, :], I["rel_table"]), msem("sp"), writes=[R("relaug")])
        P.dma("sp", DMA(oh_sb[0:33, :], I["c_onehot"]), msem("sp"), writes=[R("oh")])
        P.dma("sp", DMA(b31[:], dap(I["rel_table"], 31 * 16, [[0, 128], [1, 16]])), msem("sp"), writes=[R("b31")])
        for c in range(FLEN // 512):
            bi = c % 2
            P.op("pe", MM(bank[bi][0:16, :], relaug[0:33, 0:16], oh_sb[0:33, c * 512:(c + 1) * 512], True, True),
                 reads=[R("relaug"), R("oh")], writes=[R("bank", bi)])
            P.op("act", ACP(Fsb[0:16, c * 512:(c + 1) * 512], bank[bi][0:16, :]), reads=[R("bank", bi)], writes=[R("Fsb")])
        sem_F = P.dsem("F")
        P.dma("sp", DMA(Fd, Fsb[0:16, :]), sem_F, reads=[R("Fsb")], writes=[R("Fd")])
        keepers = lambda: [R("ident"), R("J_bf"), R("J32"), R("tri"), R("E"), R("selc"), R("ovrev"), R("b31"), R("Fd")]
        P.barrier()
        for t in range(NT):
            R("h", t)

        def load_gain(gain_ap_row):
            P.dma("sp", DMA(gbc[:], dap(gain_ap_row, gain_ap_row.offset, [[0, 128], [1, D]])), msem("sp"), writes=[R("gbc")])

        def norm_tiles(tiles, to_out, tmp):
            junk_, hn_, os_ = tmp
            t0_, t1_ = tiles[0], tiles[-1] + 1
            for i in tiles:
                P.op("act", ACTF(junk_, h[:, i, :], AF.Square, accum_out=ssq[:, i:i + 1]),
                     reads=[R("h", i)], writes=[R("ssq", i), R("junk")])
            sr = [R("ssq", i) for i in tiles]
            P.op("dve", TS(rstd[:, t0_:t1_], ssq[:, t0_:t1_], 1.0 / D, 1e-6, ALU.mult, ALU.add), reads=sr, writes=[R("rstd", t0_)])
            P.op("act", ACTF(rstd[:, t0_:t1_], rstd[:, t0_:t1_], AF.Sqrt), reads=[R("rstd", t0_)], writes=[R("rstd", t0_)])
            P.op("dve", lambda e: e.reciprocal(out=rstd[:, t0_:t1_], in_=rstd[:, t0_:t1_]), reads=[R("rstd", t0_)], writes=[R("rstd", t0_)])
            for i in tiles:
                sl = i % 2
                if to_out:
                    P.op("dve", STT(os_[sl], h[:, i, :], rstd[:, i:i + 1], gbc[:], ALU.mult, ALU.mult),
                         reads=[R("h", i), R("rstd", t0_), R("gbc")], writes=[R("ostage", sl)])
                    P.dma("sp", DMA(out[i * 128:(i + 1) * 128, :], os_[sl]), sem_out[sl], reads=[R("ostage", sl)])
                    continue
                P.op("dve", STT(hn_[sl], h[:, i, :], rstd[:, i:i + 1], gbc[:], ALU.mult, ALU.mult),
                     reads=[R("h", i), R("rstd", t0_), R("gbc")], writes=[R("hn_bf", sl)])
                mb = misc_bank()
                bb = bank_bf(mb)
                for kc in range(KC):
                    P.op("pe", TR(bb[:, kc * 128:(kc + 1) * 128], hn_[sl][:, kc * 128:(kc + 1) * 128], ident[:]),
                         reads=[R("hn_bf", sl), R("ident")], writes=[R("bank", mb)])
                P.op("act", ACP(hnT[:, :, i * 128:(i + 1) * 128], bb.rearrange("p (k t) -> p k t", k=KC)),
                     reads=[R("bank", mb)], writes=[R("hnT", i)])

        def norm_phase(gain_ap_row, to_out=False):
            load_gain(gain_ap_row)
            norm_tiles(list(range(NT)), to_out, (junk, hn_bf, ostage))

        def load_w256(src2d, col_specs):
            wi = next_wbuf()
            srcv = src2d.rearrange("(k p) n -> p k n", p=128)
            for ci, (d0, s0, n) in enumerate(col_specs):
                P.dma("pool", DMA(wbuf[wi][:, :, d0:d0 + n], srcv[:, :, s0:s0 + n]), sem_w[wi], writes=[R("wbuf", wi)], cont=(ci > 0))
            return wi

        pre_w = {}

        def load_qpair_chunk(src2d, c, key=None):
            if key is not None and key in pre_w:
                return pre_w.pop(key)
            wi = next_wbuf()
            srcv = src2d.rearrange("(k p) n -> p k n", p=128)
            dst5 = wbuf[wi][:].rearrange("p k (pl two d) -> p k pl two d", pl=2, two=2)
            for two in range(2):
                s = srcv[:, :, two * 512 + 128 * c: two * 512 + 128 * c + 128].rearrange("p k (pl d) -> p k pl d", pl=2)
                for kc in range(KC):
                    P.dma("pool", DMA(dst5[:, kc, :, two, :], s[:, kc, :, :]), sem_w[wi], writes=[R("wbuf", wi)], cont=not (two == 0 and kc == 0))
            return wi

        def hn_reads(tc):
            return [R("hnT", t) for t in range(4 * tc, 4 * tc + 4)]

        def proj_feat(wi, c0, evac, tag):
            for tc in range(4):
                mb = misc_bank()
                for kc in range(KC):
                    P.op("pe", MM(bank[mb][:], wbuf[wi][:, kc, c0:c0 + 128], hnT[:, kc, tc * 512:(tc + 1) * 512],
                                  kc == 0, kc == KC - 1),
                         reads=[R("wbuf", wi)] + hn_reads(tc), writes=[R("bank", mb)])
                evac(tc, mb)

        def proj_tok(wi, c0, n, evac):
            for t in range(NT):
                mb = misc_bank()
                for kc in range(KC):
                    P.op("pe", MM(bank[mb][:, 0:n], hnT[:, kc, t * 128:(t + 1) * 128], wbuf[wi][:, kc, c0:c0 + n],
                                  kc == 0, kc == KC - 1),
                         reads=[R("wbuf", wi), R("hnT", t)], writes=[R("bank", mb)])
                evac(t, mb)

        def prefetch_q(src2d, name):
            for c in range(2):
                pre_w[(name, c)] = load_qpair_chunk(src2d, c)

        def q_proj(src2d, has_bias, name):
            for c in range(4):
                wi = load_qpair_chunk(src2d, c, key=(name, c))
                for pl in range(2):
                    g = 2 * c + pl

                    def ev(tc, mb, g=g):
                        if has_bias:
                            P.op("dve", TS(qTp[:, g, tc * 512:(tc + 1) * 512], bank[mb][:], bq[:, g:g + 1], 0.125, ALU.add, ALU.mult),
                                 reads=[R("bank", mb), R("bq")], writes=[R("qTp", g, tc)])
                        else:
                            P.op("dve", TS(qTp[:, g, tc * 512:(tc + 1) * 512], bank[mb][:], 0.125, None, ALU.mult),
                                 reads=[R("bank", mb)], writes=[R("qTp", g, tc)])
                    proj_feat(wi, pl * 128, ev, "q")

        pipe = []

        def unit(hk, qb, kT, kcol0, tab4, const_bias, Vrhs, ncols, first, last, emask_kb=None, tabres=None,
                 kres=None, vres=None, pre=None, post=None, ob=4):
            pipe.append(dict(hk=hk, qb=qb, kT=kT, kcol0=kcol0, tab4=tab4, const_bias=const_bias, Vrhs=Vrhs, ncols=ncols,
                             first=first, last=last, emask_kb=emask_kb, tabres=tabres, kres=kres, vres=vres, pre=pre, post=post, ob=ob))

        SKEW = 3

        def emit_S(u, half, sl):
            hk, qb = u["hk"], u["qb"]
            if half == 0 and u["pre"] is not None:
                u["pre"]()
            qrd = [R("qTp", g, qb // 4) for g in range(4 * half, 4 * half + 4)]
            bi = next_sbank()
            rhs = qTp[:, 4 * half:4 * half + 4, qb * 128:(qb + 1) * 128]
            P.op("pe", MM(bank[bi][:], u["kT"][:, u["kcol0"]:u["kcol0"] + 128], rhs, True, u["emask_kb"] is None),
                 reads=[u["kres"] or R("kT")] + qrd, writes=[R("bank", bi)])
            if u["emask_kb"] is not None:
                ek = u["emask_kb"]
                P.op("pe", MM(bank[bi][:], E_bf[:, ek * 128:(ek + 1) * 128], MBT[:], False, True),
                     reads=[R("E"), R("MBT")], writes=[R("bank", bi)])
            bv3 = bank[bi][:].rearrange("p (g j) -> p g j", g=4)
            if u["const_bias"]:
                in1 = b31[:, 8 * hk + 4 * half: 8 * hk + 4 * half + 4].unsqueeze(2).to_broadcast([128, 4, 128])
                P.op("dve", TT(bv3, bv3, in1, ALU.add), reads=[R("bank", bi), R("b31")], writes=[R("bank", bi)])
            else:
                P.op("dve", TT(bv3, bv3, u["tab4"][:, 4 * half:4 * half + 4, :], ALU.add),
                     reads=[R("bank", bi), u["tabres"] or R("tab")], writes=[R("bank", bi)])
            P.op("act", ACTF(PT[sl], bank[bi][:], AF.Exp), reads=[R("bank", bi)], writes=[R("PT", sl)])

        def emit_PV(u, half, sl):
            ncols = u["ncols"]
            ob = u["ob"] + half
            for g4 in range(4):
                oap = bank[ob][:, g4 * ncols:(g4 + 1) * ncols]
                P.op("pe", MM(oap, PT[sl][:, g4 * 128:(g4 + 1) * 128], u["Vrhs"], u["first"] and (g4 == 0), u["last"]),
                     reads=[R("PT", sl), u["vres"] or R("V")], writes=[R("bank", ob)])
            if half == 1 and u["post"] is not None:
                u["post"]()

        deferred = []
        cur_i = [0]

        def defer(k, fn):
            deferred.append((cur_i[0] + k, fn))

        def run_deferred(upto):
            keep_ = []
            for (due, fn) in list(deferred):
                if due <= upto:
                    deferred.remove((due, fn))
                    fn()
            return

        def flush_pipe():
            att_mode[0] = True
            hu = [(u, half) for u in pipe for half in range(2)]
            n = len(hu)
            for i in range(n + SKEW):
                cur_i[0] = i
                run_deferred(i)
                if i < n:
                    emit_S(hu[i][0], hu[i][1], i % 4)
                if i >= SKEW:
                    j = i - SKEW
                    emit_PV(hu[j][0], hu[j][1], j % 4)
            while deferred:
                cur_i[0] += 1
                run_deferred(cur_i[0])
            del pipe[:]
            att_mode[0] = False

        def O2(ncols, ob):
            return Obuf[(ob - 4) // 2][:].rearrange("p (h c) -> p h c", h=2)[:, :, 0:4 * ncols].rearrange("p h (g c) -> p h g c", g=4)

        def Ores(ob):
            return [R("bank", ob), R("bank", ob + 1)]

        def g8(ap):
            return ap.rearrange("p (h g) -> p h g", h=2)

        def Oview(b, ncols, ob):
            return bank[ob + b][:, 0:4 * ncols].rearrange("p (g c) -> p g c", g=4)

        hrr = [0]

        def build_table(dst, hk, off, add_tri, hbufs, tres):
            base = (8 * hk) * FLEN + FOFF + off * 128 - 127
            hrr[0] = (hrr[0] + 1) % len(hbufs)
            Htmp, hres = hbufs[hrr[0]]
            P.dma("sp", DMA(Htmp, dap(Fd, base, [[1, 128], [FLEN, 8], [1, 128]])), msem("sp"),
                  reads=[R("Fd")], writes=[hres])
            for half in range(2):
                mb = misc_bank()
                P.op("pe", MM(bank[mb][:], J32[:], Htmp[:, 4 * half:4 * half + 4, :].rearrange("p g j -> p (g j)"), True, True),
                     reads=[R("J32"), hres], writes=[R("bank", mb)])
                dv = dst[:, 4 * half:4 * half + 4, :]
                if add_tri:
                    P.op("dve", TT(dv, bank[mb][:].rearrange("p (g j) -> p g j", g=4),
                                   tri[:].unsqueeze(1).to_broadcast([128, 4, 128]), ALU.add),
                         reads=[R("bank", mb), R("tri")], writes=[tres])
                else:
                    P.op("act", ACP(dv, bank[mb][:].rearrange("p (g j) -> p g j", g=4)),
                         reads=[R("bank", mb)], writes=[tres])

        def outproj(qb, aT, nk, kc0, wtile):
            for nh in range(2):
                mb = misc_bank()
                for kc in range(nk):
                    P.op("pe", MM(bank[mb][:], aT[:, kc * 128:(kc + 1) * 128], wtile[:, kc0 + kc, nh * 512:(nh + 1) * 512],
                                  kc == 0, kc == nk - 1),
                         reads=[R("attnT"), R("w_o")], writes=[R("bank", mb)])
                P.op("dve", TT(h[:, qb, nh * 512:(nh + 1) * 512], bank[mb][:], h[:, qb, nh * 512:(nh + 1) * 512], ALU.add),
                     reads=[R("bank", mb), R("h", qb)], writes=[R("h", qb)])

        def load_wo(src2d):
            srcv = src2d.rearrange("(k p) n -> p k n", p=128)
            for kc in range(KC):
                P.dma("pool", DMA(w_o[:, kc, :], srcv[:, kc, :]), sem_wo, writes=[R("w_o")], cont=(kc > 0))

        sem_wo = P.dsem("wo")
        sem_wd = P.dsem("wd")
        sem_out = [P.dsem("o0"), P.dsem("o1")]

        def prefetch_ffn(layer):
            wgu = I["ffn_w_gu"][layer]
            wdv = I["ffn_w_down"][layer].rearrange("(f p) n -> p f n", p=128)
            for fc in range(2):
                pre_w[("gu", layer, fc)] = load_w256(wgu, [(0, 128 * fc, 128), (128, DFF + 128 * fc, 128)])
            for fl in range(11):
                P.dma("pool", DMA(wd[:, fl, :], wdv[:, fl, :]), sem_wd, writes=[R("wd")], cont=(fl > 0))
            pre_w[("wd", layer)] = True

        def wkeep():
            return [R("wbuf", 0), R("wbuf", 1), R("wd")]

        def ffn(layer, fuse_norm=None):
            wgu = I["ffn_w_gu"][layer]
            wdn = I["ffn_w_down"][layer]
            wdv = wdn.rearrange("(f p) n -> p f n", p=128)
            sgr = [0]
            for half in range(2):
                if not (half == 0 and ("wd", layer) in pre_w):
                    for fl in range(11):
                        P.dma("pool", DMA(wd[:, fl, :], wdv[:, 11 * half + fl, :]), sem_wd, writes=[R("wd")], cont=(fl > 0))
                else:
                    pre_w.pop(("wd", layer))
                for fl in range(11):
                    fc = 11 * half + fl
                    if ("gu", layer, fc) in pre_w:
                        wi = pre_w.pop(("gu", layer, fc))
                    else:
                        wi = load_w256(wgu, [(0, 128 * fc, 128), (128, DFF + 128 * fc, 128)])
                    for tc in range(4):
                        bg = 2 * (tc % 2)
                        bu = bg + 1
                        for kc in range(KC):
                            P.op("pe", MM(bank[bg][:], wbuf[wi][:, kc, 0:128], hnT[:, kc, tc * 512:(tc + 1) * 512], kc == 0, kc == KC - 1),
                                 reads=[R("wbuf", wi)] + hn_reads(tc), writes=[R("bank", bg)])
                        for kc in range(KC):
                            P.op("pe", MM(bank[bu][:], wbuf[wi][:, kc, 128:256], hnT[:, kc, tc * 512:(tc + 1) * 512], kc == 0, kc == KC - 1),
                                 reads=[R("wbuf", wi)] + hn_reads(tc), writes=[R("bank", bu)])
                        si = sgr[0]
                        sgr[0] ^= 1
                        P.op("act", ACTF(sg[si], bank[bg][:], AF.Silu), reads=[R("bank", bg)], writes=[R("sg", si)])
                        P.op("dve", TT(actT[:, fl, tc * 512:(tc + 1) * 512], sg[si], bank[bu][:], ALU.mult),
                             reads=[R("sg", si), R("bank", bu)], writes=[R("actT", tc)])
                if half == 0:
                    for fc in (11, 12):
                        pre_w[("gu", layer, fc)] = load_w256(wgu, [(0, 128 * fc, 128), (128, DFF + 128 * fc, 128)])
                for t in range(NT):
                    for nh in range(2):
                        mb = 4 + (2 * t + nh) % 4
                        for fl in range(11):
                            P.op("pe", MM(bank[mb][:], actT[:, fl, t * 128:(t + 1) * 128], wd[:, fl, nh * 512:(nh + 1) * 512], fl == 0, fl == 10),
                                 reads=[R("actT", t // 4), R("wd")], writes=[R("bank", mb)])
                        P.op("dve", TT(h[:, t, nh * 512:(nh + 1) * 512], bank[mb][:], h[:, t, nh * 512:(nh + 1) * 512], ALU.add),
                             reads=[R("bank", mb), R("h", t)], writes=[R("h", t)])
                    if half == 1 and fuse_norm is not None and t % 4 == 3:
                        norm_tiles(list(range(t - 3, t + 1)), fuse_norm == "out", (junk2, hn_bf2, ostage2))

        def layer_A():
            W = I["a_w_qkv"]
            b = I["a_b_qkv"]
            for two in range(2):
                P.dma("sp", DMA(bq[64 * two:64 * two + 64, :], dap(b, 512 * two, [[1, 64], [64, 8]]), slow=True), msem("sp"), writes=[R("bq")])
            P.dma("sp", DMA(bk[:], dap(b, 1024, [[1, 128], [1, 1]]), slow=True), msem("sp"), writes=[R("bk")])
            P.dma("sp", DMA(bv_bc[:], dap(b, 1152, [[0, 128], [1, 128]])), msem("sp"), writes=[R("bv")])
            q_proj(W, True, "qA")
            wi = load_w256(W, [(0, 1024, 256)])

            P.op("pool", MSET(kT1[64:128, :], 0.0), writes=[R("kTz", 0)])
            P.op("pool", MSET(kT2[0:64, :], 0.0), writes=[R("kTz", 1)])

            def ev_k(tc, mb):
                P.op("dve", TS(kT1[0:64, tc * 512:(tc + 1) * 512], bank[mb][0:64, :], bk[0:64, 0:1], None, ALU.add),
                     reads=[R("bank", mb), R("bk")], writes=[R("kT")])
                P.op("dve", TS(kT2[64:128, tc * 512:(tc + 1) * 512], bank[mb][64:128, :], bk[64:128, 0:1], None, ALU.add),
                     reads=[R("bank", mb), R("bk")], writes=[R("kT")])
            proj_feat(wi, 0, ev_k, "k")

            def ev_v(t, mb):
                P.op("dve", TT(V1[:, t, :, 0:64], bank[mb][:, 0:128].rearrange("p (k d) -> p k d", k=2),
                               bv_bc[:].rearrange("p (k d) -> p k d", k=2), ALU.add),
                     reads=[R("bank", mb), R("bv")], writes=[R("V")])
            proj_tok(wi, 128, 128, ev_v)
            P.op("pool", MSET(V1[:, :, :, 64], 1.0), writes=[R("V")])
            P.barrier()
            P.dma("sp", DMA(gbc[:], dap(I["a_b_o"], 0, [[0, 128], [1, D]])), msem("sp"), writes=[R("gbc")])
            P.dma("sp", DMA(esink[:], dap(I["a_sinks"], 0, [[0, 128], [1, 16]])), msem("sp"), writes=[R("esink")])
            P.op("act", ACTF(esink[:], esink[:], AF.Exp), reads=[R("esink")], writes=[R("esink")])
            for hk in range(2):
                build_table(tabs[:, hk], hk, 0, False, ((Htmp, R("Htmp")), (tabX[:], R("tabX"))), R("tab", hk))
                build_table(tabs[:, 2 + hk], hk, 1, True, ((Htmp, R("Htmp")), (tabX[:], R("tabX"))), R("tab", 2 + hk))
            load_wo(I["a_w_o"])
            def epiA(hk, qb, ob):
                P.op("dve", TT(g8(den[:]), O2(65, ob)[:, :, :, 64], g8(esink[:, 8 * hk:8 * hk + 8]), ALU.add),
                     reads=Ores(ob) + [R("esink")], writes=[R("den")])
                P.op("dve", lambda e: e.reciprocal(out=rd[:], in_=den[:]), reads=[R("den")], writes=[R("rd")])
                c0 = 8 * hk * 64
                P.op("dve", TT(attn_bf[:, c0:c0 + 512].rearrange("p (h g d) -> p h g d", h=2, g=4), O2(65, ob)[:, :, :, 0:64],
                               g8(rd[:]).unsqueeze(3).to_broadcast([128, 2, 4, 64]), ALU.mult),
                     reads=Ores(ob) + [R("rd")], writes=[R("attn_bf")])
                if hk == 1:
                    P.op("pool", TT(h[:, qb, :], h[:, qb, :], gbc[:], ALU.add), reads=[R("h", qb), R("gbc")], writes=[R("h", qb)])

                    def opA(qb=qb):
                        mb = misc_bank()
                        bb_ = bank_bf(mb)
                        for kc in range(KC):
                            P.op("pe", TR(bb_[:, kc * 128:(kc + 1) * 128], attn_bf[:, kc * 128:(kc + 1) * 128], ident[:]),
                                 reads=[R("attn_bf"), R("ident")], writes=[R("bank", mb)])
                        P.op("act", ACP(attnT, bb_), reads=[R("bank", mb)], writes=[R("attnT")])
                        outproj(qb, attnT, KC, 0, w_o)
                    defer(3, opA)

            for qb in range(NT):
                for hk in range(2):
                    kbs = [kb for kb in (qb - 1, qb) if kb >= 0]
                    for i, kb in enumerate(kbs):
                        tab = tabs[:, hk] if kb == qb else tabs[:, 2 + hk]
                        tres_ = R("tab", hk) if kb == qb else R("tab", 2 + hk)
                        lastu = (i == len(kbs) - 1)
                        if i == 0:
                            ob_ = new_branch()
                        unit(hk, qb, kT1 if hk == 0 else kT2, kb * 128, tab, False, V1[:, kb, hk, :], 65, i == 0, lastu, tabres=tres_,
                             post=(lambda hk=hk, qb=qb, ob_=ob_: epiA(hk, qb, ob_)) if lastu else None, ob=ob_)
            flush_pipe()
            P.barrier()

        def layer_B():
            W = I["b_w_in"]
            for which in range(2):
                w1v = I["b_cmp_w1"][which].rearrange("(l d) m -> d l m", d=64)
                for lq in range(4):
                    P.dma("pool", DMA(w1sb[64 * which:64 * which + 64, 8 * lq:8 * lq + 8, :], w1v[:, 8 * lq:8 * lq + 8, :]), sem_wo,
                          writes=[R("w1sb")], cont=not (which == 0 and lq == 0))
                P.dma("sp", DMA(posT[64 * which:64 * which + 64, :], I["b_cmp_pos"][which].rearrange("l d -> d l"), slow=True),
                      msem("sp"), writes=[R("posT")])
            w2k = I["b_cmp_w2"][0].rearrange("(c p) d -> p c d", p=128)
            for two in range(2):
                P.dma("pool", DMA(w2dup[:, :, 64 * two:64 * two + 64], w2k), msem("pool"), writes=[R("w2k", two)])
            P.dma("pool", DMA(w2v[:], I["b_cmp_w2"][1].rearrange("(c p) d -> p c d", p=128)), msem("pool"), writes=[R("w2v")])
            q_proj(W, False, "qB")
            P.op("pool", MSET(cmpT[:, :, 2048:2064], 0.0), writes=[R("cmpT", 0), R("cmpT", 1)])
            wi = load_w256(W, [(0, 1024, 64), (64, 1152, 64), (128, 1088, 64), (192, 1216, 64)])
            for hk_ in range(2):
                def ev_c(tc, mb, hk_=hk_):
                    P.op("act", ACP(cmpT[:, hk_, tc * 512:(tc + 1) * 512], bank[mb][:]), reads=[R("bank", mb)], writes=[R("cmpT", hk_)])
                proj_feat(wi, 128 * hk_, ev_c, "c")
            for (c0, kTd, Vd, nm) in ((1280, kT1, V1, "s"), (1536, kT2, V2, "w")):
                wi = load_w256(W, [(0, c0, 256)])

                def ev_k(tc, mb, kTd=kTd, nm=nm):
                    P.op("act", ACP(kTd[:, tc * 512:(tc + 1) * 512], bank[mb][:]), reads=[R("bank", mb)], writes=[R("kT", nm)])

                def ev_v(t, mb, Vd=Vd, nm=nm):
                    P.op("dve", CP(Vd[:, t, :, 0:64], bank[mb][:, 0:128].rearrange("p (k d) -> p k d", k=2)),
                         reads=[R("bank", mb)], writes=[R("V", nm)])
                proj_feat(wi, 0, ev_k, "k")
                proj_tok(wi, 128, 128, ev_v)
                P.op("pool", MSET(Vd[:, :, :, 64], 1.0), writes=[R("V", nm)])
            wi = load_w256(W, [(0, 1792, 48)])

            def ev_g(t, mb):
                P.op("act", ACTF(gates[:, t, :], bank[mb][:, 0:48], AF.Sigmoid), reads=[R("bank", mb)], writes=[R("gates")])
            proj_tok(wi, 0, 48, ev_g)
            P.barrier(keep=[R("w1sb"), R("posT"), R("w2k", 0), R("w2k", 1), R("w2v")])
            P.op("pool", MSET(kcT2[:], 0.0), writes=[R("kcT2")])
            for hk in range(2):
                for a_ in range(2):
                    P.op("dve", TT(blk[:, :, 16 * a_:16 * a_ + 16], cmpT[:, hk, 16 * a_:16 * a_ + 2048].rearrange("p (c r) -> p c r", r=16),
                                   posT[:, 16 * a_:16 * a_ + 16].unsqueeze(1).to_broadcast([128, 128, 16]), ALU.add),
                         reads=[R("cmpT", hk), R("posT")], writes=[R("blk")])
                gb = [misc_bank(), misc_bank()]
                for l in range(32):
                    for which in range(2):
                        P.op("pe", MM(bank[gb[which]][:, 0:256], blk[64 * which:64 * which + 64, :, l],
                                      w1sb[64 * which:64 * which + 64, l, :], l == 0, l == 31),
                             reads=[R("blk"), R("w1sb")], writes=[R("bank", gb[which])])
                for which in range(2):
                    mb = gb[which]
                    P.op("act", ACP(xs_g, bank[mb][:, 0:256]), reads=[R("bank", mb)], writes=[R("xs")])
                    P.op("dve", TT(t1_g, xs_g, xs_g, ALU.mult), reads=[R("xs")], writes=[R("t1")])
                    P.op("dve", TS(t1_g, t1_g, 0.044715, 1.0, ALU.mult, ALU.add), reads=[R("t1")], writes=[R("t1")])
                    P.op("dve", TT(t1_g, t1_g, xs_g, ALU.mult), reads=[R("t1"), R("xs")], writes=[R("t1")])
                    P.op("act", ACTF(t1_g, t1_g, AF.Sigmoid, scale=1.5957691216057308), reads=[R("t1")], writes=[R("t1")])
                    P.op("dve", TT(gl_bf, xs_g, t1_g, ALU.mult), reads=[R("t1"), R("xs")], writes=[R("gl")])
                    bb_ = bank_bf(mb)
                    for ch in range(2):
                        P.op("pe", TR(bb_[:, ch * 128:(ch + 1) * 128], gl_bf[:, ch * 128:(ch + 1) * 128], J_bf[:]),
                             reads=[R("gl"), R("J_bf")], writes=[R("bank", mb)])
                    P.op("act", ACP(GT, bb_[:, 0:256]), reads=[R("bank", mb)], writes=[R("GT")])
                    if which == 0:
                        for ch in range(2):
                            P.op("pe", MM(bank[mb][:, 0:128], w2dup[:, ch, :], GT[:, ch * 128:(ch + 1) * 128], ch == 0, ch == 1),
                                 reads=[R("GT"), R("w2k", 0), R("w2k", 1)], writes=[R("bank", mb)])
                        P.op("dve", CP(kcT2[64 * hk:64 * hk + 64, hk, :], bank[mb][64 * hk:64 * hk + 64, 0:128]),
                             reads=[R("bank", mb)], writes=[R("kcT2")])
                    else:
                        for ch in range(2):
                            P.op("pe", MM(bank[mb][:, 0:64], GT[:, ch * 128:(ch + 1) * 128], w2v[:, ch, :], ch == 0, ch == 1),
                                 reads=[R("GT"), R("w2v")], writes=[R("bank", mb)])
                        P.op("dve", CP(Vc[:, hk, 0:64], bank[mb][:, 0:64]), reads=[R("bank", mb)], writes=[R("Vc")])
            P.op("pool", MSET(Vc[:, :, 64:65], 1.0), writes=[R("Vc")])
            for hk in range(2):
                P.op("pool", CP(Vc[:, hk, 65:97], ovrev[:]), reads=[R("ovrev")], writes=[R("Vc")])
            P.barrier()
            P.dma("pool", DMA(E_bf[0:32, :], I["c_E"]), msem("pool"), writes=[R("E")])
            P.op("pool", MSET(E_bf[32:64, :], 0.0), writes=[R("E")])
            P.op("pool", MSET(E_bf[64:128, :], 0.0), writes=[R("E")])
            P.op("pool", MSET(MBT[:], 0.0), writes=[R("MBT")])
            gview = gates[:].rearrange("p t (h b) -> p t h b", b=3)
            for hk in range(2):
                if hk == 0:
                    for i_, kTx in enumerate((kT1, kT2)):
                        P.dma("sp", DMA(ksave[i_], kTx[64:128, :]), msem("sp"), reads=[R("kT")], writes=[R("ksave")])
                    for kTx in (kT1, kT2):
                        P.op("pool", MSET(kTx[64:128, :], 0.0), writes=[R("kT")])
                else:
                    for kTx in (kT1, kT2):
                        P.op("pool", MSET(kTx[0:64, :], 0.0), writes=[R("kT")])
                    for i_, kTx in enumerate((kT1, kT2)):
                        P.dma("sp", DMA(kTx[64:128, :], ksave[i_]), msem("sp"), reads=[R("ksave")], writes=[R("kT")])
                tabc = gbc[:].rearrange("p (g j) -> p g j", g=8)
                for qb in range(NT):
                    def pre_c(hk=hk, qb=qb):
                        if qb < 8:
                            hb = ((Htmp, R("Htmp")), (tabX[:], R("tabX"))) if qb < 4 else ((Htmp, R("Htmp")),)
                            build_table(tabs[:, qb], hk, qb, False, hb, R("tab", qb))
                        if qb == 4:
                            P.op("dve", TT(tabX[:], tabs[:, 4], tri[:].unsqueeze(1).to_broadcast([128, 8, 128]), ALU.add),
                                 reads=[R("tab", 4), R("tri")], writes=[R("tabX")])
                        if hk == 0 and qb == 0:
                            load_wo(I["b_w_o"])
                        P.dma("sp", DMA(tabc, dap(Fd, (8 * hk) * FLEN + qb * 128, [[16, 128], [FLEN, 8], [1, 128]])), msem("sp"),
                              reads=[R("Fd")], writes=[R("gbc")])
                    ob_ = new_branch()
                    unit(hk, qb, kcT2[:, hk, :], 0, tabc, False, Vc[:, hk, :], 97, True, True, tabres=R("gbc"),
                         kres=R("kcT2"), vres=R("Vc"), pre=pre_c, post=(lambda hk=hk, qb=qb, ob_=ob_: epi_cmp(hk, qb, ob_)), ob=ob_)
                    kbs = [kb for kb in range(qb - 4, qb + 1) if kb >= 0]
                    for i, kb in enumerate(kbs):
                        off = qb - kb
                        tab = tabX[:] if off == 4 else tabs[:, off]
                        rt = R("tabX") if off == 4 else R("tab", off)
                        lastu = (i == len(kbs) - 1)
                        if i == 0:
                            ob_ = new_branch()
                        unit(hk, qb, kT2, kb * 128, tab, False, V2[:, kb, hk, :], 65, i == 0, lastu, tabres=rt,
                             post=(lambda hk=hk, qb=qb, ob_=ob_: branch_epilogue(hk, qb, 2, False, ob_)) if lastu else None, ob=ob_)
                    for kb in range(qb + 1):
                        off = qb - kb
                        em = kb if qb >= 8 else None
                        lastu = (kb == qb)
                        if kb == 0:
                            ob_ = new_branch()
                        po = (lambda hk=hk, qb=qb, ob_=ob_: (branch_epilogue(hk, qb, 1, True, ob_), defer(6, lambda: outproj_B(hk, qb)))) if lastu else None
                        if off <= 7:
                            unit(hk, qb, kT1, kb * 128, tabs[:, off], False, V1[:, kb, hk, :], 65, kb == 0, lastu, emask_kb=em, post=po, ob=ob_,
                                 tabres=R("tab", off))
                        else:
                            unit(hk, qb, kT1, kb * 128, None, True, V1[:, kb, hk, :], 65, kb == 0, lastu, emask_kb=em, post=po, ob=ob_)
                flush_pipe()
            P.barrier()

        def outproj_B(hk, qb):
            mb = misc_bank()
            bb_ = bank_bf(mb)
            for kc in range(4):
                P.op("pe", TR(bb_[:, kc * 128:(kc + 1) * 128], attn_bf1[:, kc * 128:(kc + 1) * 128], ident[:]),
                     reads=[R("attn_bf"), R("ident")], writes=[R("bank", mb)])
            P.op("act", ACP(attnT1, bb_[:, 0:512]), reads=[R("bank", mb)], writes=[R("attnT")])
            outproj(qb, attnT1, 4, 4 * hk, w_o)

        def epi_cmp(hk, qb, ob):
            gview = gates[:].rearrange("p t (h b) -> p t h b", b=3)
            P.op("dve", TS(g8(den[:]), O2(97, ob)[:, :, :, 64], 1e-30, None, ALU.max), reads=Ores(ob), writes=[R("den")])
            P.op("dve", lambda e: e.reciprocal(out=rd[:], in_=den[:]), reads=[R("den")], writes=[R("rd")])
            P.op("dve", TT(fac[:], rd[:], gview[:, qb, 8 * hk:8 * hk + 8, 0], ALU.mult), reads=[R("rd"), R("gates")], writes=[R("fac")])
            P.op("dve", TT(o_acc.rearrange("p (h g) d -> p h g d", h=2), O2(97, ob)[:, :, :, 0:64],
                           g8(fac[:]).unsqueeze(3).to_broadcast([128, 2, 4, 64]), ALU.mult),
                 reads=Ores(ob) + [R("fac")], writes=[R("o_acc")])
            if qb >= 8:
                P.op("dve", TT(impn[:].rearrange("p (h g) j -> p h g j", h=2), O2(97, ob)[:, :, :, 65:97],
                               g8(rd[:]).unsqueeze(3).to_broadcast([128, 2, 4, 32]), ALU.mult),
                     reads=Ores(ob) + [R("rd")], writes=[R("impn")])
                P.op("dve", lambda e: e.tensor_reduce(out=imp[:], in_=impn[:].rearrange("p g j -> p j g"), axis=AX.X, op=ALU.add),
                     reads=[R("impn")], writes=[R("imp")])
                P.op("dve", TT(sc[:], imp[:], selc[:, qb - 8, 0:32], ALU.mult), reads=[R("imp"), R("selc")], writes=[R("sc")])
                P.op("dve", TT(sc[:], sc[:], selc[:, qb - 8, 32:64], ALU.add), reads=[R("sc"), R("selc")], writes=[R("sc")])
                P.op("dve", lambda e: e.max(out=m8[:, 0:8], in_=sc[:]), reads=[R("sc")], writes=[R("m8")])
                P.op("dve", lambda e: e.match_replace(out=sc2[:], in_to_replace=m8[:, 0:8], in_values=sc[:], imm_value=NEG),
                     reads=[R("sc"), R("m8")], writes=[R("sc2")])
                P.op("dve", lambda e: e.max(out=m8[:, 8:16], in_=sc2[:]), reads=[R("sc2")], writes=[R("m8")])
                P.op("dve", TS(sc2[:], sc[:], m8[:, 15:16], None, ALU.is_ge), reads=[R("sc"), R("m8")], writes=[R("sc2")])
                P.op("dve", TS(mb_bf[:], sc2[:], -1.0, 30000.0, ALU.add, ALU.mult), reads=[R("sc2")], writes=[R("mb_bf")])
                def mbt_part():
                    mb = misc_bank()
                    bb_ = bank_bf(mb)
                    P.op("pe", TR(bb_[0:32, 0:128], mb_bf[:, 0:32], ident[:]), reads=[R("mb_bf"), R("ident")], writes=[R("bank", mb)])
                    for rr in range(4):
                        P.op("act", ACP(MBT[0:32, rr * 128:(rr + 1) * 128], bb_[0:32, 0:128]), reads=[R("bank", mb)], writes=[R("MBT")])
                defer(5, mbt_part)

        tmpm = P.sb("tmpm", [128, 8, 64], F32)

        def branch_epilogue(hk, qb, br, is_last, ob):
            gview = gates[:].rearrange("p t (h b) -> p t h b", b=3)
            P.op("dve", lambda e: e.reciprocal(out=g8(rd[:]), in_=O2(65, ob)[:, :, :, 64]), reads=Ores(ob), writes=[R("rd")])
            P.op("dve", TT(fac[:], rd[:], gview[:, qb, 8 * hk:8 * hk + 8, br], ALU.mult), reads=[R("rd"), R("gates")], writes=[R("fac")])
            P.op("dve", TT(tmpm[:].rearrange("p (h g) d -> p h g d", h=2), O2(65, ob)[:, :, :, 0:64],
                           g8(fac[:]).unsqueeze(3).to_broadcast([128, 2, 4, 64]), ALU.mult),
                 reads=Ores(ob) + [R("fac")], writes=[R("tmpm")])
            if is_last:
                P.op("pool", TT(attn_bf1.rearrange("p (g d) -> p g d", g=8), o_acc, tmpm[:], ALU.add),
                     reads=[R("o_acc"), R("tmpm")], writes=[R("attn_bf")])
            else:
                P.op("pool", TT(o_acc, o_acc, tmpm[:], ALU.add), reads=[R("o_acc"), R("tmpm")], writes=[R("o_acc")])

        nl = len(layer_ids)
        qsrc = {0: (I["a_w_qkv"], "qA"), 1: (I["b_w_in"], "qB")}
        for k_, li in enumerate(layer_ids):
            if k_ == 0:
                prefetch_q(*qsrc[li % 2])
                norm_phase(I["attn_norm"][li:li + 1, :])
                P.barrier(keep=wkeep())
            if li % 2 == 0:
                layer_A()
            else:
                layer_B()
            prefetch_ffn(li)
            norm_phase(I["ffn_norm"][li:li + 1, :])
            P.barrier(keep=wkeep())
            if k_ + 1 < nl:
                load_gain(I["attn_norm"][layer_ids[k_ + 1]:layer_ids[k_ + 1] + 1, :])
                ffn(li, fuse_norm="hnT")
                prefetch_q(*qsrc[layer_ids[k_ + 1] % 2])
            elif final:
                load_gain(I["final_norm"][0:1, :])
                ffn(li, fuse_norm="out")
            else:
                ffn(li)
            P.barrier(keep=wkeep())
        if not final:
            for t in range(NT):
                P.dma("sp", DMA(out[t * 128:(t + 1) * 128, :], h[:, t, :]), sem_out[t % 2], reads=[R("h", t)])
        P.barrier()
        stats = P.emit()
        stats["sbuf_left"] = nc.sbuf_bytes_remaining
    return nc, stats


FUSED = True


def _prep_shared(inputs):
    f = lambda a: np.ascontiguousarray(np.asarray(a, dtype=np.float32))
    sh = {
        "rel_table": f(inputs["rel_table"]), "attn_norm": f(inputs["attn_norm"]), "ffn_norm": f(inputs["ffn_norm"]),
        "final_norm": f(inputs["final_norm"]).reshape(1, D),
        "a_w_qkv": f(inputs["a_w_qkv"])[0], "a_b_qkv": f(inputs["a_b_qkv"]).reshape(1, 1280),
        "a_sinks": f(inputs["a_sinks"]).reshape(1, 16), "a_w_o": f(inputs["a_w_o"])[0],
        "a_b_o": f(inputs["a_b_o"]).reshape(1, D), "b_w_in": f(inputs["b_w_in"])[0],
        "b_cmp_pos": f(inputs["b_cmp_pos"])[0], "b_cmp_w1": f(inputs["b_cmp_w1"])[0],
        "b_cmp_w2": f(inputs["b_cmp_w2"])[0], "b_w_o": f(inputs["b_w_o"])[0],
        "ffn_w_gu": f(inputs["ffn_w_gu"]), "ffn_w_down": f(inputs["ffn_w_down"]),
    }
    sh.update(host_consts())
    return sh


def run_prog(layer_ids, final, xs, shared):
    nc, stats = build(layer_ids, final)
    in_maps = []
    for xb in xs:
        m = dict(shared)
        m["x"] = np.ascontiguousarray(xb, dtype=np.float32)
        in_maps.append(m)
    res = run_bass_kernel_spmd(nc, in_maps, core_ids=list(range(len(xs))))
    return [np.asarray(r["out"]) for r in res.results]


def kernel(**inputs):
    x = np.asarray(inputs["x"], dtype=np.float32)
    shared = _prep_shared(inputs)
    xs = [x[b] for b in range(x.shape[0])]
    if FUSED:
        outs = run_prog([0, 1], True, xs, shared)
    else:
        hs = run_prog([0], False, xs, shared)
        outs = run_prog([1], True, hs, shared)
    return np.stack(outs, axis=0).astype(np.float32)
```
